# Optimizing a Trainium2 kernel written in Bass

```python
import jax, jax.numpy as jnp
from jax import lax
import numpy as np

D_MODEL = 1024
BATCH = 4
SEQ = 4096
DEPTH = 2
DEC_BATCH = 128
DEC_SEQ = 8
PAST_LEN = 2048
PAGE_SIZE = 128

N_META = 16
N_A_LAYERS = DEPTH // 2
N_B_LAYERS = DEPTH - N_A_LAYERS
H_A = 8
DK_A = D_MODEL // (2 * H_A)
DV_A = D_MODEL // H_A
CHUNK_A = 64
GATE_CAP = 15.0
H_B = 16
DH_B = D_MODEL // H_B
Q_BLOCK = 128
D_FF = -(-8 * D_MODEL // (3 * 256)) * 256
EPS = 1e-6

kernel_name = 'fox_mlstm_yoco_step'


def rmsnorm(x, g):
    xf = x.astype(jnp.float32)
    r = lax.rsqrt(jnp.mean(xf * xf, axis=-1, keepdims=True) + EPS)
    return (xf * r * g.astype(jnp.float32)).astype(x.dtype)


def softcap(x):
    return GATE_CAP * jnp.tanh(x / GATE_CAP)


def mlstm_chunk(state, inp):
    C, n, m = state
    q, k, v, li, lf = inp
    L = q.shape[2]
    b = jnp.cumsum(lf, axis=-1)
    causal = jnp.tril(jnp.ones((L, L), dtype=bool))
    dmat = jnp.where(causal, b[..., :, None] - b[..., None, :] + li[..., None, :], -jnp.inf)
    inter = b + m[..., None]
    m_t = jnp.maximum(inter, jnp.max(dmat, axis=-1))
    w_inter = jnp.exp(inter - m_t)
    s = jnp.einsum('bhtk,bhsk->bhts', q, k) * jnp.exp(dmat - m_t[..., None])
    num = w_inter[..., None] * jnp.einsum('bhtk,bhkv->bhtv', q, C) + jnp.einsum('bhts,bhsv->bhtv', s, v)
    den = w_inter * jnp.einsum('bhtk,bhk->bht', q, n) + jnp.sum(s, axis=-1)
    h = num / jnp.maximum(jnp.abs(den), jnp.exp(-m_t))[..., None]
    b_end = b[..., -1]
    g = b_end[..., None] - b + li
    m_new = jnp.maximum(b_end + m, jnp.max(g, axis=-1))
    w_c = jnp.exp(b_end + m - m_new)
    w_g = jnp.exp(g - m_new[..., None])
    C_new = w_c[..., None, None] * C + jnp.einsum('bhs,bhsk,bhsv->bhkv', w_g, k, v)
    n_new = w_c[..., None] * n + jnp.einsum('bhs,bhsk->bhk', w_g, k)
    return (C_new, n_new, m_new), h


def mlstm_sequence(q, k, v, li, lf, state, lead):
    T = q.shape[2]
    state, h_lead = mlstm_chunk(state, (q[:, :, :lead], k[:, :, :lead], v[:, :, :lead], li[:, :, :lead], lf[:, :, :lead]))
    rest = T - lead
    if rest == 0:
        return h_lead, state
    nc = rest // CHUNK_A

    def to_chunks(a):
        a = a[:, :, lead:]
        return jnp.moveaxis(a.reshape(a.shape[:2] + (nc, CHUNK_A) + a.shape[3:]), 2, 0)

    state, h_rest = lax.scan(mlstm_chunk, state, (to_chunks(q), to_chunks(k), to_chunks(v), to_chunks(li), to_chunks(lf)))
    Bq, H = q.shape[:2]
    h_rest = jnp.moveaxis(h_rest, 0, 2).reshape(Bq, H, rest, DV_A)
    return jnp.concatenate([h_lead, h_rest], axis=2), state


def mlstm_mixer(xn, C0, n0, m0, lead, w_in, b_ig, b_fg, mh_g, w_out):
    B, T, _ = xn.shape
    HK, HV = H_A * DK_A, H_A * DV_A
    p = xn @ w_in
    f32 = jnp.float32

    def heads(a, d):
        return a.reshape(B, T, H_A, d).transpose(0, 2, 1, 3).astype(f32)

    q = heads(p[..., :HK], DK_A) * DK_A ** -0.5
    k = heads(p[..., HK:2 * HK], DK_A)
    v = heads(p[..., 2 * HK:2 * HK + HV], DV_A)
    og = p[..., 2 * HK + HV:2 * HK + 2 * HV]
    gi = p[..., 2 * HK + 2 * HV:2 * HK + 2 * HV + H_A]
    gf = p[..., 2 * HK + 2 * HV + H_A:]
    li = softcap(gi.astype(f32) + b_ig.astype(f32)).transpose(0, 2, 1)
    lf = jax.nn.log_sigmoid(softcap(gf.astype(f32) + b_fg.astype(f32))).transpose(0, 2, 1)
    state0 = (C0.astype(f32), n0.astype(f32), m0.astype(f32))
    h, state = mlstm_sequence(q, k, v, li, lf, state0, lead)
    h = rmsnorm(h.transpose(0, 2, 1, 3), mh_g)
    out = (h.reshape(B, T, HV).astype(xn.dtype) * jax.nn.sigmoid(og)) @ w_out
    return out, state


def fox_attention(q, cq, pos_q, k, v, ck, pos_k):
    B, Tq, H, Dh = q.shape
    f32 = jnp.float32
    blk = min(Q_BLOCK, Tq)
    nblk = -(-Tq // blk)
    pad = nblk * blk - Tq
    if pad:
        q = jnp.pad(q, ((0, 0), (0, pad), (0, 0), (0, 0)))
        cq = jnp.pad(cq, ((0, 0), (0, pad), (0, 0)))
        pos_q = jnp.pad(pos_q, (0, pad), mode='edge')
    qb = q.reshape(B, nblk, blk, H, Dh).swapaxes(0, 1)
    cqb = cq.reshape(B, nblk, blk, H).swapaxes(0, 1)
    pqb = pos_q.reshape(nblk, blk)
    ckT = ck.astype(f32).transpose(0, 2, 1)

    def one_block(args):
        qi, ci, pi = args
        s = jnp.einsum('bqhd,bkhd->bhqk', qi, k, preferred_element_type=f32)
        s = s + ci.astype(f32).transpose(0, 2, 1)[..., None] - ckT[:, :, None, :]
        s = jnp.where(pos_k[None, :] <= pi[:, None], s, -jnp.inf)
        p = jax.nn.softmax(s, axis=-1)
        return jnp.einsum('bhqk,bkhd->bqhd', p.astype(v.dtype), v)

    o = lax.map(one_block, (qb, cqb, pqb))
    return o.swapaxes(0, 1).reshape(B, nblk * blk, H, Dh)[:, :Tq]


def fox_mixer(xn, shared, attend, w_qo, q_g, w_out):
    B, T, _ = xn.shape
    HD = H_B * DH_B
    p = xn @ w_qo
    q = rmsnorm(p[..., :HD].reshape(B, T, H_B, DH_B), q_g) * DH_B ** -0.5
    og = p[..., HD:]
    k, v, lf = shared
    o = attend(q, k, v, lf).reshape(B, T, HD)
    return (o * jax.nn.sigmoid(og)) @ w_out


def shared_kv(h, g_kv, w_kvf, b_f, k_g):
    B, T, _ = h.shape
    HD = H_B * DH_B
    xs = rmsnorm(h, g_kv)
    p = xs @ w_kvf
    k = rmsnorm(p[..., :HD].reshape(B, T, H_B, DH_B), k_g)
    v = p[..., HD:2 * HD].reshape(B, T, H_B, DH_B)
    lf = jax.nn.log_sigmoid((p[..., 2 * HD:] + b_f).astype(jnp.float32))
    return k, v, lf


def swiglu(xn, w_gu, w_d):
    gu = xn @ w_gu
    return (jax.nn.silu(gu[..., :D_FF]) * gu[..., D_FF:]) @ w_d


def run_trunk(h, C0, n0, m0, lead, attend, prm):
    Cs, ns, ms = [], [], []
    shared = None
    for l in range(DEPTH):
        if l < N_A_LAYERS:
            i = l
            a, (C, n, m) = mlstm_mixer(rmsnorm(h, prm['norm_a'][i]), C0[i], n0[i], m0[i], lead,
                                       prm['w_in_a'][i], prm['b_ig_a'][i], prm['b_fg_a'][i],
                                       prm['mh_norm_a'][i], prm['w_out_a'][i])
            h = h + a
            Cs.append(C); ns.append(n); ms.append(m)
        else:
            j = l - N_A_LAYERS
            h = h + fox_mixer(rmsnorm(h, prm['norm_b'][j]), shared, attend,
                              prm['w_qo_b'][j], prm['q_norm_b'][j], prm['w_out_b'][j])
        h = h + swiglu(rmsnorm(h, prm['norm_ffn'][l]), prm['w_gate_up'][l], prm['w_down'][l])
        if l == N_A_LAYERS - 1:
            shared = shared_kv(h, prm['norm_kv'], prm['w_kvf'], prm['b_fg_b'], prm['k_norm_b'])
    y = rmsnorm(h, prm['norm_final'])
    return y, jnp.stack(Cs), jnp.stack(ns), jnp.stack(ms), shared


def setup_inputs(seed: int = 0) -> dict:
    key = jax.random.key(seed)
    ks = jax.random.split(key, 32)
    f32 = jnp.float32
    n_pages = PAST_LEN // PAGE_SIZE
    n_phys = (DEC_BATCH * n_pages * 5) // 4
    HK, HV, HD = H_A * DK_A, H_A * DV_A, H_B * DH_B

    def nrm(k, shape, scale=1.0):
        return jax.random.normal(k, shape, f32) * scale

    def gain(k, shape):
        return 1.0 + 0.05 * jax.random.normal(k, shape, f32)

    page_table = jax.random.permutation(ks[8], n_phys)[:DEC_BATCH * n_pages].reshape(DEC_BATCH, n_pages).astype(jnp.int32)
    return {
        'x_prompt': nrm(ks[0], (BATCH, SEQ, D_MODEL)),
        'x_sample': nrm(ks[1], (DEC_BATCH, DEC_SEQ, D_MODEL)),
        'state_C': nrm(ks[2], (N_A_LAYERS, DEC_BATCH, H_A, DK_A, DV_A), 0.5),
        'state_n': nrm(ks[3], (N_A_LAYERS, DEC_BATCH, H_A, DK_A), 0.5),
        'state_m': nrm(ks[4], (N_A_LAYERS, DEC_BATCH, H_A)),
        'cache_k': nrm(ks[5], (n_phys, PAGE_SIZE, H_B, DH_B)),
        'cache_v': nrm(ks[6], (n_phys, PAGE_SIZE, H_B, DH_B)),
        'cache_logf': jax.nn.log_sigmoid(3.0 + nrm(ks[7], (n_phys, PAGE_SIZE, H_B))),
        'page_table': page_table,
        'meta_tokens': nrm(ks[9], (N_META, D_MODEL)),
        'norm_a': gain(ks[10], (N_A_LAYERS, D_MODEL)),
        'w_in_a': nrm(ks[11], (N_A_LAYERS, D_MODEL, 2 * HK + 2 * HV + 2 * H_A), D_MODEL ** -0.5),
        'b_ig_a': nrm(ks[12], (N_A_LAYERS, H_A), 0.5),
        'b_fg_a': 3.0 + nrm(ks[13], (N_A_LAYERS, H_A), 0.5),
        'mh_norm_a': gain(ks[14], (N_A_LAYERS, H_A, DV_A)),
        'w_out_a': nrm(ks[15], (N_A_LAYERS, HV, D_MODEL), HV ** -0.5),
        'norm_kv': gain(ks[16], (D_MODEL,)),
        'w_kvf': nrm(ks[17], (D_MODEL, 2 * HD + H_B), D_MODEL ** -0.5),
        'b_fg_b': 3.0 + nrm(ks[18], (H_B,), 0.5),
        'k_norm_b': gain(ks[19], (DH_B,)),
        'norm_b': gain(ks[20], (N_B_LAYERS, D_MODEL)),
        'w_qo_b': nrm(ks[21], (N_B_LAYERS, D_MODEL, HD + D_MODEL), D_MODEL ** -0.5),
        'q_norm_b': gain(ks[22], (N_B_LAYERS, DH_B)),
        'w_out_b': nrm(ks[23], (N_B_LAYERS, HD, D_MODEL), HD ** -0.5),
        'norm_ffn': gain(ks[24], (DEPTH, D_MODEL)),
        'w_gate_up': nrm(ks[25], (DEPTH, D_MODEL, 2 * D_FF), D_MODEL ** -0.5),
        'w_down': nrm(ks[26], (DEPTH, D_FF, D_MODEL), D_FF ** -0.5),
        'norm_final': gain(ks[27], (D_MODEL,)),
    }


def reference(x_prompt, x_sample, state_C, state_n, state_m, cache_k, cache_v, cache_logf, page_table,
              meta_tokens, norm_a, w_in_a, b_ig_a, b_fg_a, mh_norm_a, w_out_a, norm_kv, w_kvf, b_fg_b,
              k_norm_b, norm_b, w_qo_b, q_norm_b, w_out_b, norm_ffn, w_gate_up, w_down, norm_final):
    prm = {'norm_a': norm_a, 'w_in_a': w_in_a, 'b_ig_a': b_ig_a, 'b_fg_a': b_fg_a,
           'mh_norm_a': mh_norm_a, 'w_out_a': w_out_a, 'norm_kv': norm_kv, 'w_kvf': w_kvf,
           'b_fg_b': b_fg_b, 'k_norm_b': k_norm_b, 'norm_b': norm_b, 'w_qo_b': w_qo_b,
           'q_norm_b': q_norm_b, 'w_out_b': w_out_b, 'norm_ffn': norm_ffn, 'w_gate_up': w_gate_up,
           'w_down': w_down, 'norm_final': norm_final}
    f32 = jnp.float32

    meta = jnp.broadcast_to(meta_tokens[None].astype(x_prompt.dtype), (BATCH, N_META, D_MODEL))
    h0 = jnp.concatenate([meta, x_prompt], axis=1)
    C0 = jnp.zeros((N_A_LAYERS, BATCH, H_A, DK_A, DV_A), f32)
    n0 = jnp.zeros((N_A_LAYERS, BATCH, H_A, DK_A), f32)
    m0 = jnp.zeros((N_A_LAYERS, BATCH, H_A), f32)

    def attend_prompt(q, k, v, lf):
        c = jnp.cumsum(lf, axis=1)
        pos = jnp.arange(q.shape[1], dtype=jnp.int32)
        return fox_attention(q, c, pos, k, v, c, pos)

    yp, p_C, p_n, p_m, (p_k, p_v, p_logf) = run_trunk(h0, C0, n0, m0, N_META, attend_prompt, prm)
    y_prompt = yp[:, N_META:]

    def attend_sample(q, k, v, lf):
        n_pages = PAST_LEN // PAGE_SIZE
        Bd, T = q.shape[:2]
        kp = cache_k[page_table].reshape(Bd, n_pages * PAGE_SIZE, H_B, DH_B)
        vp = cache_v[page_table].reshape(Bd, n_pages * PAGE_SIZE, H_B, DH_B)
        lp = cache_logf[page_table].reshape(Bd, n_pages * PAGE_SIZE, H_B)
        kc = jnp.concatenate([kp, k.astype(kp.dtype)], axis=1)
        vc = jnp.concatenate([vp, v.astype(vp.dtype)], axis=1)
        c = jnp.cumsum(jnp.concatenate([lp.astype(f32), lf], axis=1), axis=1)
        pos_k = jnp.arange(PAST_LEN + T, dtype=jnp.int32)
        pos_q = PAST_LEN + jnp.arange(T, dtype=jnp.int32)
        return fox_attention(q, c[:, PAST_LEN:], pos_q, kc, vc, c, pos_k)

    y_sample, s_C, s_n, s_m, (s_k, s_v, s_logf) = run_trunk(x_sample, state_C, state_n, state_m, DEC_SEQ,
                                                           attend_sample, prm)
    return (y_prompt, y_sample, p_C, p_n, p_m, p_k, p_v, p_logf, s_C, s_n, s_m, s_k, s_v, s_logf)
```

```python
import types
import collections
import numpy as np
import concourse.bass as bass
import concourse.mybir as mybir
from concourse.bass_utils import run_bass_kernel_spmd
from contextlib import ExitStack

F32 = mybir.dt.float32
BF16 = mybir.dt.bfloat16
I32 = mybir.dt.int32
AF = mybir.ActivationFunctionType
ALU = mybir.AluOpType
AX = mybir.AxisListType

D = 1024
NMETA = 16
HA, DK, DV = 8, 64, 128
HB, DH = 16, 64
DFF = 2816
EPS = 1e-6
CAP = 15.0
NEG = -30000.0
NCORE = 8
PAIRS = [[0, 4], [1, 5], [2, 6], [3, 7]]
QUADS = [[0, 1, 2, 3], [4, 5, 6, 7]]


class Cfg:
    def __init__(self, SEQ=4096, PAST=2048, NPHYS=2560, PAGE=128, phases="ABCD"):
        self.SEQ, self.PAST, self.NPHYS, self.PAGE = SEQ, PAST, NPHYS, PAGE
        self.TP = SEQ + NMETA
        self.NT = SEQ // 128 + 1
        self.HALF = self.TP // 2
        self.NOWN = -(-self.HALF // 128)
        self.NPG = PAST // PAGE
        self.phases = phases

    def trange(self, i):
        if i == 0:
            return 0, NMETA
        return NMETA + 128 * (i - 1), NMETA + 128 * i


class TT:
    def __init__(self, t):
        self.t = t
        self.w = None
        self.r = {}


NDS = 48


class B:
    GEN = 16000

    def __init__(self, nc, es):
        self.nc = nc
        self.es = es
        self.E = {'pe': nc.tensor, 'dve': nc.vector, 'act': nc.scalar, 'pool': nc.gpsimd, 'sp': nc.sync}
        self.cnt = {k: 0 for k in ('pe', 'dve', 'act', 'pool')}
        self.esem = {k: [es.enter_context(nc.semaphore(f"s_{k}{g}")) for g in range(6)] for k in self.cnt}
        self.dsem = [es.enter_context(nc.semaphore(f"s_d{i}")) for i in range(NDS)]
        self.dval = [0] * NDS
        self.dnext = 0
        self.csem = es.enter_context(nc.semaphore("s_cc"))
        self.cval = 0
        self.seen = {k: {} for k in self.E}
        self.uid = 0
        self.rec = None

    def wait(self, e, ev):
        if ev is None:
            return
        kind, key, val = ev
        if kind == 'eng' and key == e and e == 'pe':
            return
        if self.seen[e].get((kind, key), 0) >= val:
            return
        self.seen[e][(kind, key)] = val
        if kind == 'eng':
            g = (val - 1) // self.GEN
            self.E[e].wait_ge(self.esem[key][g], val - g * self.GEN)
        elif kind == 'dma':
            self.E[e].wait_ge(self.dsem[key], val)
        else:
            self.E[e].wait_ge(self.csem, val)

    def _deps(self, e, R, W):
        for t in R:
            self.wait(e, t.w)
        for t in W:
            self.wait(e, t.w)
            for ev in list(t.r.values()):
                self.wait(e, ev)

    def _commit(self, ev, R, W):
        for t in R:
            t.r[(ev[0], ev[1])] = ev
        for t in W:
            t.w = ev
            t.r = {}

    @staticmethod
    def _freeze(fn):
        if fn.__closure__ is None:
            return fn
        cells = []
        for c in fn.__closure__:
            try:
                cells.append(types.CellType(c.cell_contents))
            except ValueError:
                cells.append(c)
        return types.FunctionType(fn.__code__, fn.__globals__, fn.__name__, fn.__defaults__, tuple(cells))

    def record_begin(self):
        self.rec = []

    def record_end(self):
        r, self.rec = self.rec, None
        return collections.deque(r)

    def emit(self, item):
        kind = item[0]
        if kind == 'op':
            self.op(*item[1:])
        else:
            self.dma(*item[1:-1], indirect=item[-1])

    def op(self, e, fn, R=(), W=()):
        if self.rec is not None:
            self.rec.append(('op', e, self._freeze(fn), list(R), list(W)))
            return
        self._deps(e, R, W)
        ins = fn()
        self.cnt[e] += 1
        n = self.cnt[e]
        g = (n - 1) // self.GEN
        ins.then_inc(self.esem[e][g], 1)
        self._commit(('eng', e, n), R, W)

    def dma(self, q, out, in_, R=(), W=(), indirect=None):
        if self.rec is not None:
            self.rec.append(('dma', q, out, in_, list(R), list(W), indirect))
            return
        self._deps(q, R, W)
        i = self.dnext
        self.dnext = (i + 1) % NDS
        if self.dval[i] > 0:
            self.wait(q, ('dma', i, self.dval[i]))
        if indirect is None:
            ins = self.E[q].dma_start(out=out, in_=in_)
        else:
            ins = self.E[q].indirect_dma_start(out=out, out_offset=None, in_=in_, in_offset=indirect)
        self.dval[i] += 16
        ins.then_inc(self.dsem[i], 16)
        self._commit(('dma', i, self.dval[i]), R, W)

    def cc(self, kind, op, groups, in_t, out_t):
        self._deps('pool', [in_t], [out_t])
        ins = self.nc.gpsimd.collective_compute(kind, op, replica_groups=groups,
                                                ins=[in_t.t.ap().opt()], outs=[out_t.t.ap().opt()])
        self.cval += 1
        ins.then_inc(self.csem)
        self._commit(('cc', 0, self.cval), [in_t], [out_t])

    def barrier(self):
        evs = [('eng', k, self.cnt[k]) for k in self.cnt if self.cnt[k] > 0]
        evs += [('dma', i, self.dval[i]) for i in range(NDS) if self.dval[i] > 0]
        if self.cval:
            evs.append(('cc', 0, self.cval))
        for e in self.E:
            for ev in evs:
                self.wait(e, ev)

    def sb(self, es, shape, dt, name=None):
        self.uid += 1
        return TT(es.enter_context(self.nc.sbuf_tensor(f"{name or 't'}_{self.uid}", list(shape), dt)))

    def dram(self, shape, dt, name):
        return TT(self.nc.dram_tensor(name, list(shape), dt))


class Ring:
    def __init__(self, items):
        self.items = items
        self.i = 0

    def next(self):
        t = self.items[self.i]
        self.i = (self.i + 1) % len(self.items)
        return t


CONST_NAMES = ["ident", "tri", "ones", "maskc", "triB", "onesB", "maskB", "maskB2", "maskT", "su", "maskN0", "maskN1", "msuf"]


NPG_GLOBAL = [16]


def make_consts():
    i = np.arange(128)
    s, t = i[:, None], i[None, :]
    same = (s // 8) == (t // 8)
    c = {}
    c["ident"] = (s == t).astype(np.float32)
    c["tri"] = (s <= t).astype(np.float32)
    c["ones"] = np.ones((128, 128), np.float32)
    c["maskc"] = np.where(t <= s, 0.0, NEG).astype(np.float32)
    c["triB"] = (same & (s <= t)).astype(np.float32)
    c["onesB"] = same.astype(np.float32)
    c["maskB"] = np.where(same & (t <= s), 0.0, NEG).astype(np.float32)
    c["maskB2"] = np.where(same, 0.0, NEG).astype(np.float32)
    c["maskT"] = np.where(s <= t, 0.0, NEG).astype(np.float32)
    c["su"] = (s > t).astype(np.float32)
    key = i[:, None]
    col = np.arange(256)[None, :]
    ii, hq = col // 16, col % 16
    mN = np.where((key // 8 == ii) & (key % 8 <= hq % 8), 0.0, NEG).astype(np.float32)
    c["maskN0"], c["maskN1"] = mN[:, :128], mN[:, 128:]
    ms = np.zeros((128, 128), np.float32)
    for npg in [NPG_GLOBAL[0]]:
        jj, hh = np.arange(2 * npg) // 2, np.arange(2 * npg) % 2
        ms[:2 * npg, :2 * npg] = ((hh[:, None] == hh[None, :]) & (jj[:, None] > jj[None, :])).astype(np.float32)
    c["msuf"] = ms
    arr = np.concatenate([c[k] for k in CONST_NAMES], axis=1)
    bm = (i[:, None] // 8 == np.arange(16)[None, :]).astype(np.float32)
    sel = (i[:, None] == 8 * np.arange(16)[None, :] + 7).astype(np.float32)
    return np.ascontiguousarray(np.concatenate([arr, bm, sel], axis=1))


NCONST = 128 * len(CONST_NAMES) + 32

ROWS = {}
_o = 0
for _n, _w in [("big_own", 4), ("bfg_own", 4), ("big_full", 8), ("bfg_full", 8), ("bfb_own", 8), ("bfb_full", 16),
               ("kg", 64), ("qg", 64), ("nfin", 1024), ("oh", 8)]:
    ROWS[_n] = (_o, _w)
    _o += _w
NROWS = _o
VECS = {}
_o = 0
for _n, _w in [("norm_a", 8), ("nffn0", 8), ("nffn1", 8), ("norm_kv", 8), ("norm_b", 8), ("mh_own", 4), ("mh_full", 8)]:
    VECS[_n] = (_o, _w)
    _o += _w
NVECS = _o

W_IN_OWN = 4 * (DK + DK + DV + DV) + 8
W_IN_FULL = 2 * HA * DK + 2 * HA * DV + 2 * HA


def build(cfg):
    nc = bass.Bass("TRN2", target_bir_lowering=False)
    es = ExitStack()
    b = B(nc, es)

    def din(name, shape, dt=F32):
        return TT(nc.dram_tensor(name, list(shape), dt, kind="ExternalInput"))

    def dout(name, shape, dt=F32):
        return TT(nc.dram_tensor(name, list(shape), dt, kind="ExternalOutput"))

    TP, NT, HALF, NOWN = cfg.TP, cfg.NT, cfg.HALF, cfg.NOWN
    I = {}
    I["consts"] = din("consts", [128, NCONST])
    I["rows"] = din("rows", [128, NROWS])
    I["vecs"] = din("vecs", [128, NVECS])
    I["xp"] = din("xp", [TP, D])
    I["xs"] = din("xs", [128, D])
    I["sCn"] = din("sCn", [DK, HA, 16, DV + 1])
    I["smt"] = din("smt", [128, HA])
    I["w_in_own"] = din("w_in_own", [D, W_IN_OWN])
    I["w_in_full"] = din("w_in_full", [D, W_IN_FULL])
    I["w_oa_own"] = din("w_oa_own", [4 * DV, D])
    I["w_oa_full"] = din("w_oa_full", [HA * DV, D])
    NPG = cfg.NPG
    I["rec"] = din("rec", [cfg.NPHYS * 128, 258])
    I["pt"] = din("pt", [1, 128 * NPG], I32)
    I["xown"] = din("xown", [HALF, D])
    I["w_gu"] = din("w_gu", [2, D, 2 * DFF])
    I["w_d"] = din("w_d", [2, DFF, D])
    I["w_kvf_own"] = din("w_kvf_own", [D, 1032])
    I["w_kvf_full"] = din("w_kvf_full", [D, 2064])
    I["w_qo_own"] = din("w_qo_own", [D, 1024])
    I["w_qo_full"] = din("w_qo_full", [D, 2048])
    I["w_ob_own"] = din("w_ob_own", [512, D])
    I["w_ob_full"] = din("w_ob_full", [D, D])
    R_ = NOWN * 128
    S = {}
    S["rs1_in"] = b.dram([TP, D], F32, "rs1_in")
    S["rs1_out"] = b.dram([HALF, D], F32, "rs1_out")
    S["rs2_in"] = b.dram([TP, D], F32, "rs2_in")
    S["rs2_out"] = b.dram([HALF, D], F32, "rs2_out")
    S["hs"] = b.dram([128, D], F32, "hs_dram")
    AGC = [(t0, min(4, NOWN - t0)) for t0 in range(0, NOWN, 4)]
    for k, (t0, nt) in enumerate(AGC):
        S[f"ag_in{k}"] = b.dram([nt * 128, D], BF16, f"ag_in{k}")
        S[f"ag_out{k}"] = b.dram([2 * nt * 128, D], BF16, f"ag_out{k}")
    S["xs2"] = b.dram([128, D], BF16, "xs2_dram")
    S["h"] = b.dram([(NOWN + 1) * 128, D], F32, "h_dram")
    G1C = [(c0, min(512, 3088 - c0)) for c0 in range(0, 3088, 512)]
    for k, (c0, w) in enumerate(G1C):
        S[f"g1_in{k}"] = b.dram([128, w], F32, f"g1_in{k}")
        S[f"g1_q{k}"] = b.dram([4 * 128, w], F32, f"g1_q{k}")
        S[f"g1_all{k}"] = b.dram([8 * 128, w], F32, f"g1_all{k}")
    S["sgs"] = b.dram([128, D], F32, "sgs_dram")
    for k in range(2):
        S[f"g2_in{k}"] = b.dram([4 * 128, 128], F32, f"g2_in{k}")
        S[f"g2_q{k}"] = b.dram([4 * 4 * 128, 128], F32, f"g2_q{k}")
        S[f"g2_all{k}"] = b.dram([8 * 4 * 128, 128], F32, f"g2_all{k}")
    S["os"] = b.dram([128, D], F32, "os_dram")
    for k in range(4):
        S[f"recd{k}"] = b.dram([32 * 128, NPG * 258], F32, f"recd{k}")
    O = {}
    O["y_own"] = dout("y_own", [R_, D])
    O["y_s"] = dout("y_s", [128, D])
    O["pk"] = dout("pk", [TP, 512])
    O["pv"] = dout("pv", [TP, 512])
    O["plf"] = dout("plf", [TP, 8])
    O["sk"] = dout("sk", [128, D])
    O["sv"] = dout("sv", [128, D])
    O["slf"] = dout("slf", [128, 16])
    O["pC"] = dout("pC", [4, DK, DV + 1])
    O["pm"] = dout("pm", [1, 4])
    O["sCo"] = dout("sCo", [16, HA, DK, DV + 1])
    O["smo"] = dout("smo", [16, HA])

    cst = b.sb(es, [128, NCONST], F32, "cst")
    rows = b.sb(es, [128, NROWS], F32, "rows")
    vecs = b.sb(es, [128, NVECS], F32, "vecs")
    identb = b.sb(es, [128, 128], BF16, "identb")
    b.dma('sp', cst.t[:], I["consts"].t[:, :], R=[I["consts"]], W=[cst])
    b.dma('sp', rows.t[:], I["rows"].t[:, :], R=[I["rows"]], W=[rows])
    b.dma('sp', vecs.t[:], I["vecs"].t[:, :], R=[I["vecs"]], W=[vecs])
    b.op('dve', lambda: nc.vector.tensor_copy(out=identb.t[:], in_=cst.t[:, 0:128]), R=[cst], W=[identb])

    def C(name, r=128, c=128):
        o = CONST_NAMES.index(name) * 128
        return cst.t[0:r, o:o + c]

    BM = lambda: cst.t[:, 128 * len(CONST_NAMES):128 * len(CONST_NAMES) + 16]
    SEL = lambda: cst.t[:, 128 * len(CONST_NAMES) + 16:128 * len(CONST_NAMES) + 32]

    def ROW(name, r=128, lo=0, hi=None):
        o, w = ROWS[name]
        hi = w if hi is None else hi
        return rows.t[0:r, o + lo:o + hi]

    def VEC(name, j):
        o, w = VECS[name]
        return vecs.t[:, o + j:o + j + 1]

    psf = Ring([TT(es.enter_context(nc.psum_tensor(f"psf{i}", [128, 512], F32))) for i in range(4)])
    pso = Ring([TT(es.enter_context(nc.psum_tensor(f"pso{i}", [128, 512], F32))) for i in range(2)])
    psb = Ring([TT(es.enter_context(nc.psum_tensor(f"psb{i}", [128, 1024], BF16))) for i in range(2)])

    wstage = Ring([b.sb(es, [128, 1024], F32, "wst") for _ in range(3)])

    wcnt = [0]

    def load_w(dst, src_ap_fn, KC, N, gain=None, q='sp'):
        for k in range(KC):
            for c0 in range(0, N, 1024):
                n = min(1024, N - c0)
                st = wstage.next()
                b.dma(q, st.t[:, 0:n], src_ap_fn(k, c0, n), R=[], W=[st])
                wcnt[0] += 1
                if wcnt[0] % 2 == 0:
                    if gain is None:
                        b.op('act', lambda: nc.scalar.copy(out=dst.t[:, k, c0:c0 + n], in_=st.t[:, 0:n]), R=[st], W=[dst])
                    else:
                        b.op('act', lambda: nc.scalar.activation(out=dst.t[:, k, c0:c0 + n], in_=st.t[:, 0:n], func=AF.Copy, scale=gain(k)), R=[st, vecs], W=[dst])
                else:
                    if gain is None:
                        b.op('dve', lambda: nc.vector.tensor_copy(out=dst.t[:, k, c0:c0 + n], in_=st.t[:, 0:n]), R=[st], W=[dst])
                    else:
                        b.op('dve', lambda: nc.vector.tensor_scalar(out=dst.t[:, k, c0:c0 + n], in0=st.t[:, 0:n], scalar1=gain(k), scalar2=None, op0=ALU.mult), R=[st, vecs], W=[dst])

    def ring(es_, n, shape, dt, name):
        return Ring([b.sb(es_, shape, dt, name) for _ in range(n)])

    def rms_normalize(src_ap, L, dst_bf, scr, ssq, tiles_R, nfeat=D):
        b.op('act', lambda: nc.scalar.activation(out=scr.t[0:L, 0:nfeat], in_=src_ap, func=AF.Square, accum_out=ssq.t[0:L, 0:1]),
             R=tiles_R, W=[scr, ssq])
        b.op('act', lambda: nc.scalar.activation(out=ssq.t[0:L, 1:2], in_=ssq.t[0:L, 0:1], func=AF.Ln, scale=1.0 / nfeat, bias=EPS),
             R=[ssq], W=[ssq])
        b.op('act', lambda: nc.scalar.activation(out=ssq.t[0:L, 1:2], in_=ssq.t[0:L, 1:2], func=AF.Exp, scale=-0.5), R=[ssq], W=[ssq])
        b.op('dve', lambda: nc.vector.tensor_scalar(out=dst_bf.t[0:L, 0:nfeat], in0=src_ap, scalar1=ssq.t[0:L, 1:2], scalar2=None, op0=ALU.mult),
             R=tiles_R + [ssq], W=[dst_bf])

    def transpose_to(dstT, src_bf, L, nchunk, evac='act', cw=128, c0=0, d0=0):
        ps = psb.next()
        for k in range(nchunk):
            b.op('pe', lambda: nc.tensor.transpose(ps.t[0:cw, k * 128:k * 128 + L], src_bf.t[0:L, c0 + k * cw:c0 + (k + 1) * cw], identb.t[0:L, 0:L]),
                 R=[src_bf, identb], W=[ps])
        view = ps.t[0:cw, 0:nchunk * 128].rearrange("p (k l) -> p k l", l=128)[:, :, 0:L]
        if evac == 'act':
            b.op('act', lambda: nc.scalar.copy(out=dstT.t[0:cw, d0:d0 + nchunk, 0:L], in_=view), R=[ps], W=[dstT])
        else:
            b.op('dve', lambda: nc.vector.tensor_copy(out=dstT.t[0:cw, d0:d0 + nchunk, 0:L], in_=view), R=[ps], W=[dstT])

    def proj_tok(dst, xT, L, W, KC, blocks, evac_engs=('act', 'dve')):
        for bi, (c0, n) in enumerate(blocks):
            ps = psf.next()
            for k in range(KC):
                b.op('pe', lambda: nc.tensor.matmul(ps.t[0:L, 0:n], lhsT=xT.t[:, k, 0:L], rhs=W.t[:, k, c0:c0 + n], start=(k == 0), stop=(k == KC - 1)),
                     R=[xT, W], W=[ps])
            e = evac_engs[bi % len(evac_engs)]
            if e == 'act':
                b.op('act', lambda: nc.scalar.copy(out=dst.t[0:L, c0:c0 + n], in_=ps.t[0:L, 0:n]), R=[ps], W=[dst])
            else:
                b.op('dve', lambda: nc.vector.tensor_copy(out=dst.t[0:L, c0:c0 + n], in_=ps.t[0:L, 0:n]), R=[ps], W=[dst])

    def phase_A(mode):
        pes = ExitStack()
        NH = 4 if mode == 'prompt' else HA
        NB = 2 if mode == 'prompt' else 1
        if mode == 'prompt':
            w_in = b.sb(pes, [128, 8, W_IN_OWN], BF16, "w_in")
            w_oa = b.sb(pes, [128, 4, D], BF16, "w_oa")
            load_w(w_in, lambda k, c0, n: I["w_in_own"].t[k * 128:(k + 1) * 128, c0:c0 + n], 8, W_IN_OWN, gain=lambda k: VEC("norm_a", k))
            load_w(w_oa, lambda k, c0, n: I["w_oa_own"].t[k * 128:(k + 1) * 128, c0:c0 + n], 4, D, gain=lambda k: VEC("mh_own", k))
        else:
            w_in = b.sb(pes, [128, 8, W_IN_FULL], BF16, "w_inf")
            w_oa = b.sb(pes, [128, 8, D], BF16, "w_oaf")
            load_w(w_in, lambda k, c0, n: I["w_in_full"].t[k * 128:(k + 1) * 128, c0:c0 + n], 8, W_IN_FULL, gain=lambda k: VEC("norm_a", k))
            load_w(w_oa, lambda k, c0, n: I["w_oa_full"].t[k * 128:(k + 1) * 128, c0:c0 + n], 8, D, gain=lambda k: VEC("mh_full", k))
        xt_r = ring(pes, NB, [128, D], F32, "xt")
        xh_r = ring(pes, NB, [128, D], BF16, "xh")
        scr = b.sb(pes, [128, D], F32, "scr")
        ssq_r = ring(pes, 2, [128, 2], F32, "ssq")
        xT_r = ring(pes, NB, [128, 8, 128], BF16, "xT")
        p_r = ring(pes, NB, [128, W_IN_FULL if mode != "prompt" else W_IN_OWN], F32, "p")
        pb_r = ring(pes, NB, [128, 2 * NH * DK + NH * (DV + 1)], BF16, "pb")
        qkT_r = ring(pes, NB, [64, 2 * NH, 128], BF16, "qkT")
        g_r = ring(pes, NB, [128, 48], F32, "gates")
        gated_r = ring(pes, NB, [128, NH * DV], BF16, "gated")
        gT_r = ring(pes, NB, [128, 8, 128], BF16, "gT")
        aout_r = ring(pes, NB, [128, D], F32, "aout")
        uB_r = ring(pes, 4, [128, 128], F32, "uB")
        dm_r = ring(pes, 4, [128, 128], F32, "dm")
        de_r = ring(pes, 4, [128, 128], F32, "de")
        sm_r = ring(pes, 4, [128, 128], BF16, "smm")
        smT_r = ring(pes, 4, [128, 128], BF16, "smT")
        kw_r = ring(pes, NH + 1, [128, DK], BF16, "kw")
        sc_r = ring(pes, (2 * NH + 2) if mode == "prompt" else 4, [128, 16], F32, "sc")
        svs_r = ring(pes, NH + 1, [128, DV + 1], F32, "svs")
        num_r = ring(pes, 4, [128, DV + 1], F32, "num")
        hr_r = ring(pes, 4, [128, DV], F32, "hr")
        sg_r = ring(pes, NB, [128, NH * DV], F32, "sg")
        ut_r = ring(pes, 4, [DK, DV + 1], F32, "ut")
        if mode == 'prompt':
            PGS = 3
            pg_rec = [b.sb(pes, [128, NPG, 258], F32, "pgrec") for _ in range(PGS)]
            pg_ptb = b.sb(pes, [128, 128 * NPG], I32, "pgptb")
            pg_idx = b.sb(pes, [128, 128 * NPG], I32, "pgidx")
            pg_iot = b.sb(pes, [128, 1], I32, "pgiot")
            b.dma('pool', pg_ptb.t[:, :], I["pt"].t[0:1, :].partition_broadcast(128), W=[pg_ptb])
            b.op('pool', lambda: nc.gpsimd.iota(pg_iot.t[:, :], pattern=[[0, 1]], base=0, channel_multiplier=1), W=[pg_iot])
            b.op('pool', lambda: nc.gpsimd.tensor_scalar(out=pg_idx.t[:, :], in0=pg_ptb.t[:, :], scalar1=128, scalar2=pg_iot.t[:, 0:1], op0=ALU.mult, op1=ALU.add),
                 R=[pg_ptb, pg_iot], W=[pg_idx])
            pg_state = {'next_g': 0, 'next_s': 0}

            def pg_gather(bg):
                rec = pg_rec[bg % PGS]
                for j in range(NPG):
                    b.dma('pool', rec.t[:, j, :], I["rec"].t[:, :], R=[pg_idx], W=[rec],
                          indirect=bass.IndirectOffsetOnAxis(ap=pg_idx.t[:, bg * NPG + j:bg * NPG + j + 1], axis=0))

            def pg_store(bg):
                rec = pg_rec[bg % PGS]
                b.dma('pool', S[f"recd{bg // 32}"].t[(bg % 32) * 128:(bg % 32 + 1) * 128, :], rec.t[:, :, :].rearrange("p j e -> p (j e)"), R=[rec], W=[S[f"recd{bg // 32}"]])

            def pg_advance(upto):
                while pg_state['next_g'] < min(upto, 128):
                    pg_gather(pg_state['next_g'])
                    pg_state['next_g'] += 1
                    if pg_state['next_g'] - 1 > pg_state['next_s']:
                        pg_store(pg_state['next_s'])
                        pg_state['next_s'] += 1
                if upto >= 128:
                    while pg_state['next_s'] < 128:
                        pg_store(pg_state['next_s'])
                        pg_state['next_s'] += 1

            Cst = [b.sb(pes, [DK, DV + 1], F32, "Cst") for _ in range(NH)]
            Cbf = [b.sb(pes, [DK, DV + 1], BF16, "Cbf") for _ in range(NH)]
            mrep = [b.sb(pes, [128, 1], F32, "mrep") for _ in range(NH)]
            for h in range(NH):
                b.op('dve', lambda: nc.vector.memset(Cst[h].t[:], 0.0), W=[Cst[h]])
                b.op('dve', lambda: nc.vector.memset(Cbf[h].t[:], 0.0), W=[Cbf[h]])
                b.op('dve', lambda: nc.vector.memset(mrep[h].t[:], 0.0), W=[mrep[h]])
        else:
            CA = b.sb(pes, [DK, 4, 16, DV + 1], F32, "CA")
            CAb_r = ring(pes, 2, [DK, 16, DV + 1], BF16, "CAb")
            Vb_r = ring(pes, 2, [128, 16, DV + 1], BF16, "Vb")
            mtok = b.sb(pes, [128, HA], F32, "mtok")
            mnew = b.sb(pes, [128, HA], F32, "mnew")
            acc_r = ring(pes, 2, [128, DV + 1], F32, "acc")
            cB_r = ring(pes, 2, [128, 2, DK], F32, "cB")
            wrow_r = ring(pes, 2, [DK, 32], F32, "wrow")
            b.dma('sp', mtok.t[:], I["smt"].t[:, :], W=[mtok])

        def gates(p, L, nh, col_gi, col_gf, big, bfg, tri, ones_m):
            g = g_r.next()
            li, lf = g.t[0:L, 0:nh], g.t[0:L, nh:2 * nh]
            b.op('dve', lambda: nc.vector.tensor_tensor(out=li, in0=p.t[0:L, col_gi:col_gi + nh], in1=big, op=ALU.add), R=[p, rows], W=[g])
            b.op('dve', lambda: nc.vector.tensor_tensor(out=lf, in0=p.t[0:L, col_gf:col_gf + nh], in1=bfg, op=ALU.add), R=[p, rows], W=[g])
            b.op('act', lambda: nc.scalar.activation(out=g.t[0:L, 0:2 * nh], in_=g.t[0:L, 0:2 * nh], func=AF.Tanh, scale=1.0 / CAP), R=[g], W=[g])
            b.op('dve', lambda: nc.vector.tensor_scalar(out=li, in0=li, scalar1=CAP, scalar2=None, op0=ALU.mult), R=[g], W=[g])
            b.op('act', lambda: nc.scalar.activation(out=lf, in_=lf, func=AF.Exp, scale=-CAP), R=[g], W=[g])
            b.op('act', lambda: nc.scalar.activation(out=lf, in_=lf, func=AF.Ln, bias=1.0), R=[g], W=[g])
            b.op('dve', lambda: nc.vector.tensor_scalar(out=lf, in0=lf, scalar1=-1.0, scalar2=None, op0=ALU.mult), R=[g], W=[g])
            ps = psf.next()
            b.op('pe', lambda: nc.tensor.matmul(ps.t[0:L, 0:nh], lhsT=tri, rhs=lf, start=True, stop=True), R=[cst, g], W=[ps])
            b.op('pe', lambda: nc.tensor.matmul(ps.t[0:128, 8:8 + nh], lhsT=ones_m, rhs=lf, start=True, stop=True), R=[cst, g], W=[ps])
            b.op('dve', lambda: nc.vector.tensor_copy(out=g.t[0:L, 2 * nh:3 * nh], in_=ps.t[0:L, 0:nh]), R=[ps], W=[g])
            b.op('dve', lambda: nc.vector.tensor_copy(out=g.t[0:128, 3 * nh:4 * nh], in_=ps.t[0:128, 8:8 + nh]), R=[ps], W=[g])
            return g

        def qk_prep(p, L, nh, col_q, col_k, col_v):
            pb = pb_r.next()
            nq = nh * DK
            b.op('act', lambda: nc.scalar.activation(out=pb.t[0:L, 0:nq], in_=p.t[0:L, col_q:col_q + nq], func=AF.Copy, scale=DK ** -0.5), R=[p], W=[pb])
            b.op('dve', lambda: nc.vector.tensor_copy(out=pb.t[0:L, nq:2 * nq], in_=p.t[0:L, col_k:col_k + nq]), R=[p], W=[pb])
            vv = pb.t[0:L, 2 * nq:2 * nq + nh * (DV + 1)].rearrange("p (h e) -> p h e", e=DV + 1)
            b.op('dve', lambda: nc.vector.tensor_copy(out=vv[:, :, 0:DV], in_=p.t[0:L, col_v:col_v + nh * DV].rearrange("p (h e) -> p h e", e=DV)), R=[p], W=[pb])
            b.op('dve', lambda: nc.vector.memset(vv[:, :, DV:DV + 1], 1.0), W=[pb])
            qkT = qkT_r.next()
            for j0 in range(0, 2 * nh, 8):
                transpose_to(qkT, pb, L, min(8, 2 * nh - j0), cw=64, c0=j0 * 64, d0=j0, evac=('act' if j0 == 0 else 'dve'))
            return pb, qkT

        POFF = {'R': 0, 'G': 128, 'SV': 256, 'Q': 0, 'U': 256}

        def hps(bank, name):
            if bank is None:
                return psf.next(), 0
            return bank, POFF[name]

        def head_common(g, L, nh, h, qkT, pb, maskc, mask2, bank=None):
            nq = nh * DK
            sc = sc_r.next()
            bcol = g.t[0:L, 2 * nh + h:2 * nh + h + 1]
            b.op('dve', lambda: nc.vector.tensor_tensor(out=g.t[0:L, 4 * nh + h:4 * nh + h + 1], in0=g.t[0:L, h:h + 1], in1=bcol, op=ALU.subtract), R=[g], W=[g])
            ucol = g.t[0:L, 4 * nh + h:4 * nh + h + 1]
            uB = uB_r.next()
            b.op('dve', lambda: nc.vector.tensor_scalar(out=uB.t[0:L, :], in0=C("ones", L, 128), scalar1=ucol, scalar2=None, op0=ALU.mult), R=[g, cst], W=[uB])
            psR, oR = hps(bank, 'R')
            b.op('pe', lambda: nc.tensor.matmul(psR.t[0:128, oR:oR + L], lhsT=uB.t[0:L, :], rhs=C("ident", L, L), start=True, stop=True), R=[uB, cst], W=[psR])
            dm = dm_r.next()
            b.op('dve', lambda: nc.vector.scalar_tensor_tensor(out=dm.t[0:L, 0:L], in0=psR.t[0:L, oR:oR + L], scalar=bcol, in1=maskc, op0=ALU.add, op1=ALU.add),
                 R=[psR, g, cst], W=[dm])
            b.op('dve', lambda: nc.vector.tensor_reduce(out=sc.t[0:L, 0:1], in_=dm.t[0:L, 0:L], axis=AX.X, op=ALU.max), R=[dm], W=[sc])
            if mask2 is None:
                b.op('dve', lambda: nc.vector.tensor_reduce(out=sc.t[0:128, 2:3], in_=psR.t[0:128, oR:oR + L], axis=AX.X, op=ALU.max), R=[psR], W=[sc])
            else:
                de0 = de_r.next()
                b.op('dve', lambda: nc.vector.tensor_tensor(out=de0.t[0:128, 0:L], in0=psR.t[0:128, oR:oR + L], in1=mask2, op=ALU.add), R=[psR, cst], W=[de0])
                b.op('dve', lambda: nc.vector.tensor_reduce(out=sc.t[0:128, 2:3], in_=de0.t[0:128, 0:L], axis=AX.X, op=ALU.max), R=[de0], W=[sc])
            b.op('dve', lambda: nc.vector.tensor_scalar(out=sc.t[0:L, 1:2], in0=sc.t[0:L, 0:1], scalar1=-1.0, scalar2=None, op0=ALU.mult), R=[sc], W=[sc])
            de = de_r.next()
            b.op('act', lambda: nc.scalar.activation(out=de.t[0:L, 0:L], in_=dm.t[0:L, 0:L], func=AF.Exp, bias=sc.t[0:L, 1:2]), R=[dm, sc], W=[de])
            psG, oG = hps(bank, 'G')
            b.op('pe', lambda: nc.tensor.matmul(psG.t[0:L, oG:oG + L], lhsT=qkT.t[0:64, h, 0:L], rhs=qkT.t[0:64, nh + h, 0:L], start=True, stop=True),
                 R=[qkT], W=[psG])
            smm = sm_r.next()
            b.op('dve', lambda: nc.vector.tensor_tensor(out=smm.t[0:L, 0:L], in0=psG.t[0:L, oG:oG + L], in1=de.t[0:L, 0:L], op=ALU.mult), R=[psG, de], W=[smm])
            if bank is None:
                psT, oT = psb.next(), 0
            else:
                psT, oT = psb.items[0], h * 128
            b.op('pe', lambda: nc.tensor.transpose(psT.t[0:L, oT:oT + L], smm.t[0:L, 0:L], identb.t[0:L, 0:L]), R=[smm, identb], W=[psT])
            smT = smT_r.next()
            b.op('act', lambda: nc.scalar.copy(out=smT.t[0:L, 0:L], in_=psT.t[0:L, oT:oT + L]), R=[psT], W=[smT])
            b.op('dve', lambda: nc.vector.tensor_scalar(out=sc.t[0:L, 11:12], in0=sc.t[0:L, 2:3], scalar1=-1.0, scalar2=None, op0=ALU.mult), R=[sc], W=[sc])
            b.op('act', lambda: nc.scalar.activation(out=sc.t[0:L, 9:10], in_=ucol, func=AF.Exp, bias=sc.t[0:L, 11:12]), R=[g, sc], W=[sc])
            kw = kw_r.next()
            b.op('dve', lambda: nc.vector.tensor_scalar(out=kw.t[0:L, :], in0=pb.t[0:L, nq + h * DK:nq + (h + 1) * DK], scalar1=sc.t[0:L, 9:10], scalar2=None, op0=ALU.mult),
                 R=[pb, sc], W=[kw])
            vp = pb.t[0:L, 2 * nq + h * (DV + 1):2 * nq + (h + 1) * (DV + 1)]
            psSV, oS = hps(bank, 'SV')
            b.op('pe', lambda: nc.tensor.matmul(psSV.t[0:L, oS:oS + DV + 1], lhsT=smT.t[0:L, 0:L], rhs=vp, start=True, stop=True), R=[smT, pb], W=[psSV])
            svs = svs_r.next()
            b.op('act', lambda: nc.scalar.copy(out=svs.t[0:L, :], in_=psSV.t[0:L, oS:oS + DV + 1]), R=[psSV], W=[svs])
            return dict(sc=sc, kw=kw, vp=vp, svs=svs, bcol=bcol)

        def finish_head(d, g, L, nh, h, p, col_og, psQC, gated, mcol):
            sc = d['sc']
            b.op('dve', lambda: nc.vector.tensor_tensor(out=sc.t[0:L, 3:4], in0=d['bcol'], in1=mcol[0], op=ALU.add), R=[g] + mcol[1], W=[sc])
            b.op('dve', lambda: nc.vector.tensor_tensor(out=sc.t[0:L, 4:5], in0=sc.t[0:L, 3:4], in1=sc.t[0:L, 0:1], op=ALU.max), R=[sc], W=[sc])
            b.op('dve', lambda: nc.vector.tensor_scalar(out=sc.t[0:L, 8:9], in0=sc.t[0:L, 4:5], scalar1=-1.0, scalar2=None, op0=ALU.mult), R=[sc], W=[sc])
            b.op('act', lambda: nc.scalar.activation(out=sc.t[0:L, 5:6], in_=sc.t[0:L, 0:1], func=AF.Exp, bias=sc.t[0:L, 8:9]), R=[sc], W=[sc])
            b.op('act', lambda: nc.scalar.activation(out=sc.t[0:L, 6:7], in_=sc.t[0:L, 3:4], func=AF.Exp, bias=sc.t[0:L, 8:9]), R=[sc], W=[sc])
            b.op('act', lambda: nc.scalar.activation(out=sc.t[0:L, 7:8], in_=sc.t[0:L, 8:9], func=AF.Exp), R=[sc], W=[sc])
            svs = d['svs']
            b.op('dve', lambda: nc.vector.tensor_scalar(out=svs.t[0:L, :], in0=svs.t[0:L, :], scalar1=sc.t[0:L, 5:6], scalar2=None, op0=ALU.mult), R=[svs, sc], W=[svs])
            num = num_r.next()
            b.op('dve', lambda: nc.vector.scalar_tensor_tensor(out=num.t[0:L, :], in0=psQC[0], scalar=sc.t[0:L, 6:7], in1=svs.t[0:L, :], op0=ALU.mult, op1=ALU.add),
                 R=psQC[1] + [sc, svs], W=[num])
            b.op('dve', lambda: nc.vector.tensor_scalar(out=sc.t[0:L, 10:11], in0=num.t[0:L, DV:DV + 1], scalar1=-1.0, scalar2=None, op0=ALU.mult), R=[num], W=[sc])
            b.op('dve', lambda: nc.vector.tensor_tensor(out=sc.t[0:L, 10:11], in0=sc.t[0:L, 10:11], in1=num.t[0:L, DV:DV + 1], op=ALU.max), R=[num, sc], W=[sc])
            b.op('dve', lambda: nc.vector.tensor_tensor(out=sc.t[0:L, 10:11], in0=sc.t[0:L, 10:11], in1=sc.t[0:L, 7:8], op=ALU.max), R=[sc], W=[sc])
            b.op('dve', lambda: nc.vector.reciprocal(out=sc.t[0:L, 10:11], in_=sc.t[0:L, 10:11]), R=[sc], W=[sc])
            hr = hr_r.next()
            b.op('dve', lambda: nc.vector.tensor_scalar(out=hr.t[0:L, :], in0=num.t[0:L, 0:DV], scalar1=sc.t[0:L, 10:11], scalar2=None, op0=ALU.mult), R=[num, sc], W=[hr])
            b.op('act', lambda: nc.scalar.activation(out=num.t[0:L, 0:DV], in_=hr.t[0:L, :], func=AF.Square, accum_out=sc.t[0:L, 12:13]), R=[hr], W=[num, sc])
            b.op('act', lambda: nc.scalar.activation(out=sc.t[0:L, 13:14], in_=sc.t[0:L, 12:13], func=AF.Ln, scale=1.0 / DV, bias=EPS), R=[sc], W=[sc])
            b.op('act', lambda: nc.scalar.activation(out=sc.t[0:L, 13:14], in_=sc.t[0:L, 13:14], func=AF.Exp, scale=-0.5), R=[sc], W=[sc])
            return hr, sc

        def og_sigmoid(p, L, nh, col_og):
            sg = sg_r.next()
            b.op('act', lambda: nc.scalar.activation(out=sg.t[0:L, 0:nh * DV], in_=p.t[0:L, col_og:col_og + nh * DV], func=AF.Sigmoid), R=[p], W=[sg])
            return sg

        def gate_out(hr, sc, sg, gated, L, h):
            b.op('dve', lambda: nc.vector.scalar_tensor_tensor(out=gated.t[0:L, h * DV:(h + 1) * DV], in0=hr.t[0:L, :], scalar=sc.t[0:L, 13:14],
                                                               in1=sg.t[0:L, h * DV:(h + 1) * DV], op0=ALU.mult, op1=ALU.mult), R=[hr, sc, sg], W=[gated])

        if mode == 'prompt':
            col_q, col_k, col_v, col_og, col_gi, col_gf = 0, 256, 512, 1024, 1536, 1540
            blocks_own = [(0, 512), (512, 512), (1024, 512), (1536, 8)]
            for i in range(NT):
                pg_advance(-(-128 * (i + 1) // NT))
                r0, r1 = cfg.trange(i)
                L = r1 - r0
                xt = xt_r.next()
                b.dma('sp', xt.t[0:L, :], I["xp"].t[r0:r1, :], W=[xt])
                xh = xh_r.next()
                rms_normalize(xt.t[0:L, :], L, xh, scr, ssq_r.next(), [xt])
                xT = xT_r.next()
                transpose_to(xT, xh, L, 8)
                p = p_r.next()
                proj_tok(p, xT, L, w_in, 8, blocks_own)
                g = gates(p, L, NH, col_gi, col_gf, ROW("big_own", L), ROW("bfg_own", L), C("tri", L, L), C("ones", L, 128))
                pb, qkT = qk_prep(p, L, NH, col_q, col_k, col_v)
                sg = og_sigmoid(p, L, NH, col_og)
                gated = gated_r.next()
                streams = []
                for h in range(NH):
                    bank = psf.items[h]
                    b.record_begin()
                    d = head_common(g, L, NH, h, qkT, pb, C("maskc", L, L), None, bank=bank)
                    sc = d['sc']
                    b.op('pe', lambda: nc.tensor.matmul(bank.t[0:L, 0:DV + 1], lhsT=qkT.t[0:64, h, 0:L], rhs=Cbf[h].t[:, :], start=True, stop=True), R=[qkT, Cbf[h]], W=[bank])
                    hr, sc = finish_head(d, g, L, NH, h, p, col_og, (bank.t[0:L, 0:DV + 1], [bank]), gated, (mrep[h].t[0:L, 0:1], [mrep[h]]))
                    gate_out(hr, sc, sg, gated, L, h)
                    bend = g.t[0:128, 3 * NH + h:3 * NH + h + 1]
                    sc2 = sc_r.next()
                    b.op('dve', lambda: nc.vector.tensor_tensor(out=sc2.t[:, 0:1], in0=bend, in1=mrep[h].t[:, 0:1], op=ALU.add), R=[g, mrep[h]], W=[sc2])
                    b.op('dve', lambda: nc.vector.tensor_tensor(out=sc2.t[:, 1:2], in0=bend, in1=sc.t[:, 2:3], op=ALU.add), R=[g, sc], W=[sc2])
                    b.op('dve', lambda: nc.vector.tensor_tensor(out=sc2.t[:, 2:3], in0=sc2.t[:, 0:1], in1=sc2.t[:, 1:2], op=ALU.max), R=[sc2], W=[sc2])
                    b.op('dve', lambda: nc.vector.tensor_scalar(out=sc2.t[:, 3:4], in0=sc2.t[:, 2:3], scalar1=-1.0, scalar2=None, op0=ALU.mult), R=[sc2], W=[sc2])
                    b.op('act', lambda: nc.scalar.activation(out=sc2.t[:, 4:5], in_=sc2.t[:, 0:1], func=AF.Exp, bias=sc2.t[:, 3:4]), R=[sc2], W=[sc2])
                    b.op('act', lambda: nc.scalar.activation(out=sc2.t[:, 5:6], in_=sc2.t[:, 1:2], func=AF.Exp, bias=sc2.t[:, 3:4]), R=[sc2], W=[sc2])
                    b.op('pe', lambda: nc.tensor.matmul(bank.t[0:DK, 256:256 + DV + 1], lhsT=d['kw'].t[0:L, :], rhs=d['vp'], start=True, stop=True), R=[d['kw'], pb], W=[bank])
                    ut = ut_r.next()
                    b.op('act', lambda: nc.scalar.activation(out=ut.t[:, :], in_=bank.t[0:DK, 256:256 + DV + 1], func=AF.Copy, scale=sc2.t[0:DK, 5:6]), R=[bank, sc2], W=[ut])
                    b.op('dve', lambda: nc.vector.scalar_tensor_tensor(out=Cst[h].t[:, :], in0=Cst[h].t[:, :], scalar=sc2.t[0:DK, 4:5], in1=ut.t[:, :], op0=ALU.mult, op1=ALU.add),
                         R=[Cst[h], sc2, ut], W=[Cst[h]])
                    b.op('act', lambda: nc.scalar.copy(out=Cbf[h].t[:, :], in_=Cst[h].t[:, :]), R=[Cst[h]], W=[Cbf[h]])
                    b.op('dve', lambda: nc.vector.tensor_copy(out=mrep[h].t[:, 0:1], in_=sc2.t[:, 2:3]), R=[sc2], W=[mrep[h]])
                    streams.append(b.record_end())
                while any(streams):
                    for st in streams:
                        if st:
                            b.emit(st.popleft())
                gT = gT_r.next()
                transpose_to(gT, gated, L, NH)
                aout = aout_r.next()
                proj_tok(aout, gT, L, w_oa, NH, [(0, 512), (512, 512)])
                b.dma('sp', S["rs1_in"].t[r0:r1, :], aout.t[0:L, :], R=[aout], W=[S["rs1_in"]])
            for h in range(NH):
                b.dma('sp', O["pC"].t[h, :, :], Cst[h].t[:, :], R=[Cst[h]], W=[O["pC"]])
                b.dma('sp', O["pm"].t[0:1, h:h + 1], mrep[h].t[0:1, 0:1], R=[mrep[h]], W=[O["pm"]])

        else:
            col_q, col_k, col_v, col_og, col_gi, col_gf = 0, 512, 1024, 2048, 3072, 3080
            blocks_full = [(0, 512), (512, 512), (1024, 512), (1536, 512), (2048, 512), (2560, 512), (3072, 16)]
            L = 128
            xt = xt_r.next()
            b.dma('sp', xt.t[:, :], I["xs"].t[:, :], W=[xt])
            xh = xh_r.next()
            rms_normalize(xt.t[0:L, :], L, xh, scr, ssq_r.next(), [xt])
            xT = xT_r.next()
            transpose_to(xT, xh, L, 8)
            p = p_r.next()
            proj_tok(p, xT, L, w_in, 8, blocks_full)
            g = gates(p, L, NH, col_gi, col_gf, ROW("big_full", L), ROW("bfg_full", L), C("triB", L, L), C("onesB", L, 128))
            pb, qkT = qk_prep(p, L, NH, col_q, col_k, col_v)
            sg = og_sigmoid(p, L, NH, col_og)
            gated = gated_r.next()
            groups = [(0, 3), (3, 3), (6, 3), (9, 3), (12, 3), (15, 1)]
            for h in range(NH):
                hl = h % 4
                if hl == 0:
                    b.dma('sp', CA.t[:], I["sCn"].t[:, h:h + 4, :, :], W=[CA])
                d = head_common(g, L, NH, h, qkT, pb, C("maskB", L, L), C("maskB2", 128, L))
                sc = d['sc']
                CAb = CAb_r.next()
                b.op('pool', lambda: nc.gpsimd.tensor_copy(out=CAb.t[:, :, :], in_=CA.t[:, hl, :, :]), R=[CA], W=[CAb])
                acc = acc_r.next()
                for (b0, nb) in groups:
                    psQ = psf.next()
                    b.op('pe', lambda: nc.tensor.matmul(psQ.t[0:L, 0:nb * (DV + 1)], lhsT=qkT.t[0:64, h, 0:L],
                                                        rhs=CAb.t[:, b0:b0 + nb, :].rearrange("p b e -> p (b e)"), start=True, stop=True), R=[qkT, CAb], W=[psQ])
                    for j in range(nb):
                        bb = b0 + j
                        src = psQ.t[0:L, j * (DV + 1):(j + 1) * (DV + 1)]
                        if bb == 0:
                            b.op('dve', lambda: nc.vector.tensor_scalar(out=acc.t[:, :], in0=src, scalar1=BM()[:, bb:bb + 1], scalar2=None, op0=ALU.mult), R=[psQ, cst], W=[acc])
                        else:
                            b.op('dve', lambda: nc.vector.scalar_tensor_tensor(out=acc.t[:, :], in0=src, scalar=BM()[:, bb:bb + 1], in1=acc.t[:, :], op0=ALU.mult, op1=ALU.add),
                                 R=[psQ, cst, acc], W=[acc])
                hr, sc = finish_head(d, g, L, NH, h, p, col_og, (acc.t[0:L, :], [acc]), gated, (mtok.t[0:L, h:h + 1], [mtok]))
                gate_out(hr, sc, sg, gated, L, h)
                bend = g.t[0:128, 3 * NH + h:3 * NH + h + 1]
                sc2 = sc_r.next()
                b.op('dve', lambda: nc.vector.tensor_tensor(out=sc2.t[:, 0:1], in0=bend, in1=mtok.t[:, h:h + 1], op=ALU.add), R=[g, mtok], W=[sc2])
                b.op('dve', lambda: nc.vector.tensor_tensor(out=sc2.t[:, 1:2], in0=bend, in1=sc.t[:, 2:3], op=ALU.add), R=[g, sc], W=[sc2])
                b.op('dve', lambda: nc.vector.tensor_tensor(out=sc2.t[:, 2:3], in0=sc2.t[:, 0:1], in1=sc2.t[:, 1:2], op=ALU.max), R=[sc2], W=[sc2])
                b.op('dve', lambda: nc.vector.tensor_scalar(out=sc2.t[:, 3:4], in0=sc2.t[:, 2:3], scalar1=-1.0, scalar2=None, op0=ALU.mult), R=[sc2], W=[sc2])
                b.op('act', lambda: nc.scalar.activation(out=sc2.t[:, 4:5], in_=sc2.t[:, 0:1], func=AF.Exp, bias=sc2.t[:, 3:4]), R=[sc2], W=[sc2])
                b.op('act', lambda: nc.scalar.activation(out=sc2.t[:, 5:6], in_=sc2.t[:, 1:2], func=AF.Exp, bias=sc2.t[:, 3:4]), R=[sc2], W=[sc2])
                b.op('dve', lambda: nc.vector.tensor_copy(out=mnew.t[:, h:h + 1], in_=sc2.t[:, 2:3]), R=[sc2], W=[mnew])
                cB = cB_r.next()
                b.op('dve', lambda: nc.vector.tensor_scalar(out=cB.t[:, 0, :], in0=C("ones", 128, DK), scalar1=sc2.t[:, 4:5], scalar2=None, op0=ALU.mult), R=[sc2, cst], W=[cB])
                b.op('dve', lambda: nc.vector.tensor_scalar(out=cB.t[:, 1, :], in0=C("ones", 128, DK), scalar1=sc2.t[:, 5:6], scalar2=None, op0=ALU.mult), R=[sc2, cst], W=[cB])
                psW = psf.next()
                b.op('pe', lambda: nc.tensor.matmul(psW.t[0:DK, 0:16], lhsT=cB.t[:, 0, :], rhs=SEL(), start=True, stop=True), R=[cB, cst], W=[psW])
                b.op('pe', lambda: nc.tensor.matmul(psW.t[0:DK, 16:32], lhsT=cB.t[:, 1, :], rhs=SEL(), start=True, stop=True), R=[cB, cst], W=[psW])
                wrow = wrow_r.next()
                b.op('act', lambda: nc.scalar.copy(out=wrow.t[:, :], in_=psW.t[0:DK, 0:32]), R=[psW], W=[wrow])
                Vb = Vb_r.next()
                for bb in range(16):
                    b.op('pool', lambda: nc.gpsimd.tensor_scalar(out=Vb.t[:, bb, :], in0=d['vp'], scalar1=BM()[:, bb:bb + 1], scalar2=None, op0=ALU.mult), R=[pb, cst], W=[Vb])
                for (b0, nb) in groups:
                    psU = psf.next()
                    b.op('pe', lambda: nc.tensor.matmul(psU.t[0:DK, 0:nb * (DV + 1)], lhsT=d['kw'].t[0:L, :], rhs=Vb.t[:, b0:b0 + nb, :].rearrange("p b e -> p (b e)"),
                                                        start=True, stop=True), R=[d['kw'], Vb], W=[psU])
                    for j in range(nb):
                        bb = b0 + j
                        ut = ut_r.next()
                        b.op('act', lambda: nc.scalar.activation(out=ut.t[:, :], in_=psU.t[0:DK, j * (DV + 1):(j + 1) * (DV + 1)], func=AF.Copy, scale=wrow.t[:, 16 + bb:17 + bb]),
                             R=[psU, wrow], W=[ut])
                        b.op('dve', lambda: nc.vector.scalar_tensor_tensor(out=CA.t[:, hl, bb, :], in0=CA.t[:, hl, bb, :], scalar=wrow.t[:, bb:bb + 1], in1=ut.t[:, :],
                                                                           op0=ALU.mult, op1=ALU.add), R=[CA, wrow, ut], W=[CA])
                b.dma('sp', O["sCo"].t[:, h, :, :].rearrange("b k e -> k b e"), CA.t[:, hl, :, :], R=[CA], W=[O["sCo"]])
            gT = gT_r.next()
            transpose_to(gT, gated, L, NH)
            aout = aout_r.next()
            proj_tok(aout, gT, L, w_oa, NH, [(0, 512), (512, 512)])
            b.op('dve', lambda: nc.vector.tensor_tensor(out=aout.t[:, :], in0=aout.t[:, :], in1=xt.t[:, :], op=ALU.add), R=[aout, xt], W=[aout])
            b.dma('sp', S["hs"].t[:, :], aout.t[:, :], R=[aout], W=[S["hs"]])
            b.dma('sp', O["smo"].t[:, :], mnew.t[0:128:8, :], R=[mnew], W=[O["smo"]])
        b.barrier()
        pes.close()


    NTOK = NOWN + 1
    LLAST = HALF - 128 * (NOWN - 1)

    def ffn(pes, l, Hs, gname):
        NTT = NTOK * 128
        xTa = b.sb(pes, [128, 8, NTT], BF16, "xTa")
        xh_r = ring(pes, 2, [128, D], BF16, "fxh")
        scr = b.sb(pes, [128, D], F32, "fscr")
        ssq_r = ring(pes, 2, [128, 2], F32, "fssq")
        xTt_r = ring(pes, 2, [128, 8, 128], BF16, "fxT")
        for t in range(NTOK):
            xh = xh_r.next()
            rms_normalize(Hs[t].t[:, :], 128, xh, scr, ssq_r.next(), [Hs[t]])
            xTt = xTt_r.next()
            transpose_to(xTt, xh, 128, 8)
            b.op('pool', lambda: nc.gpsimd.tensor_copy(out=xTa.t[:, :, t * 128:(t + 1) * 128], in_=xTt.t[:, :, :]), R=[xTt], W=[xTa])
        wg_r = ring(pes, 2, [128, 8, 256], BF16, "wg")
        wu_r = ring(pes, 2, [128, 8, 256], BF16, "wu")
        wd_r = ring(pes, 2, [128, 2, D], BF16, "wd")
        mid_r = ring(pes, 2, [128, 2, NTT], BF16, "mid")
        tmp_r = ring(pes, 2, [128, 512], F32, "ftmp")
        blocks = [(c0, min(512, NTT - c0)) for c0 in range(0, NTT, 512)]
        for gi in range(DFF // 256):
            wg, wu, wd = wg_r.next(), wu_r.next(), wd_r.next()
            load_w(wg, lambda k, c0, n: I["w_gu"].t[l, k * 128:(k + 1) * 128, gi * 256 + c0:gi * 256 + c0 + n], 8, 256, gain=lambda k: VEC(gname, k))
            load_w(wu, lambda k, c0, n: I["w_gu"].t[l, k * 128:(k + 1) * 128, DFF + gi * 256 + c0:DFF + gi * 256 + c0 + n], 8, 256, gain=lambda k: VEC(gname, k))
            load_w(wd, lambda k, c0, n: I["w_d"].t[l, gi * 256 + k * 128:gi * 256 + (k + 1) * 128, c0:c0 + n], 2, D)
            mid = mid_r.next()
            for (c0, nb) in blocks:
                for c in range(2):
                    psG, psU = psf.next(), psf.next()
                    for k in range(8):
                        b.op('pe', lambda: nc.tensor.matmul(psG.t[:, 0:nb], lhsT=wg.t[:, k, c * 128:(c + 1) * 128], rhs=xTa.t[:, k, c0:c0 + nb], start=(k == 0), stop=(k == 7)),
                             R=[wg, xTa], W=[psG])
                    for k in range(8):
                        b.op('pe', lambda: nc.tensor.matmul(psU.t[:, 0:nb], lhsT=wu.t[:, k, c * 128:(c + 1) * 128], rhs=xTa.t[:, k, c0:c0 + nb], start=(k == 0), stop=(k == 7)),
                             R=[wu, xTa], W=[psU])
                    tmp = tmp_r.next()
                    b.op('act', lambda: nc.scalar.activation(out=tmp.t[:, 0:nb], in_=psG.t[:, 0:nb], func=AF.Silu), R=[psG], W=[tmp])
                    b.op('dve', lambda: nc.vector.tensor_tensor(out=mid.t[:, c, c0:c0 + nb], in0=tmp.t[:, 0:nb], in1=psU.t[:, 0:nb], op=ALU.mult), R=[tmp, psU], W=[mid])
            for t in range(NTOK):
                for hf in range(2):
                    psD = psf.next()
                    for c in range(2):
                        b.op('pe', lambda: nc.tensor.matmul(psD.t[:, 0:512], lhsT=mid.t[:, c, t * 128:(t + 1) * 128], rhs=wd.t[:, c, hf * 512:(hf + 1) * 512], start=(c == 0), stop=(c == 1)),
                             R=[mid, wd], W=[psD])
                    b.op('dve', lambda: nc.vector.tensor_tensor(out=Hs[t].t[:, hf * 512:(hf + 1) * 512], in0=Hs[t].t[:, hf * 512:(hf + 1) * 512], in1=psD.t[:, 0:512], op=ALU.add),
                         R=[Hs[t], psD], W=[Hs[t]])

    def phase_B():
        pes = ExitStack()
        b.cc("ReduceScatter", ALU.add, PAIRS, S["rs1_in"], S["rs1_out"])
        Hs = [b.sb(pes, [128, D], F32, "H") for _ in range(NTOK)]
        aes = ExitStack()
        at_r = ring(aes, 2, [128, D], F32, "at")
        for t in range(NOWN):
            L = 128 if t < NOWN - 1 else LLAST
            at = at_r.next()
            if L < 128:
                b.op('pool', lambda: nc.gpsimd.memset(Hs[t].t[:, :], 0.0), W=[Hs[t]])
                b.op('pool', lambda: nc.gpsimd.memset(at.t[:, :], 0.0), W=[at])
            b.dma('sp', Hs[t].t[0:L, :], I["xown"].t[t * 128:t * 128 + L, :], W=[Hs[t]])
            b.dma('sp', at.t[0:L, :], S["rs1_out"].t[t * 128:t * 128 + L, :], R=[S["rs1_out"]], W=[at])
            b.op('dve', lambda: nc.vector.tensor_tensor(out=Hs[t].t[:, :], in0=Hs[t].t[:, :], in1=at.t[:, :], op=ALU.add), R=[Hs[t], at], W=[Hs[t]])
        b.dma('sp', Hs[NOWN].t[:, :], S["hs"].t[:, :], R=[S["hs"]], W=[Hs[NOWN]])
        b.barrier()
        aes.close()
        fes = ExitStack()
        ffn(fes, 0, Hs, "nffn0")
        xh_r = ring(fes, 2, [128, D], BF16, "bxh")
        scr = b.sb(fes, [128, D], F32, "bscr")
        ssq_r = ring(fes, 2, [128, 2], F32, "bssq")
        for t in range(NTOK):
            xh = xh_r.next()
            rms_normalize(Hs[t].t[:, :], 128, xh, scr, ssq_r.next(), [Hs[t]])
            if t < NOWN:
                kk, tt = t // 4, t % 4
                b.dma('sp', S[f"ag_in{kk}"].t[tt * 128:(tt + 1) * 128, :], xh.t[:, :], R=[xh], W=[S[f"ag_in{kk}"]])
            else:
                b.dma('sp', S["xs2"].t[:, :], xh.t[:, :], R=[xh], W=[S["xs2"]])
            b.dma('sp', S["h"].t[t * 128:(t + 1) * 128, :], Hs[t].t[:, :], R=[Hs[t]], W=[S["h"]])
        for k in range(len(AGC)):
            b.cc("AllGather", ALU.bypass, PAIRS, S[f"ag_in{k}"], S[f"ag_out{k}"])
        b.barrier()
        fes.close()
        pes.close()

    def fox_proj(pes, xh, L, wkv, wqo, nh, bf_row, rings):
        (xT_r, pk_r, pq_r, sq_r, ss_r, kn_r, qn_r, lf_r, sg_r) = rings
        xT = xT_r.next()
        transpose_to(xT, xh, L, 8)
        pk, pq = pk_r.next(), pq_r.next()
        nk = nh * 64
        blk = lambda n: [(c0, min(512, n - c0)) for c0 in range(0, n, 512)]
        proj_tok(pk, xT, L, wkv, 8, blk(2 * nk + nh))
        proj_tok(pq, xT, L, wqo, 8, blk(2 * nk))
        lf = lf_r.next()
        b.op('dve', lambda: nc.vector.tensor_tensor(out=lf.t[0:L, 0:nh], in0=pk.t[0:L, 2 * nk:2 * nk + nh], in1=bf_row, op=ALU.add), R=[pk, rows], W=[lf])
        b.op('act', lambda: nc.scalar.activation(out=lf.t[0:L, 0:nh], in_=lf.t[0:L, 0:nh], func=AF.Exp, scale=-1.0), R=[lf], W=[lf])
        b.op('act', lambda: nc.scalar.activation(out=lf.t[0:L, 0:nh], in_=lf.t[0:L, 0:nh], func=AF.Ln, bias=1.0), R=[lf], W=[lf])
        b.op('dve', lambda: nc.vector.tensor_scalar(out=lf.t[0:L, 0:nh], in0=lf.t[0:L, 0:nh], scalar1=-1.0, scalar2=None, op0=ALU.mult), R=[lf], W=[lf])
        outs = []
        for (src, c0, grow, scale, dst_r) in [(pk, 0, "kg", 1.0, kn_r), (pq, 0, "qg", DH ** -0.5, qn_r)]:
            sq, ss, dst = sq_r.next(), ss_r.next(), dst_r.next()
            b.op('pool', lambda: nc.gpsimd.tensor_tensor(out=sq.t[0:L, 0:nk], in0=src.t[0:L, c0:c0 + nk], in1=src.t[0:L, c0:c0 + nk], op=ALU.mult), R=[src], W=[sq])
            b.op('dve', lambda: nc.vector.tensor_reduce(out=ss.t[0:L, 0:nh], in_=sq.t[0:L, 0:nk].rearrange("p (h e) -> p h e", e=64), axis=AX.X, op=ALU.add), R=[sq], W=[ss])
            b.op('act', lambda: nc.scalar.activation(out=ss.t[0:L, 0:nh], in_=ss.t[0:L, 0:nh], func=AF.Ln, scale=1.0 / 64, bias=EPS), R=[ss], W=[ss])
            b.op('act', lambda: nc.scalar.activation(out=ss.t[0:L, 0:nh], in_=ss.t[0:L, 0:nh], func=AF.Exp, scale=-0.5), R=[ss], W=[ss])
            if scale != 1.0:
                b.op('dve', lambda: nc.vector.tensor_scalar(out=ss.t[0:L, 0:nh], in0=ss.t[0:L, 0:nh], scalar1=scale, scalar2=None, op0=ALU.mult), R=[ss], W=[ss])
            for h in range(nh):
                b.op('dve', lambda: nc.vector.scalar_tensor_tensor(out=dst.t[0:L, h * 64:(h + 1) * 64], in0=src.t[0:L, c0 + h * 64:c0 + (h + 1) * 64], scalar=ss.t[0:L, h:h + 1],
                                                                   in1=ROW(grow, L), op0=ALU.mult, op1=ALU.mult), R=[src, ss, rows], W=[dst])
            outs.append(dst)
        sg = sg_r.next()
        b.op('act', lambda: nc.scalar.activation(out=sg.t[0:L, 0:nk], in_=pq.t[0:L, nk:2 * nk], func=AF.Sigmoid), R=[pq], W=[sg])
        return dict(pk=pk, lf=lf, kn=outs[0], qn=outs[1], sg=sg, nk=nk)

    def fox_rings(pes, nh, n=2):
        nk = nh * 64
        return (ring(pes, n, [128, 8, 128], BF16, "cxT"), ring(pes, n, [128, 2 * nk + nh], F32, "cpk"), ring(pes, n, [128, 2 * nk], F32, "cpq"),
                ring(pes, n, [128, nk], F32, "csq"), ring(pes, n, [128, nh], F32, "css"), ring(pes, n, [128, nk], F32, "ckn"),
                ring(pes, n, [128, nk], F32, "cqn"), ring(pes, n, [128, nh], F32, "clf"), ring(pes, n, [128, nk], F32, "csg"))

    def phase_C0():
        pes = ExitStack()
        wkv = b.sb(pes, [128, 8, 2064], BF16, "wkvf")
        wqo = b.sb(pes, [128, 8, 2048], BF16, "wqof")
        load_w(wkv, lambda k, c0, n: I["w_kvf_full"].t[k * 128:(k + 1) * 128, c0:c0 + n], 8, 2064, gain=lambda k: VEC("norm_kv", k))
        load_w(wqo, lambda k, c0, n: I["w_qo_full"].t[k * 128:(k + 1) * 128, c0:c0 + n], 8, 2048, gain=lambda k: VEC("norm_b", k))
        xh = b.sb(pes, [128, D], BF16, "c0xh")
        b.dma('sp', xh.t[:, :], S["xs2"].t[:, :], R=[S["xs2"]], W=[xh])
        r = fox_proj(pes, xh, 128, wkv, wqo, HB, ROW("bfb_full", 128), fox_rings(pes, HB, 1))
        b.dma('sp', O["sk"].t[:, :], r['kn'].t[:, :], R=[r['kn']], W=[O["sk"]])
        b.dma('sp', O["sv"].t[:, :], r['pk'].t[:, 1024:2048], R=[r['pk']], W=[O["sv"]])
        b.dma('sp', O["slf"].t[:, :], r['lf'].t[:, 0:16], R=[r['lf']], W=[O["slf"]])
        srcs = [(0, r['qn'], 0), (1024, r['kn'], 0), (2048, r['pk'], 1024), (3072, r['lf'], 0)]
        for k, (c0, w) in enumerate(G1C):
            for (pc, tl, tc) in srcs:
                pw = 1024 if pc < 3072 else 16
                lo, hi = max(c0, pc), min(c0 + w, pc + pw)
                if lo < hi:
                    b.dma('sp', S[f"g1_in{k}"].t[:, lo - c0:hi - c0], tl.t[:, tc + lo - pc:tc + hi - pc], R=[tl], W=[S[f"g1_in{k}"]])
        b.dma('sp', S["sgs"].t[:, :], r['sg'].t[:, :], R=[r['sg']], W=[S["sgs"]])
        for k in range(len(G1C)):
            b.cc("AllGather", ALU.bypass, QUADS, S[f"g1_in{k}"], S[f"g1_q{k}"])
            b.cc("AllGather", ALU.bypass, PAIRS, S[f"g1_q{k}"], S[f"g1_all{k}"])
        b.barrier()
        pes.close()

    def phase_C():
        pes = ExitStack()
        NH = 8
        wkv = b.sb(pes, [128, 8, 1032], BF16, "wkvo")
        wqo = b.sb(pes, [128, 8, 1024], BF16, "wqoo")
        wob = b.sb(pes, [128, 4, D], BF16, "wobo")
        load_w(wkv, lambda k, c0, n: I["w_kvf_own"].t[k * 128:(k + 1) * 128, c0:c0 + n], 8, 1032, gain=lambda k: VEC("norm_kv", k))
        load_w(wqo, lambda k, c0, n: I["w_qo_own"].t[k * 128:(k + 1) * 128, c0:c0 + n], 8, 1024, gain=lambda k: VEC("norm_b", k))
        load_w(wob, lambda k, c0, n: I["w_ob_own"].t[k * 128:(k + 1) * 128, c0:c0 + n], 4, D)
        KT = b.sb(pes, [128, 4, TP], BF16, "KT")
        Vst = b.sb(pes, [128, NT, NH, DH + 1], BF16, "Vst")
        Cc = b.sb(pes, [128, NT, NH], F32, "Cc")
        carry = b.sb(pes, [128, NH], F32, "carry")
        b.op('pool', lambda: nc.gpsimd.memset(carry.t[:, :], 0.0), W=[carry])
        b.op('pool', lambda: nc.gpsimd.memset(Vst.t[:, :, :, DH:DH + 1], 1.0), W=[Vst])
        xh_r = ring(pes, 2, [128, D], BF16, "cxh")
        rings = fox_rings(pes, NH, 2)
        knb_r = ring(pes, 2, [128, 512], BF16, "knb")
        qnb_r = ring(pes, 2, [128, 512], BF16, "qnb")
        qT_r = ring(pes, 2, [128, 4, 128], BF16, "qT")
        bias_s = [ring(pes, 2, [128, NT], F32, "bias") for _ in range(2)]
        PT_s = [ring(pes, 3, [128, 128], BF16, "PT") for _ in range(2)]
        tmpS_s = [b.sb(pes, [128, 128], F32, "tmpS") for _ in range(2)]
        rec_s = [b.sb(pes, [128, 1], F32, "rec") for _ in range(2)]
        go_r = ring(pes, 2, [128, 512], BF16, "go")
        goT_r = ring(pes, 2, [128, 4, 128], BF16, "goT")
        ao_r = ring(pes, 2, [128, D], F32, "ao")
        for i in range(NT):
            r0, r1 = cfg.trange(i)
            L = r1 - r0
            xh = xh_r.next()
            a0 = r0
            while a0 < r1:
                rk = a0 // HALF
                loc = a0 - rk * HALF
                kk = loc // 512
                nt = AGC[kk][1]
                a1 = min(r1, (rk + 1) * HALF, rk * HALF + (kk + 1) * 512)
                g0 = rk * nt * 128 + (loc - kk * 512)
                b.dma('sp', xh.t[a0 - r0:a1 - r0, :], S[f"ag_out{kk}"].t[g0:g0 + (a1 - a0), :], R=[S[f"ag_out{kk}"]], W=[xh])
                a0 = a1
            r = fox_proj(pes, xh, L, wkv, wqo, NH, ROW("bfb_own", L), rings)
            pk, lf, kn, qn, sg = r['pk'], r['lf'], r['kn'], r['qn'], r['sg']
            b.dma('sp', O["pk"].t[r0:r1, :], kn.t[0:L, :], R=[kn], W=[O["pk"]])
            b.dma('sp', O["pv"].t[r0:r1, :], pk.t[0:L, 512:1024], R=[pk], W=[O["pv"]])
            b.dma('sp', O["plf"].t[r0:r1, :], lf.t[0:L, 0:NH], R=[lf], W=[O["plf"]])
            knb, qnb = knb_r.next(), qnb_r.next()
            b.op('pool', lambda: nc.gpsimd.tensor_copy(out=knb.t[0:L, :], in_=kn.t[0:L, :]), R=[kn], W=[knb])
            b.op('pool', lambda: nc.gpsimd.tensor_copy(out=qnb.t[0:L, :], in_=qn.t[0:L, :]), R=[qn], W=[qnb])
            b.op('pool', lambda: nc.gpsimd.tensor_copy(out=Vst.t[0:L, i, :, 0:DH], in_=pk.t[0:L, 512:1024].rearrange("p (h e) -> p h e", e=DH)), R=[pk], W=[Vst])
            psk = psb.next()
            for k in range(4):
                b.op('pe', lambda: nc.tensor.transpose(psk.t[:, k * 128:k * 128 + L], knb.t[0:L, k * 128:(k + 1) * 128], identb.t[0:L, 0:L]), R=[knb, identb], W=[psk])
            b.op('act', lambda: nc.scalar.copy(out=KT.t[:, :, r0:r1], in_=psk.t[:, 0:512].rearrange("p (k l) -> p k l", l=128)[:, :, 0:L]), R=[psk], W=[KT])
            qT = qT_r.next()
            transpose_to(qT, qnb, L, 4, evac='dve')
            psc = psf.next()
            b.op('pe', lambda: nc.tensor.matmul(psc.t[0:L, 0:NH], lhsT=C("tri", L, L), rhs=lf.t[0:L, 0:NH], start=True, stop=True), R=[cst, lf], W=[psc])
            b.op('pe', lambda: nc.tensor.matmul(psc.t[0:128, 8:8 + NH], lhsT=C("ones", L, 128), rhs=lf.t[0:L, 0:NH], start=True, stop=True), R=[cst, lf], W=[psc])
            b.op('dve', lambda: nc.vector.tensor_tensor(out=Cc.t[0:L, i, :], in0=psc.t[0:L, 0:NH], in1=carry.t[0:L, :], op=ALU.add), R=[psc, carry], W=[Cc])
            b.op('dve', lambda: nc.vector.tensor_tensor(out=carry.t[:, :], in0=carry.t[:, :], in1=psc.t[0:128, 8:8 + NH], op=ALU.add), R=[psc, carry], W=[carry])
            go = go_r.next()

            def head_stream(h, slot):
                pr, hh = h // 2, (h % 2) * 64
                b.record_begin()
                bias = bias_s[slot].next()
                b.op('dve', lambda: nc.vector.tensor_scalar(out=bias.t[:, 0:i + 1], in0=Cc.t[:, 0:i + 1, h], scalar1=-1.0, scalar2=carry.t[:, h:h + 1], op0=ALU.mult, op1=ALU.add),
                     R=[Cc, carry], W=[bias])
                psO = pso.items[slot]
                banks = [psf.items[2 * slot], psf.items[2 * slot + 1]]
                pend = {}

                def issue_S(j):
                    s0, s1 = cfg.trange(j)
                    Lj = s1 - s0
                    psS = banks[j % 2]
                    b.op('pe', lambda: nc.tensor.matmul(psS.t[0:Lj, 0:L], lhsT=KT.t[hh:hh + 64, pr, s0:s1], rhs=qT.t[hh:hh + 64, pr, 0:L], start=True, stop=True), R=[KT, qT], W=[psS])
                    pend[j] = (psS, Lj)

                issue_S(0)
                for j in range(i + 1):
                    psS, Lj = pend.pop(j)
                    PT = PT_s[slot].next()
                    if j == i:
                        tmpS = tmpS_s[slot]
                        b.op('dve', lambda: nc.vector.tensor_tensor(out=tmpS.t[0:Lj, 0:L], in0=psS.t[0:Lj, 0:L], in1=C("maskT", Lj, L), op=ALU.add), R=[psS, cst], W=[tmpS])
                        b.op('act', lambda: nc.scalar.activation(out=PT.t[0:Lj, 0:L], in_=tmpS.t[0:Lj, 0:L], func=AF.Exp, bias=bias.t[0:Lj, j:j + 1]), R=[tmpS, bias], W=[PT])
                    else:
                        b.op('act', lambda: nc.scalar.activation(out=PT.t[0:Lj, 0:L], in_=psS.t[0:Lj, 0:L], func=AF.Exp, bias=bias.t[0:Lj, j:j + 1]), R=[psS, bias], W=[PT])
                    if j + 1 <= i:
                        issue_S(j + 1)
                    b.op('pe', lambda: nc.tensor.matmul(psO.t[0:L, 0:DH + 1], lhsT=PT.t[0:Lj, 0:L], rhs=Vst.t[0:Lj, j, h, :], start=(j == 0), stop=(j == i)), R=[PT, Vst], W=[psO])
                rec = rec_s[slot]
                b.op('dve', lambda: nc.vector.reciprocal(out=rec.t[0:L, :], in_=psO.t[0:L, DH:DH + 1]), R=[psO], W=[rec])
                b.op('dve', lambda: nc.vector.scalar_tensor_tensor(out=go.t[0:L, h * 64:(h + 1) * 64], in0=psO.t[0:L, 0:DH], scalar=rec.t[0:L, 0:1], in1=sg.t[0:L, h * 64:(h + 1) * 64],
                                                                   op0=ALU.mult, op1=ALU.mult), R=[psO, rec, sg], W=[go])
                return b.record_end()

            hq = collections.deque(range(NH))
            slots2 = [None, None]
            while hq or any(sl for sl in slots2):
                for k in range(2):
                    if not slots2[k] and hq:
                        slots2[k] = head_stream(hq.popleft(), k)
                    if slots2[k]:
                        b.emit(slots2[k].popleft())
            goT = goT_r.next()
            transpose_to(goT, go, L, 4)
            ao = ao_r.next()
            proj_tok(ao, goT, L, wob, 4, [(0, 512), (512, 512)])
            b.dma('sp', S["rs2_in"].t[r0:r1, :], ao.t[0:L, :], R=[ao], W=[S["rs2_in"]])
        b.barrier()
        pes.close()


    def phase_C2():
        pes = ExitStack()
        NQ = 16
        pay_r = ring(pes, 2, [128, 3088], F32, "pay")
        own_r = ring(pes, 2, [128, 3, 128], F32, "own")
        lfo_r = ring(pes, 2, [128, 2], F32, "lfo")
        ownb_r = ring(pes, 2, [128, 2, 128], BF16, "ownb")
        qkT_r = ring(pes, 2, [128, 2, 128], BF16, "sqkT")
        Qbd_r = ring(pes, 2, [128, 16, NQ], BF16, "Qbd")
        vs_r = ring(pes, 2, [128, 2, DH + 1], BF16, "vs")
        bn_r = ring(pes, 2, [128, 2], F32, "bn")
        tN_r = ring(pes, 2, [128, 16, NQ], F32, "tN")
        PN_r = ring(pes, 2, [128, 16, NQ], BF16, "PN")
        KSL = 4
        recs = [b.sb(pes, [128, NPG, 258], F32, "rec") for _ in range(KSL)]
        kTps = [b.sb(pes, [128, NPG, 128], BF16, "kTp") for _ in range(KSL)]
        vps = [b.sb(pes, [128, NPG, 2, DH + 1], BF16, "vp") for _ in range(KSL)]
        for vp in vps:
            b.op('dve', lambda: nc.vector.memset(vp.t[:, :, :, DH:DH + 1], 1.0), W=[vp])
        lfps = [b.sb(pes, [128, NPG, 2], F32, "lfp") for _ in range(KSL)]
        totcs = [b.sb(pes, [2 * NPG, 1], F32, "totc") for _ in range(KSL)]
        totBs = [b.sb(pes, [2 * NPG, 128], F32, "totB") for _ in range(KSL)]
        bPs = [b.sb(pes, [128, NPG, 2], F32, "bP") for _ in range(KSL)]
        tPs = [b.sb(pes, [128, NPG, NQ], F32, "tP") for _ in range(KSL)]
        PPs = [b.sb(pes, [128, NPG, NQ], BF16, "PP") for _ in range(KSL)]
        rcs = [b.sb(pes, [16, 2], F32, "rc") for _ in range(KSL)]
        oall_r = ring(pes, 2, [16, 16, 2, DH], F32, "oall")
        bank_src = pso.items[0]
        OFF_B, OFF_T, OFF_O = 256, 296, 304

        def prep_source(sidx):
            pay = pay_r.next()
            for k, (c0, w) in enumerate(G1C):
                b.dma('sp', pay.t[:, c0:c0 + w], S[f"g1_all{k}"].t[sidx * 128:(sidx + 1) * 128, :], R=[S[f"g1_all{k}"]], W=[pay])
            own, lfo = own_r.next(), lfo_r.next()
            for f_ in range(3):
                for dd in range(8):
                    src = pay.t[:, f_ * 1024 + dd * 128:f_ * 1024 + (dd + 1) * 128]
                    if dd == 0:
                        b.op('dve', lambda: nc.vector.tensor_scalar(out=own.t[:, f_, :], in0=src, scalar1=ROW("oh", 128, 0, 1), scalar2=None, op0=ALU.mult), R=[pay, rows], W=[own])
                    else:
                        b.op('dve', lambda: nc.vector.scalar_tensor_tensor(out=own.t[:, f_, :], in0=src, scalar=ROW("oh", 128, dd, dd + 1), in1=own.t[:, f_, :], op0=ALU.mult, op1=ALU.add),
                             R=[pay, rows, own], W=[own])
            for dd in range(8):
                src = pay.t[:, 3072 + 2 * dd:3072 + 2 * dd + 2]
                if dd == 0:
                    b.op('dve', lambda: nc.vector.tensor_scalar(out=lfo.t[:, :], in0=src, scalar1=ROW("oh", 128, 0, 1), scalar2=None, op0=ALU.mult), R=[pay, rows], W=[lfo])
                else:
                    b.op('dve', lambda: nc.vector.scalar_tensor_tensor(out=lfo.t[:, :], in0=src, scalar=ROW("oh", 128, dd, dd + 1), in1=lfo.t[:, :], op0=ALU.mult, op1=ALU.add),
                         R=[pay, rows, lfo], W=[lfo])
            ownb = ownb_r.next()
            b.op('act', lambda: nc.scalar.copy(out=ownb.t[:, :, :], in_=own.t[:, 0:2, :]), R=[own], W=[ownb])
            qkT = qkT_r.next()
            pst = psb.next()
            for k in range(2):
                b.op('pe', lambda: nc.tensor.transpose(pst.t[:, k * 128:(k + 1) * 128], ownb.t[:, k, :], identb.t[:, :]), R=[ownb, identb], W=[pst])
            b.op('act', lambda: nc.scalar.copy(out=qkT.t[:, :, :], in_=pst.t[:, 0:256].rearrange("p (k l) -> p k l", l=128)), R=[pst], W=[qkT])
            Qbd = Qbd_r.next()
            b.op('dve', lambda: nc.vector.memset(Qbd.t[:, :, :], 0.0), W=[Qbd])
            b.op('dve', lambda: nc.vector.tensor_copy(out=Qbd.t[0:64, :, 0:8], in_=qkT.t[0:64, 0, :].rearrange("p (i q) -> p i q", q=8)), R=[qkT], W=[Qbd])
            b.op('dve', lambda: nc.vector.tensor_copy(out=Qbd.t[64:128, :, 8:16], in_=qkT.t[64:128, 0, :].rearrange("p (i q) -> p i q", q=8)), R=[qkT], W=[Qbd])
            vs = vs_r.next()
            b.op('dve', lambda: nc.vector.memset(vs.t[:, :, DH:DH + 1], 1.0), W=[vs])
            b.op('dve', lambda: nc.vector.tensor_copy(out=vs.t[:, :, 0:DH], in_=own.t[:, 2, :].rearrange("p (h e) -> p h e", e=DH)), R=[own], W=[vs])
            b.op('pe', lambda: nc.tensor.matmul(bank_src.t[:, 300:302], lhsT=C("triB"), rhs=lfo.t[:, :], start=True, stop=True), R=[cst, lfo], W=[bank_src])
            bn = bn_r.next()
            b.op('dve', lambda: nc.vector.tensor_scalar(out=bn.t[:, :], in0=bank_src.t[:, 300:302], scalar1=-1.0, scalar2=None, op0=ALU.mult), R=[bank_src], W=[bn])
            b.op('pe', lambda: nc.tensor.matmul(bank_src.t[:, 0:256], lhsT=qkT.t[:, 1, :], rhs=Qbd.t[:, :, :].rearrange("p i q -> p (i q)"), start=True, stop=True), R=[qkT, Qbd], W=[bank_src])
            tN = tN_r.next()
            tNf = tN.t[:, :, :].rearrange("p i q -> p (i q)")
            b.op('dve', lambda: nc.vector.tensor_tensor(out=tNf[:, 0:128], in0=bank_src.t[:, 0:128], in1=C("maskN0"), op=ALU.add), R=[bank_src, cst], W=[tN])
            b.op('dve', lambda: nc.vector.tensor_tensor(out=tNf[:, 128:256], in0=bank_src.t[:, 128:256], in1=C("maskN1"), op=ALU.add), R=[bank_src, cst], W=[tN])
            PN = PN_r.next()
            for hh in range(2):
                b.op('act', lambda: nc.scalar.activation(out=PN.t[:, :, hh * 8:(hh + 1) * 8], in_=tN.t[:, :, hh * 8:(hh + 1) * 8], func=AF.Exp, bias=bn.t[:, hh:hh + 1]), R=[tN, bn], W=[PN])
            return dict(Qbd=Qbd, PN=PN, vs=vs, oall=oall_r.next())

        def batch_ops(sl, sidx, i, sd):
            Qbd, PN, vs, oall = sd['Qbd'], sd['PN'], sd['vs'], sd['oall']
            bg = sidx * 16 + i
            rec, kTp, vp, lfp, totc, totB, bP, tP, PP, rc = recs[sl], kTps[sl], vps[sl], lfps[sl], totcs[sl], totBs[sl], bPs[sl], tPs[sl], PPs[sl], rcs[sl]
            bank = psf.items[sl]
            b.record_begin()
            b.dma('sp', rec.t[:, :, :], S[f"recd{bg // 32}"].t[(bg % 32) * 128:(bg % 32 + 1) * 128, :].rearrange("p (j e) -> p j e", e=258), R=[S[f"recd{bg // 32}"]], W=[rec])
            b.op('act', lambda: nc.scalar.copy(out=kTp.t[:, :, :], in_=rec.t[:, :, 0:128]), R=[rec], W=[kTp])
            b.op('dve', lambda: nc.vector.tensor_copy(out=vp.t[:, :, :, 0:DH], in_=rec.t[:, :, 128:256].rearrange("p j (h e) -> p j h e", e=DH)), R=[rec], W=[vp])
            b.op('dve', lambda: nc.vector.tensor_copy(out=lfp.t[:, :, :], in_=rec.t[:, :, 256:258]), R=[rec], W=[lfp])
            lfpf = lfp.t[:, :, :].rearrange("p j h -> p (j h)")
            b.op('pe', lambda: nc.tensor.matmul(bank.t[0:2 * NPG, OFF_T:OFF_T + 1], lhsT=lfpf, rhs=C("ones", 128, 1), start=True, stop=True), R=[lfp, cst], W=[bank])
            b.op('act', lambda: nc.scalar.copy(out=totc.t[:, :], in_=bank.t[0:2 * NPG, OFF_T:OFF_T + 1]), R=[bank], W=[totc])
            b.op('dve', lambda: nc.vector.tensor_scalar(out=totB.t[:, :], in0=C("ones", 2 * NPG, 128), scalar1=totc.t[:, 0:1], scalar2=None, op0=ALU.mult), R=[totc, cst], W=[totB])
            b.op('pe', lambda: nc.tensor.matmul(bank.t[:, OFF_B:OFF_B + 2 * NPG], lhsT=C("su"), rhs=lfpf, start=True, stop=False), R=[cst, lfp], W=[bank])
            b.op('pe', lambda: nc.tensor.matmul(bank.t[:, OFF_B:OFF_B + 2 * NPG], lhsT=totB.t[:, :], rhs=C("msuf", 2 * NPG, 2 * NPG), start=False, stop=True), R=[totB, cst], W=[bank])
            b.op('act', lambda: nc.scalar.copy(out=bP.t[:, :, :].rearrange("p j h -> p (j h)"), in_=bank.t[:, OFF_B:OFF_B + 2 * NPG]), R=[bank], W=[bP])
            for j in range(NPG):
                b.op('pe', lambda: nc.tensor.matmul(bank.t[:, j * NQ:(j + 1) * NQ], lhsT=kTp.t[:, j, :], rhs=Qbd.t[:, i, :], start=True, stop=True), R=[kTp, Qbd], W=[bank])
            psSv = bank.t[:, 0:NPG * NQ].rearrange("p (j h q) -> p j h q", h=2, q=8)
            tPv = tP.t[:, :, :].rearrange("p j (h q) -> p j h q", q=8)
            for q in range(8):
                b.op('dve', lambda: nc.vector.tensor_tensor(out=tPv[:, :, :, q], in0=psSv[:, :, :, q], in1=bP.t[:, :, :], op=ALU.add), R=[bank, bP], W=[tP])
            b.op('act', lambda: nc.scalar.activation(out=PP.t[:, :, :], in_=tP.t[:, :, :], func=AF.Exp), R=[tP], W=[PP])
            oreg = bank.t[0:NQ, OFF_O:OFF_O + 2 * (DH + 1)]
            for j in range(NPG):
                b.op('pe', lambda: nc.tensor.matmul(oreg, lhsT=PP.t[:, j, :], rhs=vp.t[:, j, :, :].rearrange("p h e -> p (h e)"), start=(j == 0), stop=False), R=[PP, vp], W=[bank])
            b.op('pe', lambda: nc.tensor.matmul(oreg, lhsT=PN.t[:, i, :], rhs=vs.t[:, :, :].rearrange("p h e -> p (h e)"), start=False, stop=True), R=[PN, vs], W=[bank])
            psOv = oreg.rearrange("p (h e) -> p h e", e=DH + 1)
            b.op('dve', lambda: nc.vector.reciprocal(out=rc.t[:, :], in_=psOv[:, :, DH]), R=[bank], W=[rc])
            for hh in range(2):
                b.op('dve', lambda: nc.vector.tensor_scalar(out=oall.t[:, i, hh, :], in0=psOv[:, hh, 0:DH], scalar1=rc.t[:, hh:hh + 1], scalar2=None, op0=ALU.mult), R=[bank, rc], W=[oall])
            return b.record_end()

        def post_source(sidx, sd):
            kk, sl_ = sidx // 4, sidx % 4
            for hh in range(2):
                dst = S[f"g2_in{kk}"].t[sl_ * 128:(sl_ + 1) * 128, hh * DH:(hh + 1) * DH].rearrange("(i q) e -> q i e", q=8)
                b.dma('sp', dst, sd['oall'].t[hh * 8:(hh + 1) * 8, :, hh, :], R=[sd['oall']], W=[S[f"g2_in{kk}"]])

        jobs = collections.deque((sidx, i) for sidx in range(8) for i in range(16))
        slots = [None] * KSL
        sdata, remaining = {}, {}
        while jobs or any(sl is not None for sl in slots):
            for k in range(KSL):
                if slots[k] is None and jobs:
                    sidx, i = jobs.popleft()
                    if sidx not in sdata:
                        sdata[sidx] = prep_source(sidx)
                        remaining[sidx] = 16
                    slots[k] = [batch_ops(k, sidx, i, sdata[sidx]), sidx]
                if slots[k] is not None:
                    ops, sidx = slots[k]
                    b.emit(ops.popleft())
                    if not ops:
                        slots[k] = None
                        remaining[sidx] -= 1
                        if remaining[sidx] == 0:
                            post_source(sidx, sdata[sidx])
        for k in range(2):
            b.cc("AllGather", ALU.bypass, QUADS, S[f"g2_in{k}"], S[f"g2_q{k}"])
            b.cc("AllGather", ALU.bypass, PAIRS, S[f"g2_q{k}"], S[f"g2_all{k}"])
        gl_r = ring(pes, 2, [128, 4, 128], F32, "gl")
        osm = b.sb(pes, [128, D], F32, "osm2")
        for cc_ in range(8):
            for k in range(2):
                gl = gl_r.next()
                b.dma('sp', gl.t[:, :, :], S[f"g2_all{k}"].t[cc_ * 512:(cc_ + 1) * 512, :].rearrange("(s p) e -> p s e", p=128), R=[S[f"g2_all{k}"]], W=[gl])
                for sl in range(4):
                    srcid = k * 4 + sl
                    if srcid == 0:
                        b.op('dve', lambda: nc.vector.tensor_scalar(out=osm.t[:, cc_ * 128:(cc_ + 1) * 128], in0=gl.t[:, sl, :], scalar1=ROW("oh", 128, 0, 1), scalar2=None, op0=ALU.mult),
                             R=[gl, rows], W=[osm])
                    else:
                        b.op('dve', lambda: nc.vector.scalar_tensor_tensor(out=osm.t[:, cc_ * 128:(cc_ + 1) * 128], in0=gl.t[:, sl, :], scalar=ROW("oh", 128, srcid, srcid + 1),
                                                                           in1=osm.t[:, cc_ * 128:(cc_ + 1) * 128], op0=ALU.mult, op1=ALU.add), R=[gl, rows, osm], W=[osm])
        b.dma('sp', S["os"].t[:, :], osm.t[:, :], R=[osm], W=[S["os"]])
        b.barrier()
        pes.close()

    def phase_D(with_sample_attn):
        pes = ExitStack()
        b.cc("ReduceScatter", ALU.add, PAIRS, S["rs2_in"], S["rs2_out"])
        Hs = [b.sb(pes, [128, D], F32, "H") for _ in range(NTOK)]
        aes = ExitStack()
        at_r = ring(aes, 2, [128, D], F32, "at")
        for t in range(NOWN):
            L = 128 if t < NOWN - 1 else LLAST
            at = at_r.next()
            if L < 128:
                b.op('pool', lambda: nc.gpsimd.memset(at.t[:, :], 0.0), W=[at])
            b.dma('sp', Hs[t].t[:, :], S["h"].t[t * 128:(t + 1) * 128, :], R=[S["h"]], W=[Hs[t]])
            b.dma('sp', at.t[0:L, :], S["rs2_out"].t[t * 128:t * 128 + L, :], R=[S["rs2_out"]], W=[at])
            b.op('dve', lambda: nc.vector.tensor_tensor(out=Hs[t].t[:, :], in0=Hs[t].t[:, :], in1=at.t[:, :], op=ALU.add), R=[Hs[t], at], W=[Hs[t]])
        b.dma('sp', Hs[NOWN].t[:, :], S["h"].t[NOWN * 128:(NOWN + 1) * 128, :], R=[S["h"]], W=[Hs[NOWN]])
        b.barrier()
        aes.close()
        if with_sample_attn:
            ses = ExitStack()
            wobf = b.sb(ses, [128, 8, D], BF16, "wobf")
            load_w(wobf, lambda k, c0, n: I["w_ob_full"].t[k * 128:(k + 1) * 128, c0:c0 + n], 8, D)
            osm = b.sb(ses, [128, D], F32, "osm")
            sgt = b.sb(ses, [128, D], F32, "sgt")
            gob = b.sb(ses, [128, D], BF16, "gob")
            goT = b.sb(ses, [128, 8, 128], BF16, "sgoT")
            ao = b.sb(ses, [128, D], F32, "sao")
            b.dma('sp', osm.t[:, :], S["os"].t[:, :], R=[S["os"]], W=[osm])
            b.dma('sp', sgt.t[:, :], S["sgs"].t[:, :], R=[S["sgs"]], W=[sgt])
            b.op('dve', lambda: nc.vector.tensor_tensor(out=gob.t[:, :], in0=osm.t[:, :], in1=sgt.t[:, :], op=ALU.mult), R=[osm, sgt], W=[gob])
            transpose_to(goT, gob, 128, 8)
            proj_tok(ao, goT, 128, wobf, 8, [(0, 512), (512, 512)])
            b.op('dve', lambda: nc.vector.tensor_tensor(out=Hs[NOWN].t[:, :], in0=Hs[NOWN].t[:, :], in1=ao.t[:, :], op=ALU.add), R=[Hs[NOWN], ao], W=[Hs[NOWN]])
            b.barrier()
            ses.close()
        fes = ExitStack()
        ffn(fes, 1, Hs, "nffn1")
        xh_r = ring(fes, 2, [128, D], BF16, "dxh")
        scr = b.sb(fes, [128, D], F32, "dscr")
        ssq_r = ring(fes, 2, [128, 2], F32, "dssq")
        y_r = ring(fes, 2, [128, D], F32, "dy")
        for t in range(NTOK):
            ssq = ssq_r.next()
            b.op('act', lambda: nc.scalar.activation(out=scr.t[:, :], in_=Hs[t].t[:, :], func=AF.Square, accum_out=ssq.t[:, 0:1]), R=[Hs[t]], W=[scr, ssq])
            b.op('act', lambda: nc.scalar.activation(out=ssq.t[:, 1:2], in_=ssq.t[:, 0:1], func=AF.Ln, scale=1.0 / D, bias=EPS), R=[ssq], W=[ssq])
            b.op('act', lambda: nc.scalar.activation(out=ssq.t[:, 1:2], in_=ssq.t[:, 1:2], func=AF.Exp, scale=-0.5), R=[ssq], W=[ssq])
            y = y_r.next()
            b.op('dve', lambda: nc.vector.scalar_tensor_tensor(out=y.t[:, :], in0=Hs[t].t[:, :], scalar=ssq.t[:, 1:2], in1=ROW("nfin"), op0=ALU.mult, op1=ALU.mult),
                 R=[Hs[t], ssq, rows], W=[y])
            if t < NOWN:
                b.dma('sp', O["y_own"].t[t * 128:(t + 1) * 128, :], y.t[:, :], R=[y], W=[O["y_own"]])
            else:
                b.dma('sp', O["y_s"].t[:, :], y.t[:, :], R=[y], W=[O["y_s"]])
        b.barrier()
        fes.close()
        pes.close()

    if "A" in cfg.phases:
        phase_A('prompt')
    if "S" in cfg.phases:
        phase_A('sample')
    if "B" in cfg.phases:
        phase_B()
    if "0" in cfg.phases:
        phase_C0()
    if "C" in cfg.phases:
        phase_C()
    if "X" in cfg.phases:
        phase_C2()
    if "D" in cfg.phases:
        phase_D("X" in cfg.phases)

    b.barrier()
    es.close()
    return nc


def prep_core_inputs(cfg, c, inp, consts):
    bq, r = c % 4, c // 4
    f = lambda a: np.ascontiguousarray(np.asarray(a, dtype=np.float32))
    m = {}
    m["consts"] = consts
    hs = slice(4 * r, 4 * r + 4)
    fs = slice(8 * r, 8 * r + 8)
    rowsv = np.concatenate([
        inp["b_ig_a"][0][hs], inp["b_fg_a"][0][hs], inp["b_ig_a"][0], inp["b_fg_a"][0],
        inp["b_fg_b"][fs], inp["b_fg_b"], inp["k_norm_b"], inp["q_norm_b"][0], inp["norm_final"], np.eye(8, dtype=np.float32)[c]]).astype(np.float32)
    m["rows"] = np.ascontiguousarray(np.broadcast_to(rowsv[None, :], (128, NROWS)))
    col = lambda v: np.asarray(v, np.float32).reshape(-1, 128).T
    m["vecs"] = np.ascontiguousarray(np.concatenate([
        col(inp["norm_a"][0]), col(inp["norm_ffn"][0]), col(inp["norm_ffn"][1]), col(inp["norm_kv"]), col(inp["norm_b"][0]),
        col(inp["mh_norm_a"][0][hs]), col(inp["mh_norm_a"][0])], axis=1))
    m["xp"] = f(np.concatenate([inp["meta_tokens"], inp["x_prompt"][bq]], axis=0))
    m["xs"] = f(inp["x_sample"][16 * c:16 * c + 16].reshape(128, D))
    sC = np.asarray(inp["state_C"][0][16 * c:16 * c + 16], np.float32)
    sn = np.asarray(inp["state_n"][0][16 * c:16 * c + 16], np.float32)
    m["sCn"] = f(np.concatenate([sC, sn[..., None]], axis=-1).transpose(2, 1, 0, 3))
    m["smt"] = f(np.repeat(np.asarray(inp["state_m"][0][16 * c:16 * c + 16], np.float32), 8, axis=0))
    w = np.asarray(inp["w_in_a"][0], np.float32)
    HK, HV = HA * DK, HA * DV
    wq, wk, wv, wo = w[:, :HK], w[:, HK:2 * HK], w[:, 2 * HK:2 * HK + HV], w[:, 2 * HK + HV:2 * HK + 2 * HV]
    wgi, wgf = w[:, 2 * HK + 2 * HV:2 * HK + 2 * HV + HA], w[:, 2 * HK + 2 * HV + HA:]
    m["w_in_own"] = f(np.concatenate([wq[:, 256 * r:256 * r + 256], wk[:, 256 * r:256 * r + 256], wv[:, 512 * r:512 * r + 512],
                                      wo[:, 512 * r:512 * r + 512], wgi[:, hs], wgf[:, hs]], axis=1))
    m["w_in_full"] = f(w)
    HALF = cfg.HALF
    ck = np.asarray(inp["cache_k"])[:, :, 2 * c:2 * c + 2, :]
    cv = np.asarray(inp["cache_v"])[:, :, 2 * c:2 * c + 2, :]
    cl = np.asarray(inp["cache_logf"])[:, :, 2 * c:2 * c + 2]
    nph = ck.shape[0]
    rec = np.empty((nph, 128, 258), np.float32)
    rec[:, :, 0:128] = ck.transpose(0, 2, 3, 1).reshape(nph, 128, 128)
    rec[:, :, 128:256] = cv.reshape(nph, 128, 128)
    rec[:, :, 256:258] = cl
    m["rec"] = rec.reshape(nph * 128, 258)
    m["pt"] = np.ascontiguousarray(np.asarray(inp["page_table"], np.int32).reshape(1, -1))
    m["xown"] = f(m["xp"][r * HALF:(r + 1) * HALF])
    m["w_gu"] = f(inp["w_gate_up"])
    m["w_d"] = f(inp["w_down"])
    wkvf = np.asarray(inp["w_kvf"], np.float32)
    HD = HB * DH
    m["w_kvf_own"] = f(np.concatenate([wkvf[:, 512 * r:512 * r + 512], wkvf[:, HD + 512 * r:HD + 512 * r + 512], wkvf[:, 2 * HD + 8 * r:2 * HD + 8 * r + 8]], axis=1))
    m["w_kvf_full"] = f(wkvf)
    wqo = np.asarray(inp["w_qo_b"][0], np.float32)
    m["w_qo_own"] = f(np.concatenate([wqo[:, 512 * r:512 * r + 512], wqo[:, HD + 512 * r:HD + 512 * r + 512]], axis=1))
    m["w_qo_full"] = f(wqo)
    wob = np.asarray(inp["w_out_b"][0], np.float32)
    m["w_ob_own"] = f(wob[512 * r:512 * r + 512])
    m["w_ob_full"] = f(wob)
    woa = np.asarray(inp["w_out_a"][0], np.float32)
    m["w_oa_own"] = f(woa[512 * r:512 * r + 512])
    m["w_oa_full"] = f(woa)
    return m


def run(cfg, inp):
    NPG_GLOBAL[0] = cfg.NPG
    consts = make_consts()
    nc = build(cfg)
    in_maps = [prep_core_inputs(cfg, c, inp, consts) for c in range(NCORE)]
    res = run_bass_kernel_spmd(nc, in_maps, core_ids=list(range(NCORE)))
    return res.results


def kernel(**inputs):
    inp = {k: np.asarray(v) for k, v in inputs.items()}
    SEQ = inp["x_prompt"].shape[1]
    NPG = inp["page_table"].shape[1]
    cfg = Cfg(SEQ=SEQ, PAST=NPG * 128, NPHYS=inp["cache_k"].shape[0], phases="ASB0CXD")
    res = run(cfg, inp)
    TP, H = cfg.TP, cfg.HALF
    f32 = np.float32
    y_prompt = np.stack([np.concatenate([res[bq]["y_own"][:H], res[bq + 4]["y_own"][:H]], axis=0)[NMETA:] for bq in range(4)]).astype(f32)
    y_sample = np.concatenate([res[c]["y_s"] for c in range(8)], axis=0).reshape(128, 8, D).astype(f32)
    p_C = np.zeros((1, 4, HA, DK, DV), f32); p_n = np.zeros((1, 4, HA, DK), f32); p_m = np.zeros((1, 4, HA), f32)
    p_k = np.zeros((4, TP, HB, DH), f32); p_v = np.zeros((4, TP, HB, DH), f32); p_lf = np.zeros((4, TP, HB), f32)
    for c in range(8):
        bq, r = c % 4, c // 4
        p_C[0, bq, 4 * r:4 * r + 4] = res[c]["pC"][:, :, :DV]
        p_n[0, bq, 4 * r:4 * r + 4] = res[c]["pC"][:, :, DV]
        p_m[0, bq, 4 * r:4 * r + 4] = res[c]["pm"][0]
        p_k[bq, :, 8 * r:8 * r + 8] = res[c]["pk"].reshape(TP, 8, DH)
        p_v[bq, :, 8 * r:8 * r + 8] = res[c]["pv"].reshape(TP, 8, DH)
        p_lf[bq, :, 8 * r:8 * r + 8] = res[c]["plf"]
    sC = np.concatenate([res[c]["sCo"] for c in range(8)], axis=0)
    s_C = np.ascontiguousarray(sC[..., :DV])[None].astype(f32)
    s_n = np.ascontiguousarray(sC[..., DV])[None].astype(f32)
    s_m = np.concatenate([res[c]["smo"] for c in range(8)], axis=0)[None].astype(f32)
    s_k = np.concatenate([res[c]["sk"] for c in range(8)], axis=0).reshape(128, 8, HB, DH).astype(f32)
    s_v = np.concatenate([res[c]["sv"] for c in range(8)], axis=0).reshape(128, 8, HB, DH).astype(f32)
    s_lf = np.concatenate([res[c]["slf"] for c in range(8)], axis=0).reshape(128, 8, HB).astype(f32)
    return (y_prompt, y_sample, p_C, p_n, p_m, p_k, p_v, p_lf, s_C, s_n, s_m, s_k, s_v, s_lf)
```

```python
import types
import collections
import numpy as np
import concourse.bass as bass
import concourse.mybir as mybir
from concourse.bass_utils import run_bass_kernel_spmd
from contextlib import ExitStack

F32 = mybir.dt.float32
BF16 = mybir.dt.bfloat16
I32 = mybir.dt.int32
AF = mybir.ActivationFunctionType
ALU = mybir.AluOpType
AX = mybir.AxisListType

D = 1024
NMETA = 16
HA, DK, DV = 8, 64, 128
HB, DH = 16, 64
DFF = 2816
EPS = 1e-6
CAP = 15.0
NEG = -30000.0
NCORE = 8
PAIRS = [[0, 4], [1, 5], [2, 6], [3, 7]]
QUADS = [[0, 1, 2, 3], [4, 5, 6, 7]]


class Cfg:
    def __init__(self, SEQ=4096, PAST=2048, NPHYS=2560, PAGE=128, phases="ABCD"):
        self.SEQ, self.PAST, self.NPHYS, self.PAGE = SEQ, PAST, NPHYS, PAGE
        self.TP = SEQ + NMETA
        self.NT = SEQ // 128 + 1
        self.HALF = self.TP // 2
        self.NOWN = -(-self.HALF // 128)
        self.NPG = PAST // PAGE
        self.phases = phases

    def trange(self, i):
        if i == 0:
            return 0, NMETA
        return NMETA + 128 * (i - 1), NMETA + 128 * i


class TT:
    def __init__(self, t):
        self.t = t
        self.w = None
        self.r = {}


NDS = 80


class B:
    GEN = 16000

    def __init__(self, nc, es):
        self.nc = nc
        self.es = es
        self.E = {'pe': nc.tensor, 'dve': nc.vector, 'act': nc.scalar, 'pool': nc.gpsimd, 'sp': nc.sync}
        self.cnt = {k: 0 for k in ('pe', 'dve', 'act', 'pool')}
        self.esem = {k: [es.enter_context(nc.semaphore(f"s_{k}{g}")) for g in range(4)] for k in self.cnt}
        self.dsem = [es.enter_context(nc.semaphore(f"s_d{i}")) for i in range(NDS)]
        self.dval = [0] * NDS
        self.dnext = 0
        self.csem = es.enter_context(nc.semaphore("s_cc"))
        self.cval = 0
        self.seen = {k: {} for k in self.E}
        self.uid = 0
        self.rec = None

    def wait(self, e, ev):
        if ev is None:
            return
        kind, key, val = ev
        if kind == 'eng' and key == e and e == 'pe':
            return
        if self.seen[e].get((kind, key), 0) >= val:
            return
        self.seen[e][(kind, key)] = val
        if kind == 'eng':
            g = (val - 1) // self.GEN
            self.E[e].wait_ge(self.esem[key][g], val - g * self.GEN)
        elif kind == 'dma':
            self.E[e].wait_ge(self.dsem[key], val)
        else:
            self.E[e].wait_ge(self.csem, val)

    def _deps(self, e, R, W):
        for t in R:
            self.wait(e, t.w)
        for t in W:
            self.wait(e, t.w)
            for ev in list(t.r.values()):
                self.wait(e, ev)

    def _commit(self, ev, R, W):
        for t in R:
            t.r[(ev[0], ev[1])] = ev
        for t in W:
            t.w = ev
            t.r = {}

    @staticmethod
    def _freeze(fn):
        if fn.__closure__ is None:
            return fn
        cells = []
        for c in fn.__closure__:
            try:
                cells.append(types.CellType(c.cell_contents))
            except ValueError:
                cells.append(c)
        return types.FunctionType(fn.__code__, fn.__globals__, fn.__name__, fn.__defaults__, tuple(cells))

    def record_begin(self):
        self.rec = []

    def record_end(self):
        r, self.rec = self.rec, None
        return collections.deque(r)

    def emit(self, item):
        kind = item[0]
        if kind == 'op':
            self.op(*item[1:])
        else:
            self.dma(*item[1:-1], indirect=item[-1])

    def op(self, e, fn, R=(), W=()):
        if self.rec is not None:
            self.rec.append(('op', e, self._freeze(fn), list(R), list(W)))
            return
        self._deps(e, R, W)
        ins = fn()
        self.cnt[e] += 1
        n = self.cnt[e]
        g = (n - 1) // self.GEN
        ins.then_inc(self.esem[e][g], 1)
        self._commit(('eng', e, n), R, W)

    def dma(self, q, out, in_, R=(), W=(), indirect=None):
        if self.rec is not None:
            self.rec.append(('dma', q, out, in_, list(R), list(W), indirect))
            return
        self._deps(q, R, W)
        i = self.dnext
        self.dnext = (i + 1) % NDS
        if self.dval[i] > 0:
            self.wait(q, ('dma', i, self.dval[i]))
        if indirect is None:
            ins = self.E[q].dma_start(out=out, in_=in_)
        else:
            ins = self.E[q].indirect_dma_start(out=out, out_offset=None, in_=in_, in_offset=indirect)
        self.dval[i] += 16
        ins.then_inc(self.dsem[i], 16)
        self._commit(('dma', i, self.dval[i]), R, W)

    def cc(self, kind, op, groups, in_t, out_t):
        self._deps('pool', [in_t], [out_t])
        ins = self.nc.gpsimd.collective_compute(kind, op, replica_groups=groups,
                                                ins=[in_t.t.ap().opt()], outs=[out_t.t.ap().opt()])
        self.cval += 1
        ins.then_inc(self.csem)
        self._commit(('cc', 0, self.cval), [in_t], [out_t])

    def barrier(self, final=False):
        evs = [('eng', k, self.cnt[k]) for k in self.cnt if self.cnt[k] > 0]
        evs += [('dma', i, self.dval[i]) for i in range(NDS) if self.dval[i] > 0]
        if self.cval and final:
            evs.append(('cc', 0, self.cval))
        for e in self.E:
            for ev in evs:
                self.wait(e, ev)

    def sb(self, es, shape, dt, name=None):
        self.uid += 1
        return TT(es.enter_context(self.nc.sbuf_tensor(f"{name or 't'}_{self.uid}", list(shape), dt)))

    def dram(self, shape, dt, name):
        return TT(self.nc.dram_tensor(name, list(shape), dt))


class Ring:
    def __init__(self, items):
        self.items = items
        self.i = 0

    def next(self):
        t = self.items[self.i]
        self.i = (self.i + 1) % len(self.items)
        return t


CONST_NAMES = ["ident", "tri", "ones", "maskc", "triB", "onesB", "maskB", "maskB2", "maskT", "su", "maskN0", "maskN1", "msuf"]


NPG_GLOBAL = [16]


def make_consts():
    i = np.arange(128)
    s, t = i[:, None], i[None, :]
    same = (s // 8) == (t // 8)
    c = {}
    c["ident"] = (s == t).astype(np.float32)
    c["tri"] = (s <= t).astype(np.float32)
    c["ones"] = np.ones((128, 128), np.float32)
    c["maskc"] = np.where(t <= s, 0.0, NEG).astype(np.float32)
    c["triB"] = (same & (s <= t)).astype(np.float32)
    c["onesB"] = same.astype(np.float32)
    c["maskB"] = np.where(same & (t <= s), 0.0, NEG).astype(np.float32)
    c["maskB2"] = np.where(same, 0.0, NEG).astype(np.float32)
    c["maskT"] = np.where(s <= t, 0.0, NEG).astype(np.float32)
    c["su"] = (s > t).astype(np.float32)
    key = i[:, None]
    col = np.arange(256)[None, :]
    ii, hq = col // 16, col % 16
    mN = np.where((key // 8 == ii) & (key % 8 <= hq % 8), 0.0, NEG).astype(np.float32)
    c["maskN0"], c["maskN1"] = mN[:, :128], mN[:, 128:]
    ms = np.zeros((128, 128), np.float32)
    for npg in [NPG_GLOBAL[0]]:
        jj, hh = np.arange(2 * npg) // 2, np.arange(2 * npg) % 2
        ms[:2 * npg, :2 * npg] = ((hh[:, None] == hh[None, :]) & (jj[:, None] > jj[None, :])).astype(np.float32)
    c["msuf"] = ms
    arr = np.concatenate([c[k] for k in CONST_NAMES], axis=1)
    bm = (i[:, None] // 8 == np.arange(16)[None, :]).astype(np.float32)
    sel = (i[:, None] == 8 * np.arange(16)[None, :] + 7).astype(np.float32)
    return np.ascontiguousarray(np.concatenate([arr, bm, sel], axis=1))


NCONST = 128 * len(CONST_NAMES) + 32

ROWS = {}
_o = 0
for _n, _w in [("big_own", 4), ("bfg_own", 4), ("big_full", 8), ("bfg_full", 8), ("bfb_own", 8), ("bfb_full", 16),
               ("kg", 64), ("qg", 64), ("nfin", 1024), ("oh", 8)]:
    ROWS[_n] = (_o, _w)
    _o += _w
NROWS = _o
VECS = {}
_o = 0
for _n, _w in [("norm_a", 8), ("nffn0", 8), ("nffn1", 8), ("norm_kv", 8), ("norm_b", 8), ("mh_own", 4), ("mh_full", 8)]:
    VECS[_n] = (_o, _w)
    _o += _w
NVECS = _o

W_IN_OWN = 4 * (DK + DK + DV + DV) + 8
W_IN_FULL = 2 * HA * DK + 2 * HA * DV + 2 * HA


def build(cfg):
    nc = bass.Bass("TRN2", target_bir_lowering=False)
    es = ExitStack()
    b = B(nc, es)

    def din(name, shape, dt=F32):
        return TT(nc.dram_tensor(name, list(shape), dt, kind="ExternalInput"))

    def dout(name, shape, dt=F32):
        return TT(nc.dram_tensor(name, list(shape), dt, kind="ExternalOutput"))

    TP, NT, HALF, NOWN = cfg.TP, cfg.NT, cfg.HALF, cfg.NOWN
    I = {}
    I["consts"] = din("consts", [128, NCONST])
    I["rows"] = din("rows", [128, NROWS])
    I["vecs"] = din("vecs", [128, NVECS])
    I["xp"] = din("xp", [TP, D])
    I["xs"] = din("xs", [128, D])
    I["sCn"] = din("sCn", [DK, HA, 16, DV + 1])
    I["smt"] = din("smt", [128, HA])
    I["w_in_own"] = din("w_in_own", [D, W_IN_OWN])
    I["w_in_full"] = din("w_in_full", [D, W_IN_FULL])
    I["w_oa_own"] = din("w_oa_own", [4 * DV, D])
    I["w_oa_full"] = din("w_oa_full", [HA * DV, D])
    NPG = cfg.NPG
    I["rec"] = din("rec", [cfg.NPHYS * 128, 258])
    I["pt"] = din("pt", [1, 128 * NPG], I32)
    I["xown"] = din("xown", [HALF, D])
    I["w_gu"] = din("w_gu", [2, D, 2 * DFF])
    I["w_d"] = din("w_d", [2, DFF, D])
    I["w_kvf_own"] = din("w_kvf_own", [D, 1032])
    I["w_kvf_full"] = din("w_kvf_full", [D, 2064])
    I["w_qo_own"] = din("w_qo_own", [D, 1024])
    I["w_qo_full"] = din("w_qo_full", [D, 2048])
    I["w_ob_own"] = din("w_ob_own", [512, D])
    I["w_ob_full"] = din("w_ob_full", [D, D])
    R_ = NOWN * 128
    S = {}
    S["rs1_in"] = b.dram([TP, D], F32, "rs1_in")
    S["rs1_out"] = b.dram([HALF, D], F32, "rs1_out")
    S["rs2_in"] = b.dram([TP, D], F32, "rs2_in")
    S["rs2_out"] = b.dram([HALF, D], F32, "rs2_out")
    S["hs"] = b.dram([128, D], F32, "hs_dram")
    AGC = [(t0, min(4, NOWN - t0)) for t0 in range(0, NOWN, 4)]
    for k, (t0, nt) in enumerate(AGC):
        S[f"ag_in{k}"] = b.dram([nt * 128, D], BF16, f"ag_in{k}")
        S[f"ag_out{k}"] = b.dram([2 * nt * 128, D], BF16, f"ag_out{k}")
    S["xs2"] = b.dram([128, D], BF16, "xs2_dram")
    S["h"] = b.dram([(NOWN + 1) * 128, D], F32, "h_dram")
    G1C = [(c0, min(512, 3088 - c0)) for c0 in range(0, 3088, 512)]
    for k, (c0, w) in enumerate(G1C):
        S[f"g1_in{k}"] = b.dram([128, w], F32, f"g1_in{k}")
        S[f"g1_q{k}"] = b.dram([4 * 128, w], F32, f"g1_q{k}")
        S[f"g1_all{k}"] = b.dram([8 * 128, w], F32, f"g1_all{k}")
    S["sgs"] = b.dram([128, D], F32, "sgs_dram")
    for k in range(2):
        S[f"g2_in{k}"] = b.dram([4 * 128, 128], F32, f"g2_in{k}")
        S[f"g2_q{k}"] = b.dram([4 * 4 * 128, 128], F32, f"g2_q{k}")
        S[f"g2_all{k}"] = b.dram([8 * 4 * 128, 128], F32, f"g2_all{k}")
    S["os"] = b.dram([128, D], F32, "os_dram")
    O = {}
    O["y_own"] = dout("y_own", [R_, D])
    O["y_s"] = dout("y_s", [128, D])
    O["pk"] = dout("pk", [TP, 512])
    O["pv"] = dout("pv", [TP, 512])
    O["plf"] = dout("plf", [TP, 8])
    O["sk"] = dout("sk", [128, D])
    O["sv"] = dout("sv", [128, D])
    O["slf"] = dout("slf", [128, 16])
    O["pC"] = dout("pC", [4, DK, DV + 1])
    O["pm"] = dout("pm", [1, 4])
    O["sCo"] = dout("sCo", [16, HA, DK, DV + 1])
    O["smo"] = dout("smo", [16, HA])

    cst = b.sb(es, [128, NCONST], F32, "cst")
    rows = b.sb(es, [128, NROWS], F32, "rows")
    vecs = b.sb(es, [128, NVECS], F32, "vecs")
    identb = b.sb(es, [128, 128], BF16, "identb")
    b.dma('sp', cst.t[:], I["consts"].t[:, :], R=[I["consts"]], W=[cst])
    b.dma('sp', rows.t[:], I["rows"].t[:, :], R=[I["rows"]], W=[rows])
    b.dma('sp', vecs.t[:], I["vecs"].t[:, :], R=[I["vecs"]], W=[vecs])
    b.op('dve', lambda: nc.vector.tensor_copy(out=identb.t[:], in_=cst.t[:, 0:128]), R=[cst], W=[identb])

    def C(name, r=128, c=128):
        o = CONST_NAMES.index(name) * 128
        return cst.t[0:r, o:o + c]

    BM = lambda: cst.t[:, 128 * len(CONST_NAMES):128 * len(CONST_NAMES) + 16]
    SEL = lambda: cst.t[:, 128 * len(CONST_NAMES) + 16:128 * len(CONST_NAMES) + 32]

    def ROW(name, r=128, lo=0, hi=None):
        o, w = ROWS[name]
        hi = w if hi is None else hi
        return rows.t[0:r, o + lo:o + hi]

    def VEC(name, j):
        o, w = VECS[name]
        return vecs.t[:, o + j:o + j + 1]

    psf = Ring([TT(es.enter_context(nc.psum_tensor(f"psf{i}", [128, 512], F32))) for i in range(4)])
    pso = Ring([TT(es.enter_context(nc.psum_tensor(f"pso{i}", [128, 512], F32))) for i in range(2)])
    psb = Ring([TT(es.enter_context(nc.psum_tensor(f"psb{i}", [128, 1024], BF16))) for i in range(2)])

    wstage = Ring([b.sb(es, [128, 1024], F32, "wst") for _ in range(3)])

    wcnt = [0]

    def load_w(dst, src_ap_fn, KC, N, gain=None, q='sp'):
        for k in range(KC):
            for c0 in range(0, N, 1024):
                n = min(1024, N - c0)
                st = wstage.next()
                b.dma(q, st.t[:, 0:n], src_ap_fn(k, c0, n), R=[], W=[st])
                wcnt[0] += 1
                if wcnt[0] % 2 == 0:
                    if gain is None:
                        b.op('act', lambda: nc.scalar.copy(out=dst.t[:, k, c0:c0 + n], in_=st.t[:, 0:n]), R=[st], W=[dst])
                    else:
                        b.op('act', lambda: nc.scalar.activation(out=dst.t[:, k, c0:c0 + n], in_=st.t[:, 0:n], func=AF.Copy, scale=gain(k)), R=[st, vecs], W=[dst])
                else:
                    if gain is None:
                        b.op('dve', lambda: nc.vector.tensor_copy(out=dst.t[:, k, c0:c0 + n], in_=st.t[:, 0:n]), R=[st], W=[dst])
                    else:
                        b.op('dve', lambda: nc.vector.tensor_scalar(out=dst.t[:, k, c0:c0 + n], in0=st.t[:, 0:n], scalar1=gain(k), scalar2=None, op0=ALU.mult), R=[st, vecs], W=[dst])

    def ring(es_, n, shape, dt, name):
        return Ring([b.sb(es_, shape, dt, name) for _ in range(n)])

    def rms_normalize(src_ap, L, dst_bf, scr, ssq, tiles_R, nfeat=D):
        b.op('act', lambda: nc.scalar.activation(out=scr.t[0:L, 0:nfeat], in_=src_ap, func=AF.Square, accum_out=ssq.t[0:L, 0:1]),
             R=tiles_R, W=[scr, ssq])
        b.op('act', lambda: nc.scalar.activation(out=ssq.t[0:L, 1:2], in_=ssq.t[0:L, 0:1], func=AF.Ln, scale=1.0 / nfeat, bias=EPS),
             R=[ssq], W=[ssq])
        b.op('act', lambda: nc.scalar.activation(out=ssq.t[0:L, 1:2], in_=ssq.t[0:L, 1:2], func=AF.Exp, scale=-0.5), R=[ssq], W=[ssq])
        b.op('dve', lambda: nc.vector.tensor_scalar(out=dst_bf.t[0:L, 0:nfeat], in0=src_ap, scalar1=ssq.t[0:L, 1:2], scalar2=None, op0=ALU.mult),
             R=tiles_R + [ssq], W=[dst_bf])

    def transpose_to(dstT, src_bf, L, nchunk, evac='act', cw=128, c0=0, d0=0):
        ps = psb.next()
        for k in range(nchunk):
            b.op('pe', lambda: nc.tensor.transpose(ps.t[0:cw, k * 128:k * 128 + L], src_bf.t[0:L, c0 + k * cw:c0 + (k + 1) * cw], identb.t[0:L, 0:L]),
                 R=[src_bf, identb], W=[ps])
        view = ps.t[0:cw, 0:nchunk * 128].rearrange("p (k l) -> p k l", l=128)[:, :, 0:L]
        if evac == 'act':
            b.op('act', lambda: nc.scalar.copy(out=dstT.t[0:cw, d0:d0 + nchunk, 0:L], in_=view), R=[ps], W=[dstT])
        else:
            b.op('dve', lambda: nc.vector.tensor_copy(out=dstT.t[0:cw, d0:d0 + nchunk, 0:L], in_=view), R=[ps], W=[dstT])

    def proj_tok(dst, xT, L, W, KC, blocks, evac_engs=('act', 'dve')):
        for bi, (c0, n) in enumerate(blocks):
            ps = psf.next()
            for k in range(KC):
                b.op('pe', lambda: nc.tensor.matmul(ps.t[0:L, 0:n], lhsT=xT.t[:, k, 0:L], rhs=W.t[:, k, c0:c0 + n], start=(k == 0), stop=(k == KC - 1)),
                     R=[xT, W], W=[ps])
            e = evac_engs[bi % len(evac_engs)]
            if e == 'act':
                b.op('act', lambda: nc.scalar.copy(out=dst.t[0:L, c0:c0 + n], in_=ps.t[0:L, 0:n]), R=[ps], W=[dst])
            else:
                b.op('dve', lambda: nc.vector.tensor_copy(out=dst.t[0:L, c0:c0 + n], in_=ps.t[0:L, 0:n]), R=[ps], W=[dst])

    def phase_A(mode):
        pes = ExitStack()
        NH = 4 if mode == 'prompt' else HA
        NB = 2 if mode == 'prompt' else 1
        if mode == 'prompt':
            w_in = b.sb(pes, [128, 8, W_IN_OWN], BF16, "w_in")
            w_oa = b.sb(pes, [128, 4, D], BF16, "w_oa")
            load_w(w_in, lambda k, c0, n: I["w_in_own"].t[k * 128:(k + 1) * 128, c0:c0 + n], 8, W_IN_OWN, gain=lambda k: VEC("norm_a", k))
            load_w(w_oa, lambda k, c0, n: I["w_oa_own"].t[k * 128:(k + 1) * 128, c0:c0 + n], 4, D, gain=lambda k: VEC("mh_own", k))
        else:
            w_in = b.sb(pes, [128, 8, W_IN_FULL], BF16, "w_inf")
            w_oa = b.sb(pes, [128, 8, D], BF16, "w_oaf")
            load_w(w_in, lambda k, c0, n: I["w_in_full"].t[k * 128:(k + 1) * 128, c0:c0 + n], 8, W_IN_FULL, gain=lambda k: VEC("norm_a", k))
            load_w(w_oa, lambda k, c0, n: I["w_oa_full"].t[k * 128:(k + 1) * 128, c0:c0 + n], 8, D, gain=lambda k: VEC("mh_full", k))
        xt_r = ring(pes, NB, [128, D], F32, "xt")
        xh_r = ring(pes, NB, [128, D], BF16, "xh")
        scr = b.sb(pes, [128, D], F32, "scr")
        ssq_r = ring(pes, 2, [128, 2], F32, "ssq")
        xT_r = ring(pes, NB, [128, 8, 128], BF16, "xT")
        p_r = ring(pes, NB, [128, W_IN_FULL if mode != "prompt" else W_IN_OWN], F32, "p")
        pb_r = ring(pes, NB, [128, 2 * NH * DK + NH * (DV + 1)], BF16, "pb")
        qkT_r = ring(pes, NB, [64, 2 * NH, 128], BF16, "qkT")
        g_r = ring(pes, NB, [128, 48], F32, "gates")
        gated_r = ring(pes, NB, [128, NH * DV], BF16, "gated")
        gT_r = ring(pes, NB, [128, 8, 128], BF16, "gT")
        aout_r = ring(pes, NB, [128, D], F32, "aout")
        uB_r = ring(pes, 4, [128, 128], F32, "uB")
        dm_r = ring(pes, 4, [128, 128], F32, "dm")
        de_r = ring(pes, 4, [128, 128], F32, "de")
        sm_r = ring(pes, 4, [128, 128], BF16, "smm")
        smT_r = ring(pes, 4, [128, 128], BF16, "smT")
        kw_r = ring(pes, NH + 1, [128, DK], BF16, "kw")
        sc_r = ring(pes, (2 * NH + 2) if mode == "prompt" else 4, [128, 16], F32, "sc")
        svs_r = ring(pes, NH + 1, [128, DV + 1], F32, "svs")
        num_r = ring(pes, 4, [128, DV + 1], F32, "num")
        hr_r = ring(pes, 4, [128, DV], F32, "hr")
        sg_r = ring(pes, NB, [128, NH * DV], F32, "sg")
        ut_r = ring(pes, 4, [DK, DV + 1], F32, "ut")
        if mode == 'prompt':
            Cst = [b.sb(pes, [DK, DV + 1], F32, "Cst") for _ in range(NH)]
            Cbf = [b.sb(pes, [DK, DV + 1], BF16, "Cbf") for _ in range(NH)]
            mrep = [b.sb(pes, [128, 1], F32, "mrep") for _ in range(NH)]
            for h in range(NH):
                b.op('pool', lambda: nc.gpsimd.memset(Cst[h].t[:], 0.0), W=[Cst[h]])
                b.op('pool', lambda: nc.gpsimd.memset(Cbf[h].t[:], 0.0), W=[Cbf[h]])
                b.op('pool', lambda: nc.gpsimd.memset(mrep[h].t[:], 0.0), W=[mrep[h]])
        else:
            CA = b.sb(pes, [DK, 4, 16, DV + 1], F32, "CA")
            CAb_r = ring(pes, 2, [DK, 16, DV + 1], BF16, "CAb")
            Vb_r = ring(pes, 2, [128, 16, DV + 1], BF16, "Vb")
            mtok = b.sb(pes, [128, HA], F32, "mtok")
            mnew = b.sb(pes, [128, HA], F32, "mnew")
            acc_r = ring(pes, 2, [128, DV + 1], F32, "acc")
            cB_r = ring(pes, 2, [128, 2, DK], F32, "cB")
            wrow_r = ring(pes, 2, [DK, 32], F32, "wrow")
            b.dma('sp', mtok.t[:], I["smt"].t[:, :], W=[mtok])

        def gates(p, L, nh, col_gi, col_gf, big, bfg, tri, ones_m):
            g = g_r.next()
            li, lf = g.t[0:L, 0:nh], g.t[0:L, nh:2 * nh]
            b.op('dve', lambda: nc.vector.tensor_tensor(out=li, in0=p.t[0:L, col_gi:col_gi + nh], in1=big, op=ALU.add), R=[p, rows], W=[g])
            b.op('dve', lambda: nc.vector.tensor_tensor(out=lf, in0=p.t[0:L, col_gf:col_gf + nh], in1=bfg, op=ALU.add), R=[p, rows], W=[g])
            b.op('act', lambda: nc.scalar.activation(out=g.t[0:L, 0:2 * nh], in_=g.t[0:L, 0:2 * nh], func=AF.Tanh, scale=1.0 / CAP), R=[g], W=[g])
            b.op('dve', lambda: nc.vector.tensor_scalar(out=li, in0=li, scalar1=CAP, scalar2=None, op0=ALU.mult), R=[g], W=[g])
            b.op('act', lambda: nc.scalar.activation(out=lf, in_=lf, func=AF.Exp, scale=-CAP), R=[g], W=[g])
            b.op('act', lambda: nc.scalar.activation(out=lf, in_=lf, func=AF.Ln, bias=1.0), R=[g], W=[g])
            b.op('dve', lambda: nc.vector.tensor_scalar(out=lf, in0=lf, scalar1=-1.0, scalar2=None, op0=ALU.mult), R=[g], W=[g])
            ps = psf.next()
            b.op('pe', lambda: nc.tensor.matmul(ps.t[0:L, 0:nh], lhsT=tri, rhs=lf, start=True, stop=True), R=[cst, g], W=[ps])
            b.op('pe', lambda: nc.tensor.matmul(ps.t[0:128, 8:8 + nh], lhsT=ones_m, rhs=lf, start=True, stop=True), R=[cst, g], W=[ps])
            b.op('dve', lambda: nc.vector.tensor_copy(out=g.t[0:L, 2 * nh:3 * nh], in_=ps.t[0:L, 0:nh]), R=[ps], W=[g])
            b.op('dve', lambda: nc.vector.tensor_copy(out=g.t[0:128, 3 * nh:4 * nh], in_=ps.t[0:128, 8:8 + nh]), R=[ps], W=[g])
            return g

        def qk_prep(p, L, nh, col_q, col_k, col_v):
            pb = pb_r.next()
            nq = nh * DK
            b.op('act', lambda: nc.scalar.activation(out=pb.t[0:L, 0:nq], in_=p.t[0:L, col_q:col_q + nq], func=AF.Copy, scale=DK ** -0.5), R=[p], W=[pb])
            b.op('dve', lambda: nc.vector.tensor_copy(out=pb.t[0:L, nq:2 * nq], in_=p.t[0:L, col_k:col_k + nq]), R=[p], W=[pb])
            vv = pb.t[0:L, 2 * nq:2 * nq + nh * (DV + 1)].rearrange("p (h e) -> p h e", e=DV + 1)
            b.op('pool', lambda: nc.gpsimd.tensor_copy(out=vv[:, :, 0:DV], in_=p.t[0:L, col_v:col_v + nh * DV].rearrange("p (h e) -> p h e", e=DV)), R=[p], W=[pb])
            b.op('pool', lambda: nc.gpsimd.memset(vv[:, :, DV:DV + 1], 1.0), W=[pb])
            qkT = qkT_r.next()
            for j0 in range(0, 2 * nh, 8):
                transpose_to(qkT, pb, L, min(8, 2 * nh - j0), cw=64, c0=j0 * 64, d0=j0, evac=('act' if j0 == 0 else 'dve'))
            return pb, qkT

        POFF = {'R': 0, 'G': 128, 'SV': 256, 'Q': 0, 'U': 256}

        def hps(bank, name):
            if bank is None:
                return psf.next(), 0
            return bank, POFF[name]

        def head_common(g, L, nh, h, qkT, pb, maskc, mask2, bank=None):
            nq = nh * DK
            sc = sc_r.next()
            bcol = g.t[0:L, 2 * nh + h:2 * nh + h + 1]
            b.op('dve', lambda: nc.vector.tensor_tensor(out=g.t[0:L, 4 * nh + h:4 * nh + h + 1], in0=g.t[0:L, h:h + 1], in1=bcol, op=ALU.subtract), R=[g], W=[g])
            ucol = g.t[0:L, 4 * nh + h:4 * nh + h + 1]
            uB = uB_r.next()
            b.op('dve', lambda: nc.vector.tensor_scalar(out=uB.t[0:L, :], in0=C("ones", L, 128), scalar1=ucol, scalar2=None, op0=ALU.mult), R=[g, cst], W=[uB])
            psR, oR = hps(bank, 'R')
            b.op('pe', lambda: nc.tensor.matmul(psR.t[0:128, oR:oR + L], lhsT=uB.t[0:L, :], rhs=C("ident", L, L), start=True, stop=True), R=[uB, cst], W=[psR])
            dm = dm_r.next()
            b.op('dve', lambda: nc.vector.scalar_tensor_tensor(out=dm.t[0:L, 0:L], in0=psR.t[0:L, oR:oR + L], scalar=bcol, in1=maskc, op0=ALU.add, op1=ALU.add),
                 R=[psR, g, cst], W=[dm])
            b.op('dve', lambda: nc.vector.tensor_reduce(out=sc.t[0:L, 0:1], in_=dm.t[0:L, 0:L], axis=AX.X, op=ALU.max), R=[dm], W=[sc])
            if mask2 is None:
                b.op('dve', lambda: nc.vector.tensor_reduce(out=sc.t[0:128, 2:3], in_=psR.t[0:128, oR:oR + L], axis=AX.X, op=ALU.max), R=[psR], W=[sc])
            else:
                de0 = de_r.next()
                b.op('dve', lambda: nc.vector.tensor_tensor(out=de0.t[0:128, 0:L], in0=psR.t[0:128, oR:oR + L], in1=mask2, op=ALU.add), R=[psR, cst], W=[de0])
                b.op('dve', lambda: nc.vector.tensor_reduce(out=sc.t[0:128, 2:3], in_=de0.t[0:128, 0:L], axis=AX.X, op=ALU.max), R=[de0], W=[sc])
            b.op('dve', lambda: nc.vector.tensor_scalar(out=sc.t[0:L, 1:2], in0=sc.t[0:L, 0:1], scalar1=-1.0, scalar2=None, op0=ALU.mult), R=[sc], W=[sc])
            de = de_r.next()
            b.op('act', lambda: nc.scalar.activation(out=de.t[0:L, 0:L], in_=dm.t[0:L, 0:L], func=AF.Exp, bias=sc.t[0:L, 1:2]), R=[dm, sc], W=[de])
            psG, oG = hps(bank, 'G')
            b.op('pe', lambda: nc.tensor.matmul(psG.t[0:L, oG:oG + L], lhsT=qkT.t[0:64, h, 0:L], rhs=qkT.t[0:64, nh + h, 0:L], start=True, stop=True),
                 R=[qkT], W=[psG])
            smm = sm_r.next()
            b.op('dve', lambda: nc.vector.tensor_tensor(out=smm.t[0:L, 0:L], in0=psG.t[0:L, oG:oG + L], in1=de.t[0:L, 0:L], op=ALU.mult), R=[psG, de], W=[smm])
            if bank is None:
                psT, oT = psb.next(), 0
            else:
                psT, oT = psb.items[0], h * 128
            b.op('pe', lambda: nc.tensor.transpose(psT.t[0:L, oT:oT + L], smm.t[0:L, 0:L], identb.t[0:L, 0:L]), R=[smm, identb], W=[psT])
            smT = smT_r.next()
            b.op('act', lambda: nc.scalar.copy(out=smT.t[0:L, 0:L], in_=psT.t[0:L, oT:oT + L]), R=[psT], W=[smT])
            b.op('dve', lambda: nc.vector.tensor_scalar(out=sc.t[0:L, 11:12], in0=sc.t[0:L, 2:3], scalar1=-1.0, scalar2=None, op0=ALU.mult), R=[sc], W=[sc])
            b.op('act', lambda: nc.scalar.activation(out=sc.t[0:L, 9:10], in_=ucol, func=AF.Exp, bias=sc.t[0:L, 11:12]), R=[g, sc], W=[sc])
            kw = kw_r.next()
            b.op('dve', lambda: nc.vector.tensor_scalar(out=kw.t[0:L, :], in0=pb.t[0:L, nq + h * DK:nq + (h + 1) * DK], scalar1=sc.t[0:L, 9:10], scalar2=None, op0=ALU.mult),
                 R=[pb, sc], W=[kw])
            vp = pb.t[0:L, 2 * nq + h * (DV + 1):2 * nq + (h + 1) * (DV + 1)]
            psSV, oS = hps(bank, 'SV')
            b.op('pe', lambda: nc.tensor.matmul(psSV.t[0:L, oS:oS + DV + 1], lhsT=smT.t[0:L, 0:L], rhs=vp, start=True, stop=True), R=[smT, pb], W=[psSV])
            svs = svs_r.next()
            b.op('act', lambda: nc.scalar.copy(out=svs.t[0:L, :], in_=psSV.t[0:L, oS:oS + DV + 1]), R=[psSV], W=[svs])
            return dict(sc=sc, kw=kw, vp=vp, svs=svs, bcol=bcol)

        def finish_head(d, g, L, nh, h, p, col_og, psQC, gated, mcol):
            sc = d['sc']
            b.op('dve', lambda: nc.vector.tensor_tensor(out=sc.t[0:L, 3:4], in0=d['bcol'], in1=mcol[0], op=ALU.add), R=[g] + mcol[1], W=[sc])
            b.op('dve', lambda: nc.vector.tensor_tensor(out=sc.t[0:L, 4:5], in0=sc.t[0:L, 3:4], in1=sc.t[0:L, 0:1], op=ALU.max), R=[sc], W=[sc])
            b.op('dve', lambda: nc.vector.tensor_scalar(out=sc.t[0:L, 8:9], in0=sc.t[0:L, 4:5], scalar1=-1.0, scalar2=None, op0=ALU.mult), R=[sc], W=[sc])
            b.op('act', lambda: nc.scalar.activation(out=sc.t[0:L, 5:6], in_=sc.t[0:L, 0:1], func=AF.Exp, bias=sc.t[0:L, 8:9]), R=[sc], W=[sc])
            b.op('act', lambda: nc.scalar.activation(out=sc.t[0:L, 6:7], in_=sc.t[0:L, 3:4], func=AF.Exp, bias=sc.t[0:L, 8:9]), R=[sc], W=[sc])
            b.op('act', lambda: nc.scalar.activation(out=sc.t[0:L, 7:8], in_=sc.t[0:L, 8:9], func=AF.Exp), R=[sc], W=[sc])
            svs = d['svs']
            b.op('dve', lambda: nc.vector.tensor_scalar(out=svs.t[0:L, :], in0=svs.t[0:L, :], scalar1=sc.t[0:L, 5:6], scalar2=None, op0=ALU.mult), R=[svs, sc], W=[svs])
            num = num_r.next()
            b.op('dve', lambda: nc.vector.scalar_tensor_tensor(out=num.t[0:L, :], in0=psQC[0], scalar=sc.t[0:L, 6:7], in1=svs.t[0:L, :], op0=ALU.mult, op1=ALU.add),
                 R=psQC[1] + [sc, svs], W=[num])
            b.op('dve', lambda: nc.vector.tensor_scalar(out=sc.t[0:L, 10:11], in0=num.t[0:L, DV:DV + 1], scalar1=-1.0, scalar2=None, op0=ALU.mult), R=[num], W=[sc])
            b.op('dve', lambda: nc.vector.tensor_tensor(out=sc.t[0:L, 10:11], in0=sc.t[0:L, 10:11], in1=num.t[0:L, DV:DV + 1], op=ALU.max), R=[num, sc], W=[sc])
            b.op('dve', lambda: nc.vector.tensor_tensor(out=sc.t[0:L, 10:11], in0=sc.t[0:L, 10:11], in1=sc.t[0:L, 7:8], op=ALU.max), R=[sc], W=[sc])
            b.op('dve', lambda: nc.vector.reciprocal(out=sc.t[0:L, 10:11], in_=sc.t[0:L, 10:11]), R=[sc], W=[sc])
            hr = hr_r.next()
            b.op('dve', lambda: nc.vector.tensor_scalar(out=hr.t[0:L, :], in0=num.t[0:L, 0:DV], scalar1=sc.t[0:L, 10:11], scalar2=None, op0=ALU.mult), R=[num, sc], W=[hr])
            b.op('act', lambda: nc.scalar.activation(out=num.t[0:L, 0:DV], in_=hr.t[0:L, :], func=AF.Square, accum_out=sc.t[0:L, 12:13]), R=[hr], W=[num, sc])
            b.op('act', lambda: nc.scalar.activation(out=sc.t[0:L, 13:14], in_=sc.t[0:L, 12:13], func=AF.Ln, scale=1.0 / DV, bias=EPS), R=[sc], W=[sc])
            b.op('act', lambda: nc.scalar.activation(out=sc.t[0:L, 13:14], in_=sc.t[0:L, 13:14], func=AF.Exp, scale=-0.5), R=[sc], W=[sc])
            return hr, sc

        def og_sigmoid(p, L, nh, col_og):
            sg = sg_r.next()
            b.op('act', lambda: nc.scalar.activation(out=sg.t[0:L, 0:nh * DV], in_=p.t[0:L, col_og:col_og + nh * DV], func=AF.Sigmoid), R=[p], W=[sg])
            return sg

        def gate_out(hr, sc, sg, gated, L, h):
            b.op('dve', lambda: nc.vector.scalar_tensor_tensor(out=gated.t[0:L, h * DV:(h + 1) * DV], in0=hr.t[0:L, :], scalar=sc.t[0:L, 13:14],
                                                               in1=sg.t[0:L, h * DV:(h + 1) * DV], op0=ALU.mult, op1=ALU.mult), R=[hr, sc, sg], W=[gated])

        if mode == 'prompt':
            col_q, col_k, col_v, col_og, col_gi, col_gf = 0, 256, 512, 1024, 1536, 1540
            blocks_own = [(0, 512), (512, 512), (1024, 512), (1536, 8)]
            for i in range(NT):
                r0, r1 = cfg.trange(i)
                L = r1 - r0
                xt = xt_r.next()
                b.dma('sp', xt.t[0:L, :], I["xp"].t[r0:r1, :], W=[xt])
                xh = xh_r.next()
                rms_normalize(xt.t[0:L, :], L, xh, scr, ssq_r.next(), [xt])
                xT = xT_r.next()
                transpose_to(xT, xh, L, 8)
                p = p_r.next()
                proj_tok(p, xT, L, w_in, 8, blocks_own)
                g = gates(p, L, NH, col_gi, col_gf, ROW("big_own", L), ROW("bfg_own", L), C("tri", L, L), C("ones", L, 128))
                pb, qkT = qk_prep(p, L, NH, col_q, col_k, col_v)
                sg = og_sigmoid(p, L, NH, col_og)
                gated = gated_r.next()
                streams = []
                for h in range(NH):
                    bank = psf.items[h]
                    b.record_begin()
                    d = head_common(g, L, NH, h, qkT, pb, C("maskc", L, L), None, bank=bank)
                    sc = d['sc']
                    b.op('pe', lambda: nc.tensor.matmul(bank.t[0:L, 0:DV + 1], lhsT=qkT.t[0:64, h, 0:L], rhs=Cbf[h].t[:, :], start=True, stop=True), R=[qkT, Cbf[h]], W=[bank])
                    hr, sc = finish_head(d, g, L, NH, h, p, col_og, (bank.t[0:L, 0:DV + 1], [bank]), gated, (mrep[h].t[0:L, 0:1], [mrep[h]]))
                    gate_out(hr, sc, sg, gated, L, h)
                    bend = g.t[0:128, 3 * NH + h:3 * NH + h + 1]
                    sc2 = sc_r.next()
                    b.op('dve', lambda: nc.vector.tensor_tensor(out=sc2.t[:, 0:1], in0=bend, in1=mrep[h].t[:, 0:1], op=ALU.add), R=[g, mrep[h]], W=[sc2])
                    b.op('dve', lambda: nc.vector.tensor_tensor(out=sc2.t[:, 1:2], in0=bend, in1=sc.t[:, 2:3], op=ALU.add), R=[g, sc], W=[sc2])
                    b.op('dve', lambda: nc.vector.tensor_tensor(out=sc2.t[:, 2:3], in0=sc2.t[:, 0:1], in1=sc2.t[:, 1:2], op=ALU.max), R=[sc2], W=[sc2])
                    b.op('dve', lambda: nc.vector.tensor_scalar(out=sc2.t[:, 3:4], in0=sc2.t[:, 2:3], scalar1=-1.0, scalar2=None, op0=ALU.mult), R=[sc2], W=[sc2])
                    b.op('act', lambda: nc.scalar.activation(out=sc2.t[:, 4:5], in_=sc2.t[:, 0:1], func=AF.Exp, bias=sc2.t[:, 3:4]), R=[sc2], W=[sc2])
                    b.op('act', lambda: nc.scalar.activation(out=sc2.t[:, 5:6], in_=sc2.t[:, 1:2], func=AF.Exp, bias=sc2.t[:, 3:4]), R=[sc2], W=[sc2])
                    b.op('pe', lambda: nc.tensor.matmul(bank.t[0:DK, 256:256 + DV + 1], lhsT=d['kw'].t[0:L, :], rhs=d['vp'], start=True, stop=True), R=[d['kw'], pb], W=[bank])
                    ut = ut_r.next()
                    b.op('act', lambda: nc.scalar.activation(out=ut.t[:, :], in_=bank.t[0:DK, 256:256 + DV + 1], func=AF.Copy, scale=sc2.t[0:DK, 5:6]), R=[bank, sc2], W=[ut])
                    b.op('dve', lambda: nc.vector.scalar_tensor_tensor(out=Cst[h].t[:, :], in0=Cst[h].t[:, :], scalar=sc2.t[0:DK, 4:5], in1=ut.t[:, :], op0=ALU.mult, op1=ALU.add),
                         R=[Cst[h], sc2, ut], W=[Cst[h]])
                    b.op('pool', lambda: nc.gpsimd.tensor_copy(out=Cbf[h].t[:, :], in_=Cst[h].t[:, :]), R=[Cst[h]], W=[Cbf[h]])
                    b.op('dve', lambda: nc.vector.tensor_copy(out=mrep[h].t[:, 0:1], in_=sc2.t[:, 2:3]), R=[sc2], W=[mrep[h]])
                    streams.append(b.record_end())
                while any(streams):
                    for st in streams:
                        if st:
                            b.emit(st.popleft())
                gT = gT_r.next()
                transpose_to(gT, gated, L, NH)
                aout = aout_r.next()
                proj_tok(aout, gT, L, w_oa, NH, [(0, 512), (512, 512)])
                b.dma('sp', S["rs1_in"].t[r0:r1, :], aout.t[0:L, :], R=[aout], W=[S["rs1_in"]])
            for h in range(NH):
                b.dma('sp', O["pC"].t[h, :, :], Cst[h].t[:, :], R=[Cst[h]], W=[O["pC"]])
                b.dma('sp', O["pm"].t[0:1, h:h + 1], mrep[h].t[0:1, 0:1], R=[mrep[h]], W=[O["pm"]])

        else:
            col_q, col_k, col_v, col_og, col_gi, col_gf = 0, 512, 1024, 2048, 3072, 3080
            blocks_full = [(0, 512), (512, 512), (1024, 512), (1536, 512), (2048, 512), (2560, 512), (3072, 16)]
            L = 128
            xt = xt_r.next()
            b.dma('sp', xt.t[:, :], I["xs"].t[:, :], W=[xt])
            xh = xh_r.next()
            rms_normalize(xt.t[0:L, :], L, xh, scr, ssq_r.next(), [xt])
            xT = xT_r.next()
            transpose_to(xT, xh, L, 8)
            p = p_r.next()
            proj_tok(p, xT, L, w_in, 8, blocks_full)
            g = gates(p, L, NH, col_gi, col_gf, ROW("big_full", L), ROW("bfg_full", L), C("triB", L, L), C("onesB", L, 128))
            pb, qkT = qk_prep(p, L, NH, col_q, col_k, col_v)
            sg = og_sigmoid(p, L, NH, col_og)
            gated = gated_r.next()
            groups = [(0, 3), (3, 3), (6, 3), (9, 3), (12, 3), (15, 1)]
            for h in range(NH):
                hl = h % 4
                if hl == 0:
                    b.dma('sp', CA.t[:], I["sCn"].t[:, h:h + 4, :, :], W=[CA])
                d = head_common(g, L, NH, h, qkT, pb, C("maskB", L, L), C("maskB2", 128, L))
                sc = d['sc']
                CAb = CAb_r.next()
                b.op('pool', lambda: nc.gpsimd.tensor_copy(out=CAb.t[:, :, :], in_=CA.t[:, hl, :, :]), R=[CA], W=[CAb])
                acc = acc_r.next()
                for (b0, nb) in groups:
                    psQ = psf.next()
                    b.op('pe', lambda: nc.tensor.matmul(psQ.t[0:L, 0:nb * (DV + 1)], lhsT=qkT.t[0:64, h, 0:L],
                                                        rhs=CAb.t[:, b0:b0 + nb, :].rearrange("p b e -> p (b e)"), start=True, stop=True), R=[qkT, CAb], W=[psQ])
                    for j in range(nb):
                        bb = b0 + j
                        src = psQ.t[0:L, j * (DV + 1):(j + 1) * (DV + 1)]
                        if bb == 0:
                            b.op('dve', lambda: nc.vector.tensor_scalar(out=acc.t[:, :], in0=src, scalar1=BM()[:, bb:bb + 1], scalar2=None, op0=ALU.mult), R=[psQ, cst], W=[acc])
                        else:
                            b.op('dve', lambda: nc.vector.scalar_tensor_tensor(out=acc.t[:, :], in0=src, scalar=BM()[:, bb:bb + 1], in1=acc.t[:, :], op0=ALU.mult, op1=ALU.add),
                                 R=[psQ, cst, acc], W=[acc])
                hr, sc = finish_head(d, g, L, NH, h, p, col_og, (acc.t[0:L, :], [acc]), gated, (mtok.t[0:L, h:h + 1], [mtok]))
                gate_out(hr, sc, sg, gated, L, h)
                bend = g.t[0:128, 3 * NH + h:3 * NH + h + 1]
                sc2 = sc_r.next()
                b.op('dve', lambda: nc.vector.tensor_tensor(out=sc2.t[:, 0:1], in0=bend, in1=mtok.t[:, h:h + 1], op=ALU.add), R=[g, mtok], W=[sc2])
                b.op('dve', lambda: nc.vector.tensor_tensor(out=sc2.t[:, 1:2], in0=bend, in1=sc.t[:, 2:3], op=ALU.add), R=[g, sc], W=[sc2])
                b.op('dve', lambda: nc.vector.tensor_tensor(out=sc2.t[:, 2:3], in0=sc2.t[:, 0:1], in1=sc2.t[:, 1:2], op=ALU.max), R=[sc2], W=[sc2])
                b.op('dve', lambda: nc.vector.tensor_scalar(out=sc2.t[:, 3:4], in0=sc2.t[:, 2:3], scalar1=-1.0, scalar2=None, op0=ALU.mult), R=[sc2], W=[sc2])
                b.op('act', lambda: nc.scalar.activation(out=sc2.t[:, 4:5], in_=sc2.t[:, 0:1], func=AF.Exp, bias=sc2.t[:, 3:4]), R=[sc2], W=[sc2])
                b.op('act', lambda: nc.scalar.activation(out=sc2.t[:, 5:6], in_=sc2.t[:, 1:2], func=AF.Exp, bias=sc2.t[:, 3:4]), R=[sc2], W=[sc2])
                b.op('dve', lambda: nc.vector.tensor_copy(out=mnew.t[:, h:h + 1], in_=sc2.t[:, 2:3]), R=[sc2], W=[mnew])
                cB = cB_r.next()
                b.op('dve', lambda: nc.vector.tensor_scalar(out=cB.t[:, 0, :], in0=C("ones", 128, DK), scalar1=sc2.t[:, 4:5], scalar2=None, op0=ALU.mult), R=[sc2, cst], W=[cB])
                b.op('dve', lambda: nc.vector.tensor_scalar(out=cB.t[:, 1, :], in0=C("ones", 128, DK), scalar1=sc2.t[:, 5:6], scalar2=None, op0=ALU.mult), R=[sc2, cst], W=[cB])
                psW = psf.next()
                b.op('pe', lambda: nc.tensor.matmul(psW.t[0:DK, 0:16], lhsT=cB.t[:, 0, :], rhs=SEL(), start=True, stop=True), R=[cB, cst], W=[psW])
                b.op('pe', lambda: nc.tensor.matmul(psW.t[0:DK, 16:32], lhsT=cB.t[:, 1, :], rhs=SEL(), start=True, stop=True), R=[cB, cst], W=[psW])
                wrow = wrow_r.next()
                b.op('act', lambda: nc.scalar.copy(out=wrow.t[:, :], in_=psW.t[0:DK, 0:32]), R=[psW], W=[wrow])
                Vb = Vb_r.next()
                for bb in range(16):
                    b.op('pool', lambda: nc.gpsimd.tensor_scalar(out=Vb.t[:, bb, :], in0=d['vp'], scalar1=BM()[:, bb:bb + 1], scalar2=None, op0=ALU.mult), R=[pb, cst], W=[Vb])
                for (b0, nb) in groups:
                    psU = psf.next()
                    b.op('pe', lambda: nc.tensor.matmul(psU.t[0:DK, 0:nb * (DV + 1)], lhsT=d['kw'].t[0:L, :], rhs=Vb.t[:, b0:b0 + nb, :].rearrange("p b e -> p (b e)"),
                                                        start=True, stop=True), R=[d['kw'], Vb], W=[psU])
                    for j in range(nb):
                        bb = b0 + j
                        ut = ut_r.next()
                        b.op('act', lambda: nc.scalar.activation(out=ut.t[:, :], in_=psU.t[0:DK, j * (DV + 1):(j + 1) * (DV + 1)], func=AF.Copy, scale=wrow.t[:, 16 + bb:17 + bb]),
                             R=[psU, wrow], W=[ut])
                        b.op('dve', lambda: nc.vector.scalar_tensor_tensor(out=CA.t[:, hl, bb, :], in0=CA.t[:, hl, bb, :], scalar=wrow.t[:, bb:bb + 1], in1=ut.t[:, :],
                                                                           op0=ALU.mult, op1=ALU.add), R=[CA, wrow, ut], W=[CA])
                b.dma('sp', O["sCo"].t[:, h, :, :].rearrange("b k e -> k b e"), CA.t[:, hl, :, :], R=[CA], W=[O["sCo"]])
            gT = gT_r.next()
            transpose_to(gT, gated, L, NH)
            aout = aout_r.next()
            proj_tok(aout, gT, L, w_oa, NH, [(0, 512), (512, 512)])
            b.op('dve', lambda: nc.vector.tensor_tensor(out=aout.t[:, :], in0=aout.t[:, :], in1=xt.t[:, :], op=ALU.add), R=[aout, xt], W=[aout])
            b.dma('sp', S["hs"].t[:, :], aout.t[:, :], R=[aout], W=[S["hs"]])
            b.dma('sp', O["smo"].t[:, :], mnew.t[0:128:8, :], R=[mnew], W=[O["smo"]])
        b.barrier()
        pes.close()


    NTOK = NOWN + 1
    LLAST = HALF - 128 * (NOWN - 1)

    def ffn(pes, l, Hs, gname):
        NTT = NTOK * 128
        xTa = b.sb(pes, [128, 8, NTT], BF16, "xTa")
        xh_r = ring(pes, 2, [128, D], BF16, "fxh")
        scr = b.sb(pes, [128, D], F32, "fscr")
        ssq_r = ring(pes, 2, [128, 2], F32, "fssq")
        xTt_r = ring(pes, 2, [128, 8, 128], BF16, "fxT")
        for t in range(NTOK):
            xh = xh_r.next()
            rms_normalize(Hs[t].t[:, :], 128, xh, scr, ssq_r.next(), [Hs[t]])
            xTt = xTt_r.next()
            transpose_to(xTt, xh, 128, 8)
            b.op('pool', lambda: nc.gpsimd.tensor_copy(out=xTa.t[:, :, t * 128:(t + 1) * 128], in_=xTt.t[:, :, :]), R=[xTt], W=[xTa])
        wg_r = ring(pes, 2, [128, 8, 256], BF16, "wg")
        wu_r = ring(pes, 2, [128, 8, 256], BF16, "wu")
        wd_r = ring(pes, 2, [128, 2, D], BF16, "wd")
        mid_r = ring(pes, 2, [128, 2, NTT], BF16, "mid")
        tmp_r = ring(pes, 2, [128, 512], F32, "ftmp")
        blocks = [(c0, min(512, NTT - c0)) for c0 in range(0, NTT, 512)]
        for gi in range(DFF // 256):
            wg, wu, wd = wg_r.next(), wu_r.next(), wd_r.next()
            load_w(wg, lambda k, c0, n: I["w_gu"].t[l, k * 128:(k + 1) * 128, gi * 256 + c0:gi * 256 + c0 + n], 8, 256, gain=lambda k: VEC(gname, k))
            load_w(wu, lambda k, c0, n: I["w_gu"].t[l, k * 128:(k + 1) * 128, DFF + gi * 256 + c0:DFF + gi * 256 + c0 + n], 8, 256, gain=lambda k: VEC(gname, k))
            load_w(wd, lambda k, c0, n: I["w_d"].t[l, gi * 256 + k * 128:gi * 256 + (k + 1) * 128, c0:c0 + n], 2, D)
            mid = mid_r.next()
            for (c0, nb) in blocks:
                for c in range(2):
                    psG, psU = psf.next(), psf.next()
                    for k in range(8):
                        b.op('pe', lambda: nc.tensor.matmul(psG.t[:, 0:nb], lhsT=wg.t[:, k, c * 128:(c + 1) * 128], rhs=xTa.t[:, k, c0:c0 + nb], start=(k == 0), stop=(k == 7)),
                             R=[wg, xTa], W=[psG])
                    for k in range(8):
                        b.op('pe', lambda: nc.tensor.matmul(psU.t[:, 0:nb], lhsT=wu.t[:, k, c * 128:(c + 1) * 128], rhs=xTa.t[:, k, c0:c0 + nb], start=(k == 0), stop=(k == 7)),
                             R=[wu, xTa], W=[psU])
                    tmp = tmp_r.next()
                    b.op('act', lambda: nc.scalar.activation(out=tmp.t[:, 0:nb], in_=psG.t[:, 0:nb], func=AF.Silu), R=[psG], W=[tmp])
                    b.op('dve', lambda: nc.vector.tensor_tensor(out=mid.t[:, c, c0:c0 + nb], in0=tmp.t[:, 0:nb], in1=psU.t[:, 0:nb], op=ALU.mult), R=[tmp, psU], W=[mid])
            for t in range(NTOK):
                for hf in range(2):
                    psD = psf.next()
                    for c in range(2):
                        b.op('pe', lambda: nc.tensor.matmul(psD.t[:, 0:512], lhsT=mid.t[:, c, t * 128:(t + 1) * 128], rhs=wd.t[:, c, hf * 512:(hf + 1) * 512], start=(c == 0), stop=(c == 1)),
                             R=[mid, wd], W=[psD])
                    b.op('dve', lambda: nc.vector.tensor_tensor(out=Hs[t].t[:, hf * 512:(hf + 1) * 512], in0=Hs[t].t[:, hf * 512:(hf + 1) * 512], in1=psD.t[:, 0:512], op=ALU.add),
                         R=[Hs[t], psD], W=[Hs[t]])

    def phase_B():
        pes = ExitStack()
        Hs = [b.sb(pes, [128, D], F32, "H") for _ in range(NTOK)]
        aes = ExitStack()
        at_r = ring(aes, 2, [128, D], F32, "at")
        for t in range(NOWN):
            L = 128 if t < NOWN - 1 else LLAST
            at = at_r.next()
            if L < 128:
                b.op('pool', lambda: nc.gpsimd.memset(Hs[t].t[:, :], 0.0), W=[Hs[t]])
                b.op('pool', lambda: nc.gpsimd.memset(at.t[:, :], 0.0), W=[at])
            b.dma('sp', Hs[t].t[0:L, :], I["xown"].t[t * 128:t * 128 + L, :], W=[Hs[t]])
            b.dma('sp', at.t[0:L, :], S["rs1_out"].t[t * 128:t * 128 + L, :], R=[S["rs1_out"]], W=[at])
            b.op('dve', lambda: nc.vector.tensor_tensor(out=Hs[t].t[:, :], in0=Hs[t].t[:, :], in1=at.t[:, :], op=ALU.add), R=[Hs[t], at], W=[Hs[t]])
        b.dma('sp', Hs[NOWN].t[:, :], S["hs"].t[:, :], R=[S["hs"]], W=[Hs[NOWN]])
        b.barrier()
        aes.close()
        fes = ExitStack()
        ffn(fes, 0, Hs, "nffn0")
        xh_r = ring(fes, 2, [128, D], BF16, "bxh")
        scr = b.sb(fes, [128, D], F32, "bscr")
        ssq_r = ring(fes, 2, [128, 2], F32, "bssq")
        for t in range(NTOK):
            xh = xh_r.next()
            rms_normalize(Hs[t].t[:, :], 128, xh, scr, ssq_r.next(), [Hs[t]])
            if t < NOWN:
                kk, tt = t // 4, t % 4
                b.dma('sp', S[f"ag_in{kk}"].t[tt * 128:(tt + 1) * 128, :], xh.t[:, :], R=[xh], W=[S[f"ag_in{kk}"]])
            else:
                b.dma('sp', S["xs2"].t[:, :], xh.t[:, :], R=[xh], W=[S["xs2"]])
            b.dma('sp', S["h"].t[t * 128:(t + 1) * 128, :], Hs[t].t[:, :], R=[Hs[t]], W=[S["h"]])
        for k in range(len(AGC)):
            b.cc("AllGather", ALU.bypass, PAIRS, S[f"ag_in{k}"], S[f"ag_out{k}"])
        b.barrier()
        fes.close()
        pes.close()

    def fox_proj(pes, xh, L, wkv, wqo, nh, bf_row, rings):
        (xT_r, pk_r, pq_r, sq_r, ss_r, kn_r, qn_r, lf_r, sg_r) = rings
        xT = xT_r.next()
        transpose_to(xT, xh, L, 8)
        pk, pq = pk_r.next(), pq_r.next()
        nk = nh * 64
        blk = lambda n: [(c0, min(512, n - c0)) for c0 in range(0, n, 512)]
        proj_tok(pk, xT, L, wkv, 8, blk(2 * nk + nh))
        proj_tok(pq, xT, L, wqo, 8, blk(2 * nk))
        lf = lf_r.next()
        b.op('dve', lambda: nc.vector.tensor_tensor(out=lf.t[0:L, 0:nh], in0=pk.t[0:L, 2 * nk:2 * nk + nh], in1=bf_row, op=ALU.add), R=[pk, rows], W=[lf])
        b.op('act', lambda: nc.scalar.activation(out=lf.t[0:L, 0:nh], in_=lf.t[0:L, 0:nh], func=AF.Exp, scale=-1.0), R=[lf], W=[lf])
        b.op('act', lambda: nc.scalar.activation(out=lf.t[0:L, 0:nh], in_=lf.t[0:L, 0:nh], func=AF.Ln, bias=1.0), R=[lf], W=[lf])
        b.op('dve', lambda: nc.vector.tensor_scalar(out=lf.t[0:L, 0:nh], in0=lf.t[0:L, 0:nh], scalar1=-1.0, scalar2=None, op0=ALU.mult), R=[lf], W=[lf])
        outs = []
        for (src, c0, grow, scale, dst_r) in [(pk, 0, "kg", 1.0, kn_r), (pq, 0, "qg", DH ** -0.5, qn_r)]:
            sq, ss, dst = sq_r.next(), ss_r.next(), dst_r.next()
            b.op('pool', lambda: nc.gpsimd.tensor_tensor(out=sq.t[0:L, 0:nk], in0=src.t[0:L, c0:c0 + nk], in1=src.t[0:L, c0:c0 + nk], op=ALU.mult), R=[src], W=[sq])
            b.op('dve', lambda: nc.vector.tensor_reduce(out=ss.t[0:L, 0:nh], in_=sq.t[0:L, 0:nk].rearrange("p (h e) -> p h e", e=64), axis=AX.X, op=ALU.add), R=[sq], W=[ss])
            b.op('act', lambda: nc.scalar.activation(out=ss.t[0:L, 0:nh], in_=ss.t[0:L, 0:nh], func=AF.Ln, scale=1.0 / 64, bias=EPS), R=[ss], W=[ss])
            b.op('act', lambda: nc.scalar.activation(out=ss.t[0:L, 0:nh], in_=ss.t[0:L, 0:nh], func=AF.Exp, scale=-0.5), R=[ss], W=[ss])
            if scale != 1.0:
                b.op('dve', lambda: nc.vector.tensor_scalar(out=ss.t[0:L, 0:nh], in0=ss.t[0:L, 0:nh], scalar1=scale, scalar2=None, op0=ALU.mult), R=[ss], W=[ss])
            for h in range(nh):
                b.op('dve', lambda: nc.vector.scalar_tensor_tensor(out=dst.t[0:L, h * 64:(h + 1) * 64], in0=src.t[0:L, c0 + h * 64:c0 + (h + 1) * 64], scalar=ss.t[0:L, h:h + 1],
                                                                   in1=ROW(grow, L), op0=ALU.mult, op1=ALU.mult), R=[src, ss, rows], W=[dst])
            outs.append(dst)
        sg = sg_r.next()
        b.op('act', lambda: nc.scalar.activation(out=sg.t[0:L, 0:nk], in_=pq.t[0:L, nk:2 * nk], func=AF.Sigmoid), R=[pq], W=[sg])
        return dict(pk=pk, lf=lf, kn=outs[0], qn=outs[1], sg=sg, nk=nk)

    def fox_rings(pes, nh, n=2):
        nk = nh * 64
        return (ring(pes, n, [128, 8, 128], BF16, "cxT"), ring(pes, n, [128, 2 * nk + nh], F32, "cpk"), ring(pes, n, [128, 2 * nk], F32, "cpq"),
                ring(pes, n, [128, nk], F32, "csq"), ring(pes, n, [128, nh], F32, "css"), ring(pes, n, [128, nk], F32, "ckn"),
                ring(pes, n, [128, nk], F32, "cqn"), ring(pes, n, [128, nh], F32, "clf"), ring(pes, n, [128, nk], F32, "csg"))

    def phase_C0():
        pes = ExitStack()
        wkv = b.sb(pes, [128, 8, 2064], BF16, "wkvf")
        wqo = b.sb(pes, [128, 8, 2048], BF16, "wqof")
        load_w(wkv, lambda k, c0, n: I["w_kvf_full"].t[k * 128:(k + 1) * 128, c0:c0 + n], 8, 2064, gain=lambda k: VEC("norm_kv", k))
        load_w(wqo, lambda k, c0, n: I["w_qo_full"].t[k * 128:(k + 1) * 128, c0:c0 + n], 8, 2048, gain=lambda k: VEC("norm_b", k))
        xh = b.sb(pes, [128, D], BF16, "c0xh")
        b.dma('sp', xh.t[:, :], S["xs2"].t[:, :], R=[S["xs2"]], W=[xh])
        r = fox_proj(pes, xh, 128, wkv, wqo, HB, ROW("bfb_full", 128), fox_rings(pes, HB, 1))
        b.dma('sp', O["sk"].t[:, :], r['kn'].t[:, :], R=[r['kn']], W=[O["sk"]])
        b.dma('sp', O["sv"].t[:, :], r['pk'].t[:, 1024:2048], R=[r['pk']], W=[O["sv"]])
        b.dma('sp', O["slf"].t[:, :], r['lf'].t[:, 0:16], R=[r['lf']], W=[O["slf"]])
        srcs = [(0, r['qn'], 0), (1024, r['kn'], 0), (2048, r['pk'], 1024), (3072, r['lf'], 0)]
        for k, (c0, w) in enumerate(G1C):
            for (pc, tl, tc) in srcs:
                pw = 1024 if pc < 3072 else 16
                lo, hi = max(c0, pc), min(c0 + w, pc + pw)
                if lo < hi:
                    b.dma('sp', S[f"g1_in{k}"].t[:, lo - c0:hi - c0], tl.t[:, tc + lo - pc:tc + hi - pc], R=[tl], W=[S[f"g1_in{k}"]])
        b.dma('sp', S["sgs"].t[:, :], r['sg'].t[:, :], R=[r['sg']], W=[S["sgs"]])
        for k in range(len(G1C)):
            b.cc("AllGather", ALU.bypass, QUADS, S[f"g1_in{k}"], S[f"g1_q{k}"])
            b.cc("AllGather", ALU.bypass, PAIRS, S[f"g1_q{k}"], S[f"g1_all{k}"])
        b.barrier()
        pes.close()

    def phase_C():
        pes = ExitStack()
        NH = 8
        wkv = b.sb(pes, [128, 8, 1032], BF16, "wkvo")
        wqo = b.sb(pes, [128, 8, 1024], BF16, "wqoo")
        wob = b.sb(pes, [128, 4, D], BF16, "wobo")
        load_w(wkv, lambda k, c0, n: I["w_kvf_own"].t[k * 128:(k + 1) * 128, c0:c0 + n], 8, 1032, gain=lambda k: VEC("norm_kv", k))
        load_w(wqo, lambda k, c0, n: I["w_qo_own"].t[k * 128:(k + 1) * 128, c0:c0 + n], 8, 1024, gain=lambda k: VEC("norm_b", k))
        load_w(wob, lambda k, c0, n: I["w_ob_own"].t[k * 128:(k + 1) * 128, c0:c0 + n], 4, D)
        KT = b.sb(pes, [128, 4, TP], BF16, "KT")
        Vst = b.sb(pes, [128, NT, NH, DH + 1], BF16, "Vst")
        Cc = b.sb(pes, [128, NT, NH], F32, "Cc")
        carry = b.sb(pes, [128, NH], F32, "carry")
        b.op('pool', lambda: nc.gpsimd.memset(carry.t[:, :], 0.0), W=[carry])
        b.op('pool', lambda: nc.gpsimd.memset(Vst.t[:, :, :, DH:DH + 1], 1.0), W=[Vst])
        xh_r = ring(pes, 2, [128, D], BF16, "cxh")
        rings = fox_rings(pes, NH, 2)
        knb_r = ring(pes, 2, [128, 512], BF16, "knb")
        qnb_r = ring(pes, 2, [128, 512], BF16, "qnb")
        qT_r = ring(pes, 2, [128, 4, 128], BF16, "qT")
        bias_s = [ring(pes, 2, [128, NT], F32, "bias") for _ in range(2)]
        PT_s = [ring(pes, 3, [128, 128], BF16, "PT") for _ in range(2)]
        tmpS_s = [b.sb(pes, [128, 128], F32, "tmpS") for _ in range(2)]
        rec_s = [b.sb(pes, [128, 1], F32, "rec") for _ in range(2)]
        go_r = ring(pes, 2, [128, 512], BF16, "go")
        goT_r = ring(pes, 2, [128, 4, 128], BF16, "goT")
        ao_r = ring(pes, 2, [128, D], F32, "ao")
        for i in range(NT):
            r0, r1 = cfg.trange(i)
            L = r1 - r0
            xh = xh_r.next()
            a0 = r0
            while a0 < r1:
                rk = a0 // HALF
                loc = a0 - rk * HALF
                kk = loc // 512
                nt = AGC[kk][1]
                a1 = min(r1, (rk + 1) * HALF, rk * HALF + (kk + 1) * 512)
                g0 = rk * nt * 128 + (loc - kk * 512)
                b.dma('sp', xh.t[a0 - r0:a1 - r0, :], S[f"ag_out{kk}"].t[g0:g0 + (a1 - a0), :], R=[S[f"ag_out{kk}"]], W=[xh])
                a0 = a1
            r = fox_proj(pes, xh, L, wkv, wqo, NH, ROW("bfb_own", L), rings)
            pk, lf, kn, qn, sg = r['pk'], r['lf'], r['kn'], r['qn'], r['sg']
            b.dma('sp', O["pk"].t[r0:r1, :], kn.t[0:L, :], R=[kn], W=[O["pk"]])
            b.dma('sp', O["pv"].t[r0:r1, :], pk.t[0:L, 512:1024], R=[pk], W=[O["pv"]])
            b.dma('sp', O["plf"].t[r0:r1, :], lf.t[0:L, 0:NH], R=[lf], W=[O["plf"]])
            knb, qnb = knb_r.next(), qnb_r.next()
            b.op('pool', lambda: nc.gpsimd.tensor_copy(out=knb.t[0:L, :], in_=kn.t[0:L, :]), R=[kn], W=[knb])
            b.op('pool', lambda: nc.gpsimd.tensor_copy(out=qnb.t[0:L, :], in_=qn.t[0:L, :]), R=[qn], W=[qnb])
            b.op('pool', lambda: nc.gpsimd.tensor_copy(out=Vst.t[0:L, i, :, 0:DH], in_=pk.t[0:L, 512:1024].rearrange("p (h e) -> p h e", e=DH)), R=[pk], W=[Vst])
            psk = psb.next()
            for k in range(4):
                b.op('pe', lambda: nc.tensor.transpose(psk.t[:, k * 128:k * 128 + L], knb.t[0:L, k * 128:(k + 1) * 128], identb.t[0:L, 0:L]), R=[knb, identb], W=[psk])
            b.op('act', lambda: nc.scalar.copy(out=KT.t[:, :, r0:r1], in_=psk.t[:, 0:512].rearrange("p (k l) -> p k l", l=128)[:, :, 0:L]), R=[psk], W=[KT])
            qT = qT_r.next()
            transpose_to(qT, qnb, L, 4, evac='dve')
            psc = psf.next()
            b.op('pe', lambda: nc.tensor.matmul(psc.t[0:L, 0:NH], lhsT=C("tri", L, L), rhs=lf.t[0:L, 0:NH], start=True, stop=True), R=[cst, lf], W=[psc])
            b.op('pe', lambda: nc.tensor.matmul(psc.t[0:128, 8:8 + NH], lhsT=C("ones", L, 128), rhs=lf.t[0:L, 0:NH], start=True, stop=True), R=[cst, lf], W=[psc])
            b.op('dve', lambda: nc.vector.tensor_tensor(out=Cc.t[0:L, i, :], in0=psc.t[0:L, 0:NH], in1=carry.t[0:L, :], op=ALU.add), R=[psc, carry], W=[Cc])
            b.op('dve', lambda: nc.vector.tensor_tensor(out=carry.t[:, :], in0=carry.t[:, :], in1=psc.t[0:128, 8:8 + NH], op=ALU.add), R=[psc, carry], W=[carry])
            go = go_r.next()

            def head_stream(h, slot):
                pr, hh = h // 2, (h % 2) * 64
                b.record_begin()
                bias = bias_s[slot].next()
                b.op('dve', lambda: nc.vector.tensor_scalar(out=bias.t[:, 0:i + 1], in0=Cc.t[:, 0:i + 1, h], scalar1=-1.0, scalar2=carry.t[:, h:h + 1], op0=ALU.mult, op1=ALU.add),
                     R=[Cc, carry], W=[bias])
                psO = pso.items[slot]
                banks = [psf.items[2 * slot], psf.items[2 * slot + 1]]
                pend = {}

                def issue_S(j):
                    s0, s1 = cfg.trange(j)
                    Lj = s1 - s0
                    psS = banks[j % 2]
                    b.op('pe', lambda: nc.tensor.matmul(psS.t[0:Lj, 0:L], lhsT=KT.t[hh:hh + 64, pr, s0:s1], rhs=qT.t[hh:hh + 64, pr, 0:L], start=True, stop=True), R=[KT, qT], W=[psS])
                    pend[j] = (psS, Lj)

                issue_S(0)
                for j in range(i + 1):
                    psS, Lj = pend.pop(j)
                    PT = PT_s[slot].next()
                    if j == i:
                        tmpS = tmpS_s[slot]
                        b.op('dve', lambda: nc.vector.tensor_tensor(out=tmpS.t[0:Lj, 0:L], in0=psS.t[0:Lj, 0:L], in1=C("maskT", Lj, L), op=ALU.add), R=[psS, cst], W=[tmpS])
                        b.op('act', lambda: nc.scalar.activation(out=PT.t[0:Lj, 0:L], in_=tmpS.t[0:Lj, 0:L], func=AF.Exp, bias=bias.t[0:Lj, j:j + 1]), R=[tmpS, bias], W=[PT])
                    else:
                        b.op('act', lambda: nc.scalar.activation(out=PT.t[0:Lj, 0:L], in_=psS.t[0:Lj, 0:L], func=AF.Exp, bias=bias.t[0:Lj, j:j + 1]), R=[psS, bias], W=[PT])
                    if j + 1 <= i:
                        issue_S(j + 1)
                    b.op('pe', lambda: nc.tensor.matmul(psO.t[0:L, 0:DH + 1], lhsT=PT.t[0:Lj, 0:L], rhs=Vst.t[0:Lj, j, h, :], start=(j == 0), stop=(j == i)), R=[PT, Vst], W=[psO])
                rec = rec_s[slot]
                b.op('dve', lambda: nc.vector.reciprocal(out=rec.t[0:L, :], in_=psO.t[0:L, DH:DH + 1]), R=[psO], W=[rec])
                b.op('dve', lambda: nc.vector.scalar_tensor_tensor(out=go.t[0:L, h * 64:(h + 1) * 64], in0=psO.t[0:L, 0:DH], scalar=rec.t[0:L, 0:1], in1=sg.t[0:L, h * 64:(h + 1) * 64],
                                                                   op0=ALU.mult, op1=ALU.mult), R=[psO, rec, sg], W=[go])
                return b.record_end()

            hq = collections.deque(range(NH))
            slots2 = [None, None]
            while hq or any(sl for sl in slots2):
                for k in range(2):
                    if not slots2[k] and hq:
                        slots2[k] = head_stream(hq.popleft(), k)
                    if slots2[k]:
                        b.emit(slots2[k].popleft())
            goT = goT_r.next()
            transpose_to(goT, go, L, 4)
            ao = ao_r.next()
            proj_tok(ao, goT, L, wob, 4, [(0, 512), (512, 512)])
            b.dma('sp', S["rs2_in"].t[r0:r1, :], ao.t[0:L, :], R=[ao], W=[S["rs2_in"]])
        b.barrier()
        pes.close()


    def phase_C2():
        pes = ExitStack()
        NQ = 16
        idx = b.sb(pes, [128, 128 * NPG], I32, "idx")
        iot = b.sb(pes, [128, 1], I32, "iot")
        tes = ExitStack()
        ptb = b.sb(tes, [128, 128 * NPG], I32, "ptb")
        b.dma('pool', ptb.t[:, :], I["pt"].t[0:1, :].partition_broadcast(128), W=[ptb])
        b.op('pool', lambda: nc.gpsimd.iota(iot.t[:, :], pattern=[[0, 1]], base=0, channel_multiplier=1), W=[iot])
        b.op('pool', lambda: nc.gpsimd.tensor_scalar(out=idx.t[:, :], in0=ptb.t[:, :], scalar1=128, scalar2=iot.t[:, 0:1], op0=ALU.mult, op1=ALU.add), R=[ptb, iot], W=[idx])
        b.barrier()
        tes.close()
        pay_r = ring(pes, 1, [128, 3088], F32, "pay")
        own_r = ring(pes, 1, [128, 3, 128], F32, "own")
        lfo_r = ring(pes, 2, [128, 2], F32, "lfo")
        ownb_r = ring(pes, 2, [128, 2, 128], BF16, "ownb")
        qkT_r = ring(pes, 2, [128, 2, 128], BF16, "sqkT")
        Qbd_r = ring(pes, 2, [128, 16, NQ], BF16, "Qbd")
        vs_r = ring(pes, 2, [128, 2, DH + 1], BF16, "vs")
        bn_r = ring(pes, 2, [128, 2], F32, "bn")
        tN_r = ring(pes, 1, [128, 16, NQ], F32, "tN")
        PN_r = ring(pes, 2, [128, 16, NQ], BF16, "PN")
        KSL = 5
        recs = [b.sb(pes, [128, NPG, 258], F32, "rec") for _ in range(KSL)]
        kTps = [b.sb(pes, [128, NPG, 128], BF16, "kTp") for _ in range(KSL)]
        vps = [b.sb(pes, [128, NPG, 2, DH + 1], BF16, "vp") for _ in range(KSL)]
        for vp in vps:
            b.op('dve', lambda: nc.vector.memset(vp.t[:, :, :, DH:DH + 1], 1.0), W=[vp])
        lfps = [b.sb(pes, [128, NPG, 2], F32, "lfp") for _ in range(KSL)]
        totcs = [b.sb(pes, [2 * NPG, 1], F32, "totc") for _ in range(KSL)]
        totBs = [b.sb(pes, [2 * NPG, 128], F32, "totB") for _ in range(KSL)]
        bPs = [b.sb(pes, [128, NPG, 2], F32, "bP") for _ in range(KSL)]
        tPs = [b.sb(pes, [128, NPG, NQ], F32, "tP") for _ in range(KSL)]
        PPs = [b.sb(pes, [128, NPG, NQ], BF16, "PP") for _ in range(KSL)]
        rcs = [b.sb(pes, [16, 2], F32, "rc") for _ in range(KSL)]
        oall_r = ring(pes, 2, [16, 16, 2, DH], F32, "oall")
        bank_src = pso.items[0]
        OFF_B, OFF_T, OFF_O = 256, 296, 304

        def prep_source(sidx):
            pay = pay_r.next()
            for k, (c0, w) in enumerate(G1C):
                b.dma('sp', pay.t[:, c0:c0 + w], S[f"g1_all{k}"].t[sidx * 128:(sidx + 1) * 128, :], R=[S[f"g1_all{k}"]], W=[pay])
            own, lfo = own_r.next(), lfo_r.next()
            for f_ in range(3):
                for dd in range(8):
                    src = pay.t[:, f_ * 1024 + dd * 128:f_ * 1024 + (dd + 1) * 128]
                    if dd == 0:
                        b.op('dve', lambda: nc.vector.tensor_scalar(out=own.t[:, f_, :], in0=src, scalar1=ROW("oh", 128, 0, 1), scalar2=None, op0=ALU.mult), R=[pay, rows], W=[own])
                    else:
                        b.op('dve', lambda: nc.vector.scalar_tensor_tensor(out=own.t[:, f_, :], in0=src, scalar=ROW("oh", 128, dd, dd + 1), in1=own.t[:, f_, :], op0=ALU.mult, op1=ALU.add),
                             R=[pay, rows, own], W=[own])
            for dd in range(8):
                src = pay.t[:, 3072 + 2 * dd:3072 + 2 * dd + 2]
                if dd == 0:
                    b.op('dve', lambda: nc.vector.tensor_scalar(out=lfo.t[:, :], in0=src, scalar1=ROW("oh", 128, 0, 1), scalar2=None, op0=ALU.mult), R=[pay, rows], W=[lfo])
                else:
                    b.op('dve', lambda: nc.vector.scalar_tensor_tensor(out=lfo.t[:, :], in0=src, scalar=ROW("oh", 128, dd, dd + 1), in1=lfo.t[:, :], op0=ALU.mult, op1=ALU.add),
                         R=[pay, rows, lfo], W=[lfo])
            ownb = ownb_r.next()
            b.op('act', lambda: nc.scalar.copy(out=ownb.t[:, :, :], in_=own.t[:, 0:2, :]), R=[own], W=[ownb])
            qkT = qkT_r.next()
            pst = psb.next()
            for k in range(2):
                b.op('pe', lambda: nc.tensor.transpose(pst.t[:, k * 128:(k + 1) * 128], ownb.t[:, k, :], identb.t[:, :]), R=[ownb, identb], W=[pst])
            b.op('act', lambda: nc.scalar.copy(out=qkT.t[:, :, :], in_=pst.t[:, 0:256].rearrange("p (k l) -> p k l", l=128)), R=[pst], W=[qkT])
            Qbd = Qbd_r.next()
            b.op('dve', lambda: nc.vector.memset(Qbd.t[:, :, :], 0.0), W=[Qbd])
            b.op('dve', lambda: nc.vector.tensor_copy(out=Qbd.t[0:64, :, 0:8], in_=qkT.t[0:64, 0, :].rearrange("p (i q) -> p i q", q=8)), R=[qkT], W=[Qbd])
            b.op('dve', lambda: nc.vector.tensor_copy(out=Qbd.t[64:128, :, 8:16], in_=qkT.t[64:128, 0, :].rearrange("p (i q) -> p i q", q=8)), R=[qkT], W=[Qbd])
            vs = vs_r.next()
            b.op('dve', lambda: nc.vector.memset(vs.t[:, :, DH:DH + 1], 1.0), W=[vs])
            b.op('dve', lambda: nc.vector.tensor_copy(out=vs.t[:, :, 0:DH], in_=own.t[:, 2, :].rearrange("p (h e) -> p h e", e=DH)), R=[own], W=[vs])
            b.op('pe', lambda: nc.tensor.matmul(bank_src.t[:, 300:302], lhsT=C("triB"), rhs=lfo.t[:, :], start=True, stop=True), R=[cst, lfo], W=[bank_src])
            bn = bn_r.next()
            b.op('dve', lambda: nc.vector.tensor_scalar(out=bn.t[:, :], in0=bank_src.t[:, 300:302], scalar1=-1.0, scalar2=None, op0=ALU.mult), R=[bank_src], W=[bn])
            b.op('pe', lambda: nc.tensor.matmul(bank_src.t[:, 0:256], lhsT=qkT.t[:, 1, :], rhs=Qbd.t[:, :, :].rearrange("p i q -> p (i q)"), start=True, stop=True), R=[qkT, Qbd], W=[bank_src])
            tN = tN_r.next()
            tNf = tN.t[:, :, :].rearrange("p i q -> p (i q)")
            b.op('dve', lambda: nc.vector.tensor_tensor(out=tNf[:, 0:128], in0=bank_src.t[:, 0:128], in1=C("maskN0"), op=ALU.add), R=[bank_src, cst], W=[tN])
            b.op('dve', lambda: nc.vector.tensor_tensor(out=tNf[:, 128:256], in0=bank_src.t[:, 128:256], in1=C("maskN1"), op=ALU.add), R=[bank_src, cst], W=[tN])
            PN = PN_r.next()
            for hh in range(2):
                b.op('act', lambda: nc.scalar.activation(out=PN.t[:, :, hh * 8:(hh + 1) * 8], in_=tN.t[:, :, hh * 8:(hh + 1) * 8], func=AF.Exp, bias=bn.t[:, hh:hh + 1]), R=[tN, bn], W=[PN])
            return dict(Qbd=Qbd, PN=PN, vs=vs, oall=oall_r.next())

        def batch_ops(sl, sidx, i, sd):
            Qbd, PN, vs, oall = sd['Qbd'], sd['PN'], sd['vs'], sd['oall']
            bg = sidx * 16 + i
            rec, kTp, vp, lfp, totc, totB, bP, tP, PP, rc = recs[sl], kTps[sl], vps[sl], lfps[sl], totcs[sl], totBs[sl], bPs[sl], tPs[sl], PPs[sl], rcs[sl]
            bank = psf.items[sl] if sl < 4 else pso.items[1]
            b.record_begin()
            for j in range(NPG):
                b.dma('pool', rec.t[:, j, :], I["rec"].t[:, :], R=[idx], W=[rec],
                      indirect=bass.IndirectOffsetOnAxis(ap=idx.t[:, bg * NPG + j:bg * NPG + j + 1], axis=0))
            b.op('act', lambda: nc.scalar.copy(out=kTp.t[:, :, :], in_=rec.t[:, :, 0:128]), R=[rec], W=[kTp])
            b.op('dve', lambda: nc.vector.tensor_copy(out=vp.t[:, :, :, 0:DH], in_=rec.t[:, :, 128:256].rearrange("p j (h e) -> p j h e", e=DH)), R=[rec], W=[vp])
            b.op('dve', lambda: nc.vector.tensor_copy(out=lfp.t[:, :, :], in_=rec.t[:, :, 256:258]), R=[rec], W=[lfp])
            lfpf = lfp.t[:, :, :].rearrange("p j h -> p (j h)")
            b.op('pe', lambda: nc.tensor.matmul(bank.t[0:2 * NPG, OFF_T:OFF_T + 1], lhsT=lfpf, rhs=C("ones", 128, 1), start=True, stop=True), R=[lfp, cst], W=[bank])
            b.op('act', lambda: nc.scalar.copy(out=totc.t[:, :], in_=bank.t[0:2 * NPG, OFF_T:OFF_T + 1]), R=[bank], W=[totc])
            b.op('dve', lambda: nc.vector.tensor_scalar(out=totB.t[:, :], in0=C("ones", 2 * NPG, 128), scalar1=totc.t[:, 0:1], scalar2=None, op0=ALU.mult), R=[totc, cst], W=[totB])
            b.op('pe', lambda: nc.tensor.matmul(bank.t[:, OFF_B:OFF_B + 2 * NPG], lhsT=C("su"), rhs=lfpf, start=True, stop=False), R=[cst, lfp], W=[bank])
            b.op('pe', lambda: nc.tensor.matmul(bank.t[:, OFF_B:OFF_B + 2 * NPG], lhsT=totB.t[:, :], rhs=C("msuf", 2 * NPG, 2 * NPG), start=False, stop=True), R=[totB, cst], W=[bank])
            b.op('act', lambda: nc.scalar.copy(out=bP.t[:, :, :].rearrange("p j h -> p (j h)"), in_=bank.t[:, OFF_B:OFF_B + 2 * NPG]), R=[bank], W=[bP])
            for j in range(NPG):
                b.op('pe', lambda: nc.tensor.matmul(bank.t[:, j * NQ:(j + 1) * NQ], lhsT=kTp.t[:, j, :], rhs=Qbd.t[:, i, :], start=True, stop=True), R=[kTp, Qbd], W=[bank])
            psSv = bank.t[:, 0:NPG * NQ].rearrange("p (j h q) -> p j h q", h=2, q=8)
            tPv = tP.t[:, :, :].rearrange("p j (h q) -> p j h q", q=8)
            for q in range(8):
                b.op('dve', lambda: nc.vector.tensor_tensor(out=tPv[:, :, :, q], in0=psSv[:, :, :, q], in1=bP.t[:, :, :], op=ALU.add), R=[bank, bP], W=[tP])
            b.op('act', lambda: nc.scalar.activation(out=PP.t[:, :, :], in_=tP.t[:, :, :], func=AF.Exp), R=[tP], W=[PP])
            oreg = bank.t[0:NQ, OFF_O:OFF_O + 2 * (DH + 1)]
            for j in range(NPG):
                b.op('pe', lambda: nc.tensor.matmul(oreg, lhsT=PP.t[:, j, :], rhs=vp.t[:, j, :, :].rearrange("p h e -> p (h e)"), start=(j == 0), stop=False), R=[PP, vp], W=[bank])
            b.op('pe', lambda: nc.tensor.matmul(oreg, lhsT=PN.t[:, i, :], rhs=vs.t[:, :, :].rearrange("p h e -> p (h e)"), start=False, stop=True), R=[PN, vs], W=[bank])
            psOv = oreg.rearrange("p (h e) -> p h e", e=DH + 1)
            b.op('dve', lambda: nc.vector.reciprocal(out=rc.t[:, :], in_=psOv[:, :, DH]), R=[bank], W=[rc])
            for hh in range(2):
                b.op('dve', lambda: nc.vector.tensor_scalar(out=oall.t[:, i, hh, :], in0=psOv[:, hh, 0:DH], scalar1=rc.t[:, hh:hh + 1], scalar2=None, op0=ALU.mult), R=[bank, rc], W=[oall])
            return b.record_end()

        def post_source(sidx, sd):
            kk, sl_ = sidx // 4, sidx % 4
            for hh in range(2):
                dst = S[f"g2_in{kk}"].t[sl_ * 128:(sl_ + 1) * 128, hh * DH:(hh + 1) * DH].rearrange("(i q) e -> q i e", q=8)
                b.dma('sp', dst, sd['oall'].t[hh * 8:(hh + 1) * 8, :, hh, :], R=[sd['oall']], W=[S[f"g2_in{kk}"]])

        jobs = collections.deque((sidx, i) for sidx in range(8) for i in range(16))
        slots = [None] * KSL
        sdata, remaining = {}, {}
        while jobs or any(sl is not None for sl in slots):
            for k in range(KSL):
                if slots[k] is None and jobs:
                    sidx, i = jobs.popleft()
                    if sidx not in sdata:
                        sdata[sidx] = prep_source(sidx)
                        remaining[sidx] = 16
                    slots[k] = [batch_ops(k, sidx, i, sdata[sidx]), sidx]
                if slots[k] is not None:
                    ops, sidx = slots[k]
                    b.emit(ops.popleft())
                    if not ops:
                        slots[k] = None
                        remaining[sidx] -= 1
                        if remaining[sidx] == 0:
                            post_source(sidx, sdata[sidx])
        for k in range(2):
            b.cc("AllGather", ALU.bypass, QUADS, S[f"g2_in{k}"], S[f"g2_q{k}"])
            b.cc("AllGather", ALU.bypass, PAIRS, S[f"g2_q{k}"], S[f"g2_all{k}"])
        gl_r = ring(pes, 1, [128, 4, 128], F32, "gl")
        osm = b.sb(pes, [128, D], F32, "osm2")
        for cc_ in range(8):
            for k in range(2):
                gl = gl_r.next()
                b.dma('sp', gl.t[:, :, :], S[f"g2_all{k}"].t[cc_ * 512:(cc_ + 1) * 512, :].rearrange("(s p) e -> p s e", p=128), R=[S[f"g2_all{k}"]], W=[gl])
                for sl in range(4):
                    srcid = k * 4 + sl
                    if srcid == 0:
                        b.op('dve', lambda: nc.vector.tensor_scalar(out=osm.t[:, cc_ * 128:(cc_ + 1) * 128], in0=gl.t[:, sl, :], scalar1=ROW("oh", 128, 0, 1), scalar2=None, op0=ALU.mult),
                             R=[gl, rows], W=[osm])
                    else:
                        b.op('dve', lambda: nc.vector.scalar_tensor_tensor(out=osm.t[:, cc_ * 128:(cc_ + 1) * 128], in0=gl.t[:, sl, :], scalar=ROW("oh", 128, srcid, srcid + 1),
                                                                           in1=osm.t[:, cc_ * 128:(cc_ + 1) * 128], op0=ALU.mult, op1=ALU.add), R=[gl, rows, osm], W=[osm])
        b.dma('sp', S["os"].t[:, :], osm.t[:, :], R=[osm], W=[S["os"]])
        b.barrier()
        pes.close()

    def phase_D(with_sample_attn):
        pes = ExitStack()
        Hs = [b.sb(pes, [128, D], F32, "H") for _ in range(NTOK)]
        aes = ExitStack()
        at_r = ring(aes, 2, [128, D], F32, "at")
        for t in range(NOWN):
            L = 128 if t < NOWN - 1 else LLAST
            at = at_r.next()
            if L < 128:
                b.op('pool', lambda: nc.gpsimd.memset(at.t[:, :], 0.0), W=[at])
            b.dma('sp', Hs[t].t[:, :], S["h"].t[t * 128:(t + 1) * 128, :], R=[S["h"]], W=[Hs[t]])
            b.dma('sp', at.t[0:L, :], S["rs2_out"].t[t * 128:t * 128 + L, :], R=[S["rs2_out"]], W=[at])
            b.op('dve', lambda: nc.vector.tensor_tensor(out=Hs[t].t[:, :], in0=Hs[t].t[:, :], in1=at.t[:, :], op=ALU.add), R=[Hs[t], at], W=[Hs[t]])
        b.dma('sp', Hs[NOWN].t[:, :], S["h"].t[NOWN * 128:(NOWN + 1) * 128, :], R=[S["h"]], W=[Hs[NOWN]])
        b.barrier()
        aes.close()
        if with_sample_attn:
            ses = ExitStack()
            wobf = b.sb(ses, [128, 8, D], BF16, "wobf")
            load_w(wobf, lambda k, c0, n: I["w_ob_full"].t[k * 128:(k + 1) * 128, c0:c0 + n], 8, D)
            osm = b.sb(ses, [128, D], F32, "osm")
            sgt = b.sb(ses, [128, D], F32, "sgt")
            gob = b.sb(ses, [128, D], BF16, "gob")
            goT = b.sb(ses, [128, 8, 128], BF16, "sgoT")
            ao = b.sb(ses, [128, D], F32, "sao")
            b.dma('sp', osm.t[:, :], S["os"].t[:, :], R=[S["os"]], W=[osm])
            b.dma('sp', sgt.t[:, :], S["sgs"].t[:, :], R=[S["sgs"]], W=[sgt])
            b.op('dve', lambda: nc.vector.tensor_tensor(out=gob.t[:, :], in0=osm.t[:, :], in1=sgt.t[:, :], op=ALU.mult), R=[osm, sgt], W=[gob])
            transpose_to(goT, gob, 128, 8)
            proj_tok(ao, goT, 128, wobf, 8, [(0, 512), (512, 512)])
            b.op('dve', lambda: nc.vector.tensor_tensor(out=Hs[NOWN].t[:, :], in0=Hs[NOWN].t[:, :], in1=ao.t[:, :], op=ALU.add), R=[Hs[NOWN], ao], W=[Hs[NOWN]])
            b.barrier()
            ses.close()
        fes = ExitStack()
        ffn(fes, 1, Hs, "nffn1")
        xh_r = ring(fes, 2, [128, D], BF16, "dxh")
        scr = b.sb(fes, [128, D], F32, "dscr")
        ssq_r = ring(fes, 2, [128, 2], F32, "dssq")
        y_r = ring(fes, 2, [128, D], F32, "dy")
        for t in range(NTOK):
            ssq = ssq_r.next()
            b.op('act', lambda: nc.scalar.activation(out=scr.t[:, :], in_=Hs[t].t[:, :], func=AF.Square, accum_out=ssq.t[:, 0:1]), R=[Hs[t]], W=[scr, ssq])
            b.op('act', lambda: nc.scalar.activation(out=ssq.t[:, 1:2], in_=ssq.t[:, 0:1], func=AF.Ln, scale=1.0 / D, bias=EPS), R=[ssq], W=[ssq])
            b.op('act', lambda: nc.scalar.activation(out=ssq.t[:, 1:2], in_=ssq.t[:, 1:2], func=AF.Exp, scale=-0.5), R=[ssq], W=[ssq])
            y = y_r.next()
            b.op('dve', lambda: nc.vector.scalar_tensor_tensor(out=y.t[:, :], in0=Hs[t].t[:, :], scalar=ssq.t[:, 1:2], in1=ROW("nfin"), op0=ALU.mult, op1=ALU.mult),
                 R=[Hs[t], ssq, rows], W=[y])
            if t < NOWN:
                b.dma('sp', O["y_own"].t[t * 128:(t + 1) * 128, :], y.t[:, :], R=[y], W=[O["y_own"]])
            else:
                b.dma('sp', O["y_s"].t[:, :], y.t[:, :], R=[y], W=[O["y_s"]])
        b.barrier()
        fes.close()
        pes.close()

    if "A" in cfg.phases:
        phase_A('prompt')
        b.cc("ReduceScatter", ALU.add, PAIRS, S["rs1_in"], S["rs1_out"])
    if "S" in cfg.phases:
        phase_A('sample')
    if "B" in cfg.phases:
        phase_B()
    if "0" in cfg.phases:
        phase_C0()
    if "C" in cfg.phases:
        phase_C()
        b.cc("ReduceScatter", ALU.add, PAIRS, S["rs2_in"], S["rs2_out"])
    if "X" in cfg.phases:
        phase_C2()
    if "D" in cfg.phases:
        phase_D("X" in cfg.phases)

    b.barrier(final=True)
    es.close()
    return nc


def prep_core_inputs(cfg, c, inp, consts):
    bq, r = c % 4, c // 4
    f = lambda a: np.ascontiguousarray(np.asarray(a, dtype=np.float32))
    m = {}
    m["consts"] = consts
    hs = slice(4 * r, 4 * r + 4)
    fs = slice(8 * r, 8 * r + 8)
    rowsv = np.concatenate([
        inp["b_ig_a"][0][hs], inp["b_fg_a"][0][hs], inp["b_ig_a"][0], inp["b_fg_a"][0],
        inp["b_fg_b"][fs], inp["b_fg_b"], inp["k_norm_b"], inp["q_norm_b"][0], inp["norm_final"], np.eye(8, dtype=np.float32)[c]]).astype(np.float32)
    m["rows"] = np.ascontiguousarray(np.broadcast_to(rowsv[None, :], (128, NROWS)))
    col = lambda v: np.asarray(v, np.float32).reshape(-1, 128).T
    m["vecs"] = np.ascontiguousarray(np.concatenate([
        col(inp["norm_a"][0]), col(inp["norm_ffn"][0]), col(inp["norm_ffn"][1]), col(inp["norm_kv"]), col(inp["norm_b"][0]),
        col(inp["mh_norm_a"][0][hs]), col(inp["mh_norm_a"][0])], axis=1))
    m["xp"] = f(np.concatenate([inp["meta_tokens"], inp["x_prompt"][bq]], axis=0))
    m["xs"] = f(inp["x_sample"][16 * c:16 * c + 16].reshape(128, D))
    sC = np.asarray(inp["state_C"][0][16 * c:16 * c + 16], np.float32)
    sn = np.asarray(inp["state_n"][0][16 * c:16 * c + 16], np.float32)
    m["sCn"] = f(np.concatenate([sC, sn[..., None]], axis=-1).transpose(2, 1, 0, 3))
    m["smt"] = f(np.repeat(np.asarray(inp["state_m"][0][16 * c:16 * c + 16], np.float32), 8, axis=0))
    w = np.asarray(inp["w_in_a"][0], np.float32)
    HK, HV = HA * DK, HA * DV
    wq, wk, wv, wo = w[:, :HK], w[:, HK:2 * HK], w[:, 2 * HK:2 * HK + HV], w[:, 2 * HK + HV:2 * HK + 2 * HV]
    wgi, wgf = w[:, 2 * HK + 2 * HV:2 * HK + 2 * HV + HA], w[:, 2 * HK + 2 * HV + HA:]
    m["w_in_own"] = f(np.concatenate([wq[:, 256 * r:256 * r + 256], wk[:, 256 * r:256 * r + 256], wv[:, 512 * r:512 * r + 512],
                                      wo[:, 512 * r:512 * r + 512], wgi[:, hs], wgf[:, hs]], axis=1))
    m["w_in_full"] = f(w)
    HALF = cfg.HALF
    ck = np.asarray(inp["cache_k"])[:, :, 2 * c:2 * c + 2, :]
    cv = np.asarray(inp["cache_v"])[:, :, 2 * c:2 * c + 2, :]
    cl = np.asarray(inp["cache_logf"])[:, :, 2 * c:2 * c + 2]
    nph = ck.shape[0]
    rec = np.empty((nph, 128, 258), np.float32)
    rec[:, :, 0:128] = ck.transpose(0, 2, 3, 1).reshape(nph, 128, 128)
    rec[:, :, 128:256] = cv.reshape(nph, 128, 128)
    rec[:, :, 256:258] = cl
    m["rec"] = rec.reshape(nph * 128, 258)
    m["pt"] = np.ascontiguousarray(np.asarray(inp["page_table"], np.int32).reshape(1, -1))
    m["xown"] = f(m["xp"][r * HALF:(r + 1) * HALF])
    m["w_gu"] = f(inp["w_gate_up"])
    m["w_d"] = f(inp["w_down"])
    wkvf = np.asarray(inp["w_kvf"], np.float32)
    HD = HB * DH
    m["w_kvf_own"] = f(np.concatenate([wkvf[:, 512 * r:512 * r + 512], wkvf[:, HD + 512 * r:HD + 512 * r + 512], wkvf[:, 2 * HD + 8 * r:2 * HD + 8 * r + 8]], axis=1))
    m["w_kvf_full"] = f(wkvf)
    wqo = np.asarray(inp["w_qo_b"][0], np.float32)
    m["w_qo_own"] = f(np.concatenate([wqo[:, 512 * r:512 * r + 512], wqo[:, HD + 512 * r:HD + 512 * r + 512]], axis=1))
    m["w_qo_full"] = f(wqo)
    wob = np.asarray(inp["w_out_b"][0], np.float32)
    m["w_ob_own"] = f(wob[512 * r:512 * r + 512])
    m["w_ob_full"] = f(wob)
    woa = np.asarray(inp["w_out_a"][0], np.float32)
    m["w_oa_own"] = f(woa[512 * r:512 * r + 512])
    m["w_oa_full"] = f(woa)
    return m


def run(cfg, inp):
    NPG_GLOBAL[0] = cfg.NPG
    consts = make_consts()
    nc = build(cfg)
    in_maps = [prep_core_inputs(cfg, c, inp, consts) for c in range(NCORE)]
    res = run_bass_kernel_spmd(nc, in_maps, core_ids=list(range(NCORE)))
    return res.results


def kernel(**inputs):
    inp = {k: np.asarray(v) for k, v in inputs.items()}
    SEQ = inp["x_prompt"].shape[1]
    NPG = inp["page_table"].shape[1]
    cfg = Cfg(SEQ=SEQ, PAST=NPG * 128, NPHYS=inp["cache_k"].shape[0], phases="ASB0CXD")
    res = run(cfg, inp)
    TP, H = cfg.TP, cfg.HALF
    f32 = np.float32
    y_prompt = np.stack([np.concatenate([res[bq]["y_own"][:H], res[bq + 4]["y_own"][:H]], axis=0)[NMETA:] for bq in range(4)]).astype(f32)
    y_sample = np.concatenate([res[c]["y_s"] for c in range(8)], axis=0).reshape(128, 8, D).astype(f32)
    p_C = np.zeros((1, 4, HA, DK, DV), f32); p_n = np.zeros((1, 4, HA, DK), f32); p_m = np.zeros((1, 4, HA), f32)
    p_k = np.zeros((4, TP, HB, DH), f32); p_v = np.zeros((4, TP, HB, DH), f32); p_lf = np.zeros((4, TP, HB), f32)
    for c in range(8):
        bq, r = c % 4, c // 4
        p_C[0, bq, 4 * r:4 * r + 4] = res[c]["pC"][:, :, :DV]
        p_n[0, bq, 4 * r:4 * r + 4] = res[c]["pC"][:, :, DV]
        p_m[0, bq, 4 * r:4 * r + 4] = res[c]["pm"][0]
        p_k[bq, :, 8 * r:8 * r + 8] = res[c]["pk"].reshape(TP, 8, DH)
        p_v[bq, :, 8 * r:8 * r + 8] = res[c]["pv"].reshape(TP, 8, DH)
        p_lf[bq, :, 8 * r:8 * r + 8] = res[c]["plf"]
    sC = np.concatenate([res[c]["sCo"] for c in range(8)], axis=0)
    s_C = np.ascontiguousarray(sC[..., :DV])[None].astype(f32)
    s_n = np.ascontiguousarray(sC[..., DV])[None].astype(f32)
    s_m = np.concatenate([res[c]["smo"] for c in range(8)], axis=0)[None].astype(f32)
    s_k = np.concatenate([res[c]["sk"] for c in range(8)], axis=0).reshape(128, 8, HB, DH).astype(f32)
    s_v = np.concatenate([res[c]["sv"] for c in range(8)], axis=0).reshape(128, 8, HB, DH).astype(f32)
    s_lf = np.concatenate([res[c]["slf"] for c in range(8)], axis=0).reshape(128, 8, HB).astype(f32)
    return (y_prompt, y_sample, p_C, p_n, p_m, p_k, p_v, p_lf, s_C, s_n, s_m, s_k, s_v, s_lf)
```
